# Optimizing a Trainium2 kernel written in Bass

```python
import numpy as np
import jax
import jax.numpy as jnp
from jax import lax

D_MODEL = 2048
BATCH = 16
SEQ = 256
DEPTH = 4
DEC_BATCH = 4
DEC_SEQ = 2048
PAST_LEN = 256

GRID_W = 64
N_AB = (DEPTH + 1) // 2
N_C = DEPTH // 2
A_HEADS = 8
A_NOPE = 128
A_ROPE = 64
A_VDIM = 128
A_QK = A_NOPE + A_ROPE
Q_LORA = 512
KV_LORA = 256
A_WIDTH = A_HEADS * A_VDIM
B_HEADS = 8
B_DK = 128
B_DV = 128
B_KDIM = B_HEADS * B_DK
B_WIDTH = B_HEADS * B_DV
B_CHUNK = 32
C_HEADS = 16
C_KV_HEADS = 4
C_GROUP = C_HEADS // C_KV_HEADS
C_HEAD_DIM = 128
C_WIDTH = C_HEADS * C_HEAD_DIM
C_KV_WIDTH = C_KV_HEADS * C_HEAD_DIM
WINDOW = 128
C_BLOCK = 128
Q_BLOCK = 128
ROPE_BASE = 10000.0
EPS = 1e-6
NEG_BIG = -1e30
AB_SIZES = (Q_LORA, KV_LORA, A_ROPE, A_WIDTH, B_KDIM, B_KDIM, B_KDIM, B_WIDTH, B_WIDTH)
AB_IN = sum(AB_SIZES)
AB_MIX = A_WIDTH + B_WIDTH
C_SIZES = (C_WIDTH, C_KV_WIDTH, C_KV_WIDTH, C_WIDTH)
C_IN = sum(C_SIZES)

kernel_name = 'hybrid_mla_hgrn2_swa_prefix_dit_step'


def rmsnorm(x, g):
    xf = x.astype(jnp.float32)
    y = xf * lax.rsqrt(jnp.mean(xf * xf, axis=-1, keepdims=True) + EPS)
    return (y * g.astype(jnp.float32)).astype(x.dtype)


def split_cols(a, sizes):
    return jnp.split(a, np.cumsum(sizes)[:-1].tolist(), axis=-1)


def modulate_and_norm(x, cond, w_mod, b_mod, g):
    m = (jax.nn.silu(cond) @ w_mod + b_mod)[:, None, :]
    shift, scale, gate = jnp.split(m, 3, axis=-1)
    return rmsnorm(x, g) * (1 + scale) + shift, gate


def axial_rope_tables(n_tokens, rot_dim):
    rows = n_tokens // GRID_W
    row = jnp.repeat(jnp.arange(rows, dtype=jnp.float32), GRID_W)
    col = jnp.tile(jnp.arange(GRID_W, dtype=jnp.float32), rows)
    n_freq = rot_dim // 4
    inv = ROPE_BASE ** (-jnp.arange(n_freq, dtype=jnp.float32) / n_freq)
    ang = jnp.concatenate([row[:, None] * inv, col[:, None] * inv], axis=-1)
    return jnp.cos(ang), jnp.sin(ang)


def apply_rope(x, cos, sin):
    x1, x2 = jnp.split(x.astype(jnp.float32), 2, axis=-1)
    c, s = cos[:, None, :], sin[:, None, :]
    return jnp.concatenate([x1 * c - x2 * s, x1 * s + x2 * c], axis=-1).astype(x.dtype)


def rope_tail(x, cos, sin):
    return jnp.concatenate([x[..., :A_NOPE], apply_rope(x[..., A_NOPE:], cos, sin)], axis=-1)


def blocked_attention(q, k, v, scale):
    Bn, T, H, dq = q.shape
    nb = T // Q_BLOCK
    qb = jnp.moveaxis(q.reshape(Bn, nb, Q_BLOCK, H, dq), 1, 0)

    def one(qi):
        s = jnp.einsum('bqhd,bkhd->bhqk', qi, k).astype(jnp.float32) * scale
        p = jax.nn.softmax(s, axis=-1).astype(v.dtype)
        return jnp.einsum('bhqk,bkhd->bqhd', p, v)

    o = lax.map(one, qb)
    return jnp.moveaxis(o, 0, 1).reshape(Bn, T, H, v.shape[-1])


def mla_keys(c_kv, k_pe, w_kv_up, k_norm_g):
    Bn, L, _ = c_kv.shape
    kv = (c_kv @ w_kv_up).reshape(Bn, L, A_HEADS, A_NOPE + A_VDIM)
    k_pe_h = jnp.broadcast_to(k_pe[:, :, None, :], (Bn, L, A_HEADS, A_ROPE))
    k = jnp.concatenate([kv[..., :A_NOPE], k_pe_h], axis=-1)
    return rmsnorm(k, k_norm_g), kv[..., A_NOPE:]


def hgrn_lower_bounds(lb_logits):
    p = jax.nn.softmax(lb_logits.astype(jnp.float32), axis=0)
    return jnp.cumsum(p, axis=0) - p[0:1]


def hgrn_gates(f_raw, lb):
    x = f_raw.astype(jnp.float32)
    one_minus_f = (1.0 - lb) * jax.nn.sigmoid(-x)
    return jnp.log1p(-one_minus_f), one_minus_f


def hgrn_chunk_scan(q, log_f, k, v, s0):
    Bn, T, H, _ = q.shape
    DV = v.shape[-1]
    n = T // B_CHUNK

    def chunks(a):
        a = a.astype(jnp.float32).reshape(Bn, n, B_CHUNK, H, a.shape[-1])
        return jnp.moveaxis(a, 1, 0).swapaxes(2, 3)

    incl = jnp.tril(jnp.ones((B_CHUNK, B_CHUNK), dtype=bool))[:, :, None]

    def step(S, xs):
        qc, lfc, kc, vc = xs
        b = jnp.cumsum(lfc, axis=2)
        rel = jnp.where(incl, b[:, :, :, None, :] - b[:, :, None, :, :], NEG_BIG)
        scores = jnp.einsum('bhtk,bhtsk,bhsk->bhts', qc, jnp.exp(rel), kc)
        o = scores @ vc + jnp.einsum('bhtk,bhkv->bhtv', qc * jnp.exp(b), S)
        b_end = b[:, :, -1:, :]
        S = jnp.exp(b_end[:, :, 0, :, None]) * S + jnp.einsum('bhsk,bhsv->bhkv', kc * jnp.exp(b_end - b), vc)
        return S, o

    S, o = lax.scan(step, s0.astype(jnp.float32), (chunks(q), chunks(log_f), chunks(k), chunks(v)))
    o = jnp.moveaxis(o.swapaxes(2, 3), 0, 1).reshape(Bn, T, H, DV)
    return o.astype(v.dtype), S.astype(v.dtype)


def hgrn_bidir(q, v, f_fwd, f_bwd, lb, s0_fwd, s0_bwd):
    Bn, T = q.shape[:2]

    def heads(a):
        return a.reshape(Bn, T, B_HEADS, B_DK)

    def flip(a):
        return jnp.flip(a, axis=1)

    lf_f, k_f = hgrn_gates(f_fwd, lb[0])
    lf_b, k_b = hgrn_gates(f_bwd, lb[1])
    o_f, s_f = hgrn_chunk_scan(q, heads(lf_f), heads(k_f), v, s0_fwd)
    o_b, s_b = hgrn_chunk_scan(flip(q), flip(heads(lf_b)), flip(heads(k_b)), flip(v), s0_bwd)
    return o_f + flip(o_b), s_f, s_b


def ab_mixer(h, w_in, q_lora_g, kv_lora_g, w_q_up, w_kv_up, q_norm_g, k_norm_g, lb, hgrn_g, w_out,
             rope=None, ctx=None):
    Bn, T, _ = h.shape
    q_lat, kv_lat, k_pe, a_gate, b_q, b_ff, b_fb, b_i, b_gate = split_cols(h @ w_in, AB_SIZES)
    q = (rmsnorm(q_lat, q_lora_g) @ w_q_up).reshape(Bn, T, A_HEADS, A_QK)
    q = rmsnorm(q, q_norm_g)
    c_kv = rmsnorm(kv_lat, kv_lora_g)
    k, v = mla_keys(c_kv, k_pe, w_kv_up, k_norm_g)
    if ctx is None:
        keys, vals = k, v
        zero = jnp.zeros((Bn, B_HEADS, B_DK, B_DV), jnp.float32)
        s0f, s0b = zero, zero
    else:
        q, k = rope_tail(q, *rope), rope_tail(k, *rope)
        k_ctx, v_ctx = mla_keys(ctx[0], ctx[1], w_kv_up, k_norm_g)
        keys = jnp.concatenate([k_ctx, k], axis=1)
        vals = jnp.concatenate([v_ctx, v], axis=1)
        s0f, s0b = ctx[2], ctx[3]
    o_a = blocked_attention(q, keys, vals, A_QK ** -0.5).reshape(Bn, T, A_WIDTH) * jax.nn.silu(a_gate)
    qb = jax.nn.silu(b_q).reshape(Bn, T, B_HEADS, B_DK)
    vb = b_i.reshape(Bn, T, B_HEADS, B_DV)
    o_b, s_f, s_b = hgrn_bidir(qb, vb, b_ff, b_fb, lb, s0f, s0b)
    o_b = rmsnorm(o_b, hgrn_g).reshape(Bn, T, B_WIDTH) * jax.nn.silu(b_gate)
    out = jnp.concatenate([o_a, o_b], axis=-1) @ w_out
    return out, (c_kv, k_pe, s_f, s_b)


def gqa_ctx_attention(q, k, v, sink):
    Bn, T, _, d = q.shape
    nb = T // Q_BLOCK
    qb = jnp.moveaxis(q.reshape(Bn, nb, Q_BLOCK, C_KV_HEADS, C_GROUP, d), 1, 0)
    sink_l = sink.astype(jnp.float32).reshape(C_KV_HEADS, C_GROUP, 1, 1)

    def one(qi):
        s = jnp.einsum('bqkgd,bskd->bkgqs', qi, k).astype(jnp.float32) * (d ** -0.5)
        s = jnp.concatenate([jnp.broadcast_to(sink_l, s.shape[:-1] + (1,)), s], axis=-1)
        p = jax.nn.softmax(s, axis=-1)[..., 1:].astype(v.dtype)
        return jnp.einsum('bkgqs,bskd->bqkgd', p, v)

    o = lax.map(one, qb)
    return jnp.moveaxis(o, 0, 1).reshape(Bn, T, C_HEADS, d)


def gqa_band_attention(q, k, v, k_ctx, v_ctx, sink):
    Bn, T, _, d = q.shape
    L = k_ctx.shape[1]
    nb = T // C_BLOCK

    def band(a):
        ab = jnp.pad(a, ((0, 0), (C_BLOCK, C_BLOCK), (0, 0), (0, 0))).reshape(Bn, nb + 2, C_BLOCK, C_KV_HEADS, d)
        return jnp.moveaxis(jnp.concatenate([ab[:, :-2], ab[:, 1:-1], ab[:, 2:]], axis=2), 1, 0)

    qb = jnp.moveaxis(q.reshape(Bn, nb, C_BLOCK, C_KV_HEADS, C_GROUP, d), 1, 0)
    blk = jnp.arange(nb)[:, None, None]
    qpos = blk * C_BLOCK + jnp.arange(C_BLOCK)[None, :, None]
    kpos = (blk - 1) * C_BLOCK + jnp.arange(3 * C_BLOCK)[None, None, :]
    mask = (jnp.abs(kpos - qpos) <= WINDOW) & (kpos >= 0) & (kpos < T)
    sink_l = sink.astype(jnp.float32).reshape(C_KV_HEADS, C_GROUP, 1, 1)
    scale = d ** -0.5

    def one(args):
        qi, ki, vi, mi = args
        s_c = jnp.einsum('bqkgd,bskd->bkgqs', qi, k_ctx).astype(jnp.float32) * scale
        s_l = jnp.einsum('bqkgd,bskd->bkgqs', qi, ki).astype(jnp.float32) * scale
        s_l = jnp.where(mi, s_l, NEG_BIG)
        s = jnp.concatenate([jnp.broadcast_to(sink_l, s_c.shape[:-1] + (1,)), s_c, s_l], axis=-1)
        p = jax.nn.softmax(s, axis=-1).astype(vi.dtype)
        return (jnp.einsum('bkgqs,bskd->bqkgd', p[..., 1:1 + L], v_ctx)
                + jnp.einsum('bkgqs,bskd->bqkgd', p[..., 1 + L:], vi))

    o = lax.map(one, (qb, band(k), band(v), mask))
    return jnp.moveaxis(o, 0, 1).reshape(Bn, T, C_HEADS, d)


def c_mixer(h, w_in, q_norm_g, k_norm_g, sink, w_out, rope=None, ctx=None):
    Bn, T, _ = h.shape
    q, k, v, gate = split_cols(h @ w_in, C_SIZES)
    q = rmsnorm(q.reshape(Bn, T, C_HEADS, C_HEAD_DIM), q_norm_g)
    k = rmsnorm(k.reshape(Bn, T, C_KV_HEADS, C_HEAD_DIM), k_norm_g)
    v = v.reshape(Bn, T, C_KV_HEADS, C_HEAD_DIM)
    if ctx is None:
        o = gqa_ctx_attention(q, k, v, sink)
    else:
        o = gqa_band_attention(apply_rope(q, *rope), apply_rope(k, *rope), v, ctx[0], ctx[1], sink)
    out = (o.reshape(Bn, T, C_WIDTH) * jax.nn.silu(gate)) @ w_out
    return out, (k, v)


def setup_inputs(seed: int = 0) -> dict:
    key = jax.random.key(seed)
    keys = iter(jax.random.split(key, 64))

    def nrm(shape, scale):
        return jax.random.normal(next(keys), shape, jnp.float32) * scale

    def gain(shape):
        return 1.0 + 0.01 * jax.random.normal(next(keys), shape, jnp.float32)

    D = D_MODEL
    return {
        'x_prompt': nrm((BATCH, SEQ, D), 1.0),
        'x_sample': nrm((DEC_BATCH, DEC_SEQ, D), 1.0),
        'cache_ckv': nrm((DEC_BATCH, N_AB, PAST_LEN, KV_LORA), 1.0),
        'cache_kpe': nrm((DEC_BATCH, N_AB, PAST_LEN, A_ROPE), 1.0),
        'state_hgrn_fwd': nrm((DEC_BATCH, N_AB, B_HEADS, B_DK, B_DV), 0.5),
        'state_hgrn_bwd': nrm((DEC_BATCH, N_AB, B_HEADS, B_DK, B_DV), 0.5),
        'cache_k_c': nrm((DEC_BATCH, N_C, PAST_LEN, C_KV_HEADS, C_HEAD_DIM), 1.0),
        'cache_v_c': nrm((DEC_BATCH, N_C, PAST_LEN, C_KV_HEADS, C_HEAD_DIM), 1.0),
        'c': nrm((DEC_BATCH, D), 1.0),
        'c_ctx': nrm((D,), 1.0),
        'mod_w_ab': nrm((N_AB, D, 3 * D), 0.5 * D ** -0.5),
        'mod_b_ab': nrm((N_AB, 3 * D), 0.01),
        'norm_ab': gain((N_AB, D)),
        'w_in_ab': nrm((N_AB, D, AB_IN), D ** -0.5),
        'q_lora_norm': gain((N_AB, Q_LORA)),
        'kv_lora_norm': gain((N_AB, KV_LORA)),
        'w_q_up': nrm((N_AB, Q_LORA, A_HEADS * A_QK), Q_LORA ** -0.5),
        'w_kv_up': nrm((N_AB, KV_LORA, A_HEADS * (A_NOPE + A_VDIM)), KV_LORA ** -0.5),
        'q_norm_ab': gain((N_AB, A_QK)),
        'k_norm_ab': gain((N_AB, A_QK)),
        'hgrn_lb_logits': nrm((N_AB, 2, B_KDIM), 0.5),
        'hgrn_out_norm': gain((N_AB, B_DV)),
        'w_out_ab': nrm((N_AB, AB_MIX, D), AB_MIX ** -0.5),
        'mod_w_c': nrm((N_C, D, 3 * D), 0.5 * D ** -0.5),
        'mod_b_c': nrm((N_C, 3 * D), 0.01),
        'norm_c': gain((N_C, D)),
        'w_in_c': nrm((N_C, D, C_IN), D ** -0.5),
        'q_norm_c': gain((N_C, C_HEAD_DIM)),
        'k_norm_c': gain((N_C, C_HEAD_DIM)),
        'sink_c': nrm((N_C, C_HEADS), 0.5),
        'w_out_c': nrm((N_C, C_WIDTH, D), C_WIDTH ** -0.5),
    }


def reference(x_prompt, x_sample, cache_ckv, cache_kpe, state_hgrn_fwd, state_hgrn_bwd, cache_k_c, cache_v_c,
              c, c_ctx, mod_w_ab, mod_b_ab, norm_ab, w_in_ab, q_lora_norm, kv_lora_norm, w_q_up, w_kv_up,
              q_norm_ab, k_norm_ab, hgrn_lb_logits, hgrn_out_norm, w_out_ab, mod_w_c, mod_b_c, norm_c, w_in_c,
              q_norm_c, k_norm_c, sink_c, w_out_c):
    lat_len = x_sample.shape[1]
    rope_a = axial_rope_tables(lat_len, A_ROPE)
    rope_c = axial_rope_tables(lat_len, C_HEAD_DIM)
    lower = hgrn_lower_bounds(hgrn_lb_logits)
    cond_ctx = c_ctx[None, :]
    xp, xs = x_prompt, x_sample
    new_ckv, new_kpe, new_sf, new_sb, new_kc, new_vc = [], [], [], [], [], []
    for layer in range(DEPTH):
        j = layer // 2
        if layer % 2 == 0:
            w = (w_in_ab[j], q_lora_norm[j], kv_lora_norm[j], w_q_up[j], w_kv_up[j], q_norm_ab[j], k_norm_ab[j],
                 lower[j], hgrn_out_norm[j], w_out_ab[j])
            h, g = modulate_and_norm(xp, cond_ctx, mod_w_ab[j], mod_b_ab[j], norm_ab[j])
            out, (ckv, kpe, sf, sb) = ab_mixer(h, *w)
            xp = xp + g * out
            new_ckv.append(ckv)
            new_kpe.append(kpe)
            new_sf.append(sf)
            new_sb.append(sb)
            h, g = modulate_and_norm(xs, c, mod_w_ab[j], mod_b_ab[j], norm_ab[j])
            ctx = (cache_ckv[:, j], cache_kpe[:, j], state_hgrn_fwd[:, j], state_hgrn_bwd[:, j])
            out, _ = ab_mixer(h, *w, rope=rope_a, ctx=ctx)
            xs = xs + g * out
        else:
            w = (w_in_c[j], q_norm_c[j], k_norm_c[j], sink_c[j], w_out_c[j])
            h, g = modulate_and_norm(xp, cond_ctx, mod_w_c[j], mod_b_c[j], norm_c[j])
            out, (kc, vc) = c_mixer(h, *w)
            xp = xp + g * out
            new_kc.append(kc)
            new_vc.append(vc)
            h, g = modulate_and_norm(xs, c, mod_w_c[j], mod_b_c[j], norm_c[j])
            out, _ = c_mixer(h, *w, rope=rope_c, ctx=(cache_k_c[:, j], cache_v_c[:, j]))
            xs = xs + g * out
    return (xp, xs, jnp.stack(new_ckv, axis=1), jnp.stack(new_kpe, axis=1), jnp.stack(new_sf, axis=1),
            jnp.stack(new_sb, axis=1), jnp.stack(new_kc, axis=1), jnp.stack(new_vc, axis=1))
```

```python
import numpy as np
from contextlib import ExitStack
import concourse.bass as bass
import concourse.mybir as mybir
from concourse.bass_utils import run_bass_kernel_spmd

F32 = mybir.dt.float32
BF16 = mybir.dt.bfloat16
AF = mybir.ActivationFunctionType
ALU = mybir.AluOpType

D = 2048
NT = 1536
BS = 512
NBK = 3
NKEY = 2816
EPS = 1e-6
HC = 64
NCH = NT // HC
RG = [[0, 1], [2, 3], [4, 5], [6, 7]]
O_QL, O_KV, O_KPE, O_AG, O_BQ, O_F1, O_F2, O_BI, O_BG = 0, 512, 768, 832, 1856, 2880, 3904, 4928, 5952

EPOCH = 30000
N_DMA_SEMS = 20
SCRN = 15360


class Prog:
    COMPUTE = ("pe", "act", "dve", "pool")

    def __init__(self, nc):
        self.nc = nc
        self.streams = {e: [] for e in ("pe", "act", "dve", "pool", "sp")}
        self.count = {e: 0 for e in self.COMPUTE}
        self.known = {e: {} for e in self.streams}
        self.res = {}
        self.sem_names = set()
        self.dma_rr = {"sp": 0, "pool": 0}
        self.dma_val = {}
        self.last = {}

    def _need(self, eng, tok):
        key, val = tok
        if self.known[eng].get(key, 0) >= val:
            return
        self.known[eng][key] = val
        self.sem_names.add(key)
        self.streams[eng].append(("wait", key, val))

    def _wait_tok(self, eng, tok):
        if tok[0].startswith("pe_") and eng == "pe":
            return
        self._need(eng, tok)

    def _deps(self, eng, reads, writes):
        for r in reads:
            st = self.res.get(r)
            if st and st["w"] is not None:
                self._wait_tok(eng, st["w"])
        for w in writes:
            st = self.res.get(w)
            if st:
                if st["w"] is not None:
                    self._wait_tok(eng, st["w"])
                for t in st["r"].items():
                    self._wait_tok(eng, t)

    def _commit(self, tok, reads, writes):
        for r in reads:
            st = self.res.setdefault(r, {"w": None, "r": {}})
            st["r"][tok[0]] = max(st["r"].get(tok[0], 0), tok[1])
        for w in writes:
            self.res[w] = {"w": tok, "r": {}}
        self.last[tok[0]] = tok[1]

    def op(self, eng, fn, reads=(), writes=()):
        writes = tuple(writes) + tuple(r for r in reads if r.startswith("ps") and r not in writes)
        self._deps(eng, reads, writes)
        n = self.count[eng]
        self.count[eng] = n + 1
        key = f"{eng}_{n // EPOCH}"
        tok = (key, n % EPOCH + 1)
        self.sem_names.add(key)
        self.streams[eng].append(("op", fn, key, 1))
        self._commit(tok, reads, writes)
        return tok

    def dma(self, queue, fn, reads=(), writes=(), inc=16):
        self._deps(queue, reads, writes)
        i = self.dma_rr[queue]
        self.dma_rr[queue] = (i + 1) % N_DMA_SEMS
        key = f"d{queue}_{i}"
        prev = self.dma_val.get(key, 0)
        if prev:
            self._need(queue, (key, prev))
        val = prev + inc
        self.dma_val[key] = val
        self.sem_names.add(key)
        self.streams[queue].append(("op", fn, key, inc))
        tok = (key, val)
        self._commit(tok, reads, writes)
        return tok

    def barrier(self):
        toks = list(self.last.items())
        for eng in self.streams:
            for t in toks:
                self._need(eng, t)

    def emit(self, block, sems):
        def run(engine_obj, stream):
            for item in stream:
                if item[0] == "wait":
                    engine_obj.wait_ge(sems[item[1]], item[2])
                else:
                    _, fn, key, inc = item
                    fn(engine_obj).then_inc(sems[key], inc)

        @block.tensor
        def _(e):
            run(e, self.streams["pe"])

        @block.scalar
        def _(e):
            run(e, self.streams["act"])

        @block.vector
        def _(e):
            run(e, self.streams["dve"])

        @block.gpsimd
        def _(e):
            run(e, self.streams["pool"])

        @block.sync
        def _(e):
            run(e, self.streams["sp"])


def mm(P, out, lhsT, rhs, start, stop, reads, writes):
    return P.op("pe", lambda e, o=out, l=lhsT, r=rhs, s=start, t=stop:
                e.matmul(o, lhsT=l, rhs=r, start=s, stop=t), reads, writes)


def tr(P, out, in_, ident, reads, writes):
    return P.op("pe", lambda e, o=out, i=in_, d=ident: e.transpose(o, i, d), reads, writes)


def act(P, out, in_, func, reads, writes, bias=None, scale=None):
    kw = {}
    if bias is not None:
        kw["bias"] = bias
    if scale is not None:
        kw["scale"] = scale
    return P.op("act", lambda e, o=out, i=in_, f=func, k=kw: e.activation(out=o, in_=i, func=f, **k), reads, writes)


def tt(P, eng, out, in0, in1, op, reads, writes):
    return P.op(eng, lambda e, o=out, a=in0, b=in1, p=op: e.tensor_tensor(out=o, in0=a, in1=b, op=p), reads, writes)


def ts(P, eng, out, in0, s1, s2, op0, op1, reads, writes):
    if s2 is None:
        return P.op(eng, lambda e, o=out, a=in0, x=s1, p=op0:
                    e.tensor_single_scalar(out=o, in_=a, scalar=x, op=p), reads, writes)
    return P.op(eng, lambda e, o=out, a=in0, x=s1, y=s2, p=op0, q=op1:
                e.tensor_scalar(out=o, in0=a, scalar1=x, scalar2=y, op0=p, op1=q), reads, writes)


def stt(P, eng, out, in0, scalar, in1, op0, op1, reads, writes):
    return P.op(eng, lambda e, o=out, a=in0, s=scalar, b=in1, p=op0, q=op1:
                e.scalar_tensor_tensor(out=o, in0=a, scalar=s, in1=b, op0=p, op1=q), reads, writes)


def cp(P, eng, out, in_, reads, writes):
    if eng == "act":
        return P.op("act", lambda e, o=out, i=in_: e.copy(out=o, in_=i), reads, writes)
    return P.op(eng, lambda e, o=out, i=in_: e.tensor_copy(out=o, in_=i), reads, writes)


def dma(P, q, out, in_, reads, writes):
    return P.dma(q, lambda e, o=out, i=in_: e.dma_start(out=o, in_=i), reads, writes)


class K:
    pass


class _Stop(Exception):
    pass


import os as _os
_KSTOP = [_os.environ.get("KSTOP")]


def ckpt(name):
    if _KSTOP[0] == name:
        raise _Stop()


def build_nc():
    nc = bass.Bass("TRN2", target_bir_lowering=False)
    k = K()
    k.nc = nc

    def din(name, shape):
        return nc.dram_tensor(name, list(shape), F32, kind="ExternalInput").ap()

    def dout(name, shape):
        return nc.dram_tensor(name, list(shape), F32, kind="ExternalOutput").ap()

    def dint(name, shape, dt=F32):
        return nc.dram_tensor(name, list(shape), dt, kind="Internal").ap()

    I = {}
    I["xT"] = din("xT", [D, NT])
    I["condT"] = din("condT", [128, 16, 2])
    I["mod_w_ab"] = din("mod_w_ab", [2, D, 3 * D])
    I["mod_w_c"] = din("mod_w_c", [2, D, 3 * D])
    I["modb"] = din("modb", [4, 128, 48])
    I["normT"] = din("normT", [4, 128, 16])
    I["w_in_ab"] = din("w_in_ab", [2, D, 6976])
    I["w_q_up"] = din("w_q_up", [2, 512, 1536])
    I["w_kv_up"] = din("w_kv_up", [2, 256, 2048])
    I["w_out_ab"] = din("w_out_ab", [2, D, D])
    I["w_in_c"] = din("w_in_c", [2, D, 5120])
    I["w_out_c"] = din("w_out_c", [2, D, D])
    I["vecab"] = din("vecab", [2, 128, 16])
    I["lbl"] = din("lbl", [128, 2, 2, 8])
    I["vecc"] = din("vecc", [2, 128, 18])
    I["ropeA"] = din("ropeA", [2, 64, 1024])
    I["ropeC"] = din("ropeC", [2, 128, 1024])
    I["cst"] = din("cst", [128, 832])
    I["sel"] = din("sel", [128, 2])
    I["ckvT"] = din("ckvT", [2, 256, 256])
    I["kpeT"] = din("kpeT", [2, 64, 256])
    I["s0"] = din("s0", [2, 8, 128, 128])
    I["kcT"] = din("kcT", [2, 4, 128, 256])
    I["vc"] = din("vc", [2, 256, 512])
    O = {}
    O["yT"] = dout("yT", [D, NT])
    O["o_ckv"] = dout("o_ckv", [2, 256, 512])
    O["o_kpe"] = dout("o_kpe", [2, 64, 512])
    O["o_st"] = dout("o_st", [2, 2, 2, 8, 128, 128])
    O["o_kc"] = dout("o_kc", [2, 4, 128, 512])
    O["o_vc"] = dout("o_vc", [2, 512, 512])
    X = {}
    X["xs"] = [dint("xs0", [D, NT]), dint("xs1", [D, NT])]
    X["b_lat"] = dint("b_lat", [320, 1024])
    X["g_lat"] = dint("g_lat", [640, 1024])
    X["b_st"] = [dint(f"b_st{h}", [128, 128]) for h in range(8)]
    X["g_st"] = [dint(f"g_st{h}", [256, 128]) for h in range(8)]
    X["b_win"] = dint("b_win", [128, 1024])
    X["g_win"] = dint("g_win", [256, 1024])
    k.I, k.O, k.X = I, O, X

    with ExitStack() as es:
        def sb(name, shape, dt):
            return es.enter_context(nc.sbuf_tensor(name, list(shape), dt))

        k.HT = sb("HT", [128, 16, NT], BF16)
        k.OT = sb("OT", [128, 16, NT], BF16)
        k.W = [sb(f"W{i}", [128, 16, 256], BF16) for i in range(2)]
        k.SCR = sb("SCR", [128, SCRN], F32)
        k.MW = sb("MW", [128, 16, 128], BF16)
        k.CST = sb("CST", [128, 832], F32)
        k.CSTB = sb("CSTB", [128, 832], BF16)
        k.ONES = sb("ONES", [128, 128], BF16)
        k.ONEF = sb("ONEF", [128, NT], BF16)
        k.SEL = sb("SEL", [128, 2], F32)
        k.MOD = sb("MOD", [128, 4, 48, 2], F32)
        k.AMOD = sb("AMOD", [128, 4, 16, 2], F32)
        k.MODB = sb("MODB", [128, 4, 48], F32)
        k.NRM = sb("NRM", [128, 4, 16], F32)
        k.SC = sb("SC", [128, 16, 2], BF16)
        k.CONDF = sb("CONDF", [128, 16, 2], F32)
        k.VAB = sb("VAB", [128, 2, 16], F32)
        k.LBL = sb("LBL", [128, 2, 2, 8], F32)
        k.LB = sb("LB", [128, 2, 2, 8], F32)
        k.OML = sb("OML", [128, 2, 2, 8], F32)
        k.VC = sb("VC", [128, 2, 18], F32)
        k.ESINK = sb("ESINK", [128, 2, 16], F32)
        k.RPA = sb("RPA", [64, 2, 1024], F32)
        k.RPC = sb("RPC", [128, 2, 1024], F32)
        k.PS = [es.enter_context(nc.psum_tensor(f"ps{i}", [128, 512], F32)) for i in range(8)]
        k.PSB = k.PS[7]
        P = Prog(nc)
        k.P = P
        k.wi = 0
        k.modq = []
        k.mod_loaded = None
        k.mod_rate = 1
        try:
            program(k)
        except _Stop:
            P.barrier()
        sems = {s: es.enter_context(nc.semaphore(s)) for s in sorted(P.sem_names)}
        with nc.Block() as block:
            P.emit(block, sems)
    return nc


class Scr:
    def __init__(self, k, tag):
        self.k, self.tag, self.f = k, tag, 0

    def F(self, name, rows, *shape):
        n = int(np.prod(shape))
        ap = self.k.SCR[0:rows, self.f:self.f + n]
        self.f += n
        assert self.f <= SCRN, (self.tag, name, self.f)
        if len(shape) == 2:
            ap = ap.rearrange("p (a b) -> p a b", b=shape[1])
        elif len(shape) == 3:
            ap = ap.rearrange("p (a b c) -> p a b c", b=shape[1], c=shape[2])
        return ap

    def B(self, name, rows, *shape):
        n = int(np.prod(shape))
        nf = (n + 1) // 2
        ap = self.k.SCR[0:rows, self.f:self.f + nf].bitcast(BF16)[:, 0:n]
        self.f += nf
        assert self.f <= SCRN, (self.tag, name, self.f)
        if len(shape) == 2:
            ap = ap.rearrange("p (a b) -> p a b", b=shape[1])
        elif len(shape) == 3:
            ap = ap.rearrange("p (a b c) -> p a b c", b=shape[1], c=shape[2])
        return ap


def mod_consume(k):
    if k.mod_loaded is None:
        return
    P = k.P
    layer, n = k.mod_loaded
    for kc in range(16):
        mm(P, k.PS[7][:, 0:2], k.MW[:, kc, :], k.SC[:, kc, :], kc == 0, kc == 15, ("MW", "SC"), ("ps7",))
    ts(P, "dve", k.MOD[:, layer, n, :], k.PS[7][:, 0:2], k.MODB[:, layer, n:n + 1], None, ALU.add, None,
       ("ps7", "MODB"), (f"MOD{layer}",))
    k.mod_loaded = None


def mod_issue_load(k):
    if not k.modq:
        return
    P, I = k.P, k.I
    layer, n = k.modq.pop(0)
    wsrc = (I["mod_w_ab"] if layer % 2 == 0 else I["mod_w_c"])[layer // 2][:, n * 128:(n + 1) * 128]
    P.dma("pool", lambda e, s_=wsrc.rearrange("(c p) n -> p c n", p=128): e.dma_start(out=k.MW[:], in_=s_),
          (), ("MW",))
    k.mod_loaded = (layer, n)


def mod_step(k, times=1):
    for _ in range(times):
        mod_consume(k)
        mod_issue_load(k)


def mod_flush(k, layer):
    P = k.P
    while k.mod_loaded is not None or k.modq:
        mod_consume(k)
        mod_issue_load(k)
    ts(P, "dve", k.AMOD[:, layer], k.MOD[:, layer, 16:32, :], 1.0, None, ALU.add, None,
       (f"MOD{layer}",), (f"AMOD{layer}",))
    tt(P, "dve", k.AMOD[:, layer], k.AMOD[:, layer],
       k.NRM[:, layer, :].unsqueeze(2).broadcast_to([128, 16, 2]), ALU.mult, (f"AMOD{layer}", "NRM"),
       (f"AMOD{layer}",))


def load_w(k, src, nk, ncols):
    P = k.P
    mod_step(k, k.mod_rate)
    i = k.wi
    k.wi = (i + 1) % 2
    buf = k.W[i][:, 0:nk, 0:ncols]
    rn = f"W{i}"
    P.dma("pool", lambda e, o=buf, s=src.rearrange("(c p) n -> p c n", p=128): e.dma_start(out=o, in_=s),
          reads=(), writes=(rn,))
    return buf, rn


def rstd_from_ps(k, out, ps, n, reads, writes):
    P = k.P
    act(P, out, ps, AF.Ln, reads, writes, bias=EPS, scale=1.0 / n)
    act(P, out, out, AF.Exp, writes, writes, scale=-0.5)


def program(k):
    P, I, O, X = k.P, k.I, k.O, k.X
    dma(P, "sp", k.CST[:], I["cst"], (), ("CST",))
    P.dma("pool", lambda e: e.dma_start(out=k.CSTB[:], in_=I["cst"]), (), ("CSTB",))
    P.op("pool", lambda e: e.memset(k.ONES[:], 1.0), (), ("ONES",))
    P.op("pool", lambda e: e.memset(k.ONEF[:], 1.0), (), ("ONEF",))
    dma(P, "sp", k.SEL[:], I["sel"], (), ("SEL",))
    dma(P, "sp", k.CONDF[:], I["condT"], (), ("CONDF",))
    dma(P, "sp", k.MODB[:], I["modb"].rearrange("l p n -> p l n"), (), ("MODB",))
    dma(P, "sp", k.NRM[:], I["normT"].rearrange("l p n -> p l n"), (), ("NRM",))
    dma(P, "sp", k.VAB[:], I["vecab"].rearrange("l p n -> p l n"), (), ("VAB",))
    dma(P, "sp", k.LBL[:], I["lbl"], (), ("LBL",))
    dma(P, "sp", k.VC[:], I["vecc"].rearrange("l p n -> p l n"), (), ("VC",))
    dma(P, "sp", k.RPA[:], I["ropeA"].rearrange("l p n -> p l n"), (), ("RPA",))
    dma(P, "sp", k.RPC[:], I["ropeC"].rearrange("l p n -> p l n"), (), ("RPC",))
    act(P, k.SC[:], k.CONDF[:], AF.Silu, ("CONDF",), ("SC",))
    act(P, k.ESINK[:], k.VC[:, :, 2:18], AF.Exp, ("VC",), ("ESINK",))
    P.op("pool", lambda e: e.memset(k.LB[:, 0], 0.0), (), ("LB",))
    tt(P, "dve", k.LB[:, 1], k.LBL[:, 1], k.LBL[:, 0], ALU.subtract, ("LBL", "LB"), ("LB",))
    act(P, k.LB[:, 1], k.LB[:, 1], AF.Sigmoid, ("LB",), ("LB",))
    ts(P, "dve", k.OML[:], k.LB[:], -1.0, 1.0, ALU.mult, ALU.add, ("LB",), ("OML",))

    ckpt("const")
    modulation(k, 0)
    ckpt("mod")
    for layer in range(4):
        j = layer // 2
        if layer < 3:
            k.modq = [(layer + 1, n) for n in range(48)]
            k.mod_rate = 1 if layer % 2 == 0 else 2
        xin = I["xT"] if layer == 0 else X["xs"][(layer - 1) % 2]
        xout = O["yT"] if layer == 3 else X["xs"][layer % 2]
        norm_mod(k, layer, xin)
        ckpt(f"nm{layer}")
        if layer % 2 == 0:
            ab_layer(k, j)
            wo = I["w_out_ab"][j]
        else:
            c_layer(k, j)
            wo = I["w_out_c"][j]
        ckpt(f"mix{layer}")
        out_proj(k, layer, wo, xin, xout)
        if layer < 3:
            mod_flush(k, layer + 1)
        ckpt(f"out{layer}")
    P.barrier()


def modulation(k, layer):
    P, I = k.P, k.I
    wsrc = (I["mod_w_ab"] if layer % 2 == 0 else I["mod_w_c"])[layer // 2]
    for g in range(24):
        wb, rn = load_w(k, wsrc[:, g * 256:(g + 1) * 256], 16, 256)
        for h in range(2):
            n = g * 2 + h
            ps = k.PS[n % 2]
            pr = f"ps{n % 2}"
            for kc in range(16):
                mm(P, ps[:, 0:2], wb[:, kc, h * 128:(h + 1) * 128], k.SC[:, kc, :], kc == 0, kc == 15,
                   (rn, "SC"), (pr,))
            ts(P, "dve", k.MOD[:, layer, n, :], ps[:, 0:2], k.MODB[:, layer, n:n + 1], None, ALU.add, None,
               (pr, "MODB"), (f"MOD{layer}",))
    ts(P, "dve", k.AMOD[:, layer], k.MOD[:, layer, 16:32, :], 1.0, None, ALU.add, None,
       (f"MOD{layer}",), (f"AMOD{layer}",))
    tt(P, "dve", k.AMOD[:, layer], k.AMOD[:, layer],
       k.NRM[:, layer, :].unsqueeze(2).broadcast_to([128, 16, 2]), ALU.mult, (f"AMOD{layer}", "NRM"),
       (f"AMOD{layer}",))


def norm_mod(k, layer, xin):
    P = k.P
    P.barrier()
    s = Scr(k, "nm")
    XC = [s.F(f"xc{i}", 128, BS) for i in range(4)]
    SQ = [s.B(f"sq{i}", 128, BS) for i in range(2)]
    RS = s.F("rs", 128, BS)
    T = [s.F(f"t{i}", 128, BS) for i in range(2)]
    n = 0
    for tb in range(NBK):
        c = 0 if tb == 0 else 1
        cols = slice(tb * BS, (tb + 1) * BS)
        ps = k.PS[2 + tb % 2]
        pr = f"ps{2 + tb % 2}"
        for fc in range(16):
            xi = n % 4
            n += 1
            dma(P, "sp", XC[xi], xin[fc * 128:(fc + 1) * 128, cols], ("xin",), (f"nm_xc{xi}",))
            act(P, SQ[fc % 2], XC[xi], AF.Square, (f"nm_xc{xi}",), (f"nm_sq{fc % 2}",))
            mm(P, ps[:, :], k.ONES[:], SQ[fc % 2], fc == 0, fc == 15, ("ONES", f"nm_sq{fc % 2}"), (pr,))
        rstd_from_ps(k, RS, ps[:, :], D, (pr,), ("nm_rs",))
        for fc in range(16):
            xi = n % 4
            n += 1
            dma(P, "sp", XC[xi], xin[fc * 128:(fc + 1) * 128, cols], ("xin",), (f"nm_xc{xi}",))
            stt(P, "dve", T[fc % 2], XC[xi], k.AMOD[:, layer, fc, c:c + 1], RS, ALU.mult, ALU.mult,
                (f"nm_xc{xi}", "nm_rs", f"AMOD{layer}"), (f"nm_t{fc % 2}",))
            act(P, k.HT[:, fc, cols], T[fc % 2], AF.Identity, (f"nm_t{fc % 2}", f"MOD{layer}"), (f"HT{tb}",),
                bias=k.MOD[:, layer, fc, c:c + 1])


def out_proj(k, layer, wo, xin, xout):
    P = k.P
    P.barrier()
    s = Scr(k, "op")
    XC = [s.F(f"xc{i}", 128, BS) for i in range(4)]
    n = 0
    for g in range(8):
        wb, rn = load_w(k, wo[:, g * 256:(g + 1) * 256], 16, 256)
        for h in range(2):
            oc = g * 2 + h
            for tb in range(NBK):
                c = 0 if tb == 0 else 1
                cols = slice(tb * BS, (tb + 1) * BS)
                ps = k.PS[n % 4]
                pr = f"ps{n % 4}"
                xi = n % 4
                n += 1
                dma(P, "sp", XC[xi], xin[oc * 128:(oc + 1) * 128, cols], ("xin",), (f"op_xc{xi}",))
                for kc in range(16):
                    mm(P, ps[:, :], wb[:, kc, h * 128:(h + 1) * 128], k.OT[:, kc, cols], kc == 0, kc == 15,
                       (rn, f"OT{tb}"), (pr,))
                stt(P, "dve", XC[xi], ps[:, :], k.MOD[:, layer, 32 + oc, c:c + 1], XC[xi], ALU.mult, ALU.add,
                    (pr, f"op_xc{xi}", f"MOD{layer}"), (f"op_xc{xi}",))
                dma(P, "sp", xout[oc * 128:(oc + 1) * 128, cols], XC[xi], (f"op_xc{xi}",), ("xout",))
    P.res["xin"] = {"w": None, "r": {}}
    P.barrier()


def proj_block(k, wb, rn, c0, ncol, tb, ps, pr, nk=16, rhs=None, rres=None):
    P = k.P
    cols = slice(tb * BS, (tb + 1) * BS)
    for kc in range(nk):
        r = k.HT[:, kc, cols] if rhs is None else rhs[:, kc, cols]
        mm(P, ps[0:ncol, :], wb[:, kc, c0:c0 + ncol], r, kc == 0, kc == nk - 1,
           (rn, rres or f"HT{tb}"), (pr,))


def ab_layer(k, j):
    P, I, O, X = k.P, k.I, k.O, k.X
    P.barrier()
    s = Scr(k, "mla")
    QLN = k.OT[:, 8:12, :]
    CKV = k.OT[:, 12:16, :].rearrange("p a b -> p (a b)")[:, 0:2 * NKEY].rearrange("p (c n) -> p c n", c=2)
    KPEG = s.B("kpeg", 64, NKEY)
    KPSQ = s.B("kpsq", 64, NKEY)
    KTN = s.B("ktn", 128, NKEY)
    KTP = s.B("ktp", 64, NKEY)
    VH = s.B("vh", 128, 22, 128)
    QTN = s.B("qtn", 128, NT)
    QTP = s.B("qtp", 64, NT)
    PT = [s.B(f"pt{i}", 128, BS) for i in range(2)]
    SQb = [s.B(f"sqb{i}", 128, BS) for i in range(2)]
    TF = [s.F(f"tf{i}", 128, BS) for i in range(6)]
    RS = s.F("rs", 128, BS)
    w_in = I["w_in_ab"][j]
    RA = k.CST[0:64, 640:704]
    vab = k.VAB[:, j, :]

    def rope64(dst, x, tb, xres, dres):
        tc_ = slice((tb - 1) * BS, tb * BS)
        mm(P, k.PS[5][0:64, :], RA, x, True, True, ("CST", xres), ("ps5",))
        tt(P, "dve", TF[3][0:64, :], k.PS[5][0:64, :], k.RPA[:, 1, tc_], ALU.mult, ("ps5", "RPA"), ("tf3",))
        tt(P, "dve", x, x, k.RPA[:, 0, tc_], ALU.mult, (xres, "RPA"), (xres,))
        tt(P, "dve", dst, x, TF[3][0:64, :], ALU.add, (xres, "tf3"), (dres,))

    P.dma("pool", lambda e: e.dma_start(out=CKV[:, :, 512:768], in_=I["ckvT"][j].rearrange("(c p) n -> p c n", p=128)),
          (), ("CKVctx",))
    dma(P, "sp", TF[0][0:64, 0:256], I["kpeT"][j], (), ("tf0",))
    act(P, KPSQ[:, 512:768], TF[0][0:64, 0:256], AF.Square, ("tf0",), ("KPSQctx",))
    ts(P, "dve", KPEG[:, 512:768], TF[0][0:64, 0:256], vab[0:64, 9:10], None, ALU.mult, None, ("tf0", "VAB"),
       ("KPEGctx",))

    wq = []
    for g in range(2):
        wq.append(load_w(k, w_in[:, O_QL + g * 256:O_QL + (g + 1) * 256], 16, 256))
    for tb in range(NBK):
        cols = slice(tb * BS, (tb + 1) * BS)
        for c in range(4):
            wb, rn = wq[c // 2]
            proj_block(k, wb, rn, (c % 2) * 128, 128, tb, k.PS[c], f"ps{c}")
            cp(P, "act", TF[c], k.PS[c][:, :], (f"ps{c}",), (f"tf{c}",))
            act(P, SQb[c % 2], TF[c], AF.Square, (f"tf{c}",), (f"sqb{c % 2}",))
            mm(P, k.PS[4][:, :], k.ONES[:], SQb[c % 2], c == 0, c == 3, ("ONES", f"sqb{c % 2}"), ("ps4",))
        rstd_from_ps(k, RS, k.PS[4][:, :], 512, ("ps4",), ("rs",))
        for c in range(4):
            stt(P, "dve", QLN[:, c, cols], TF[c], vab[:, c:c + 1], RS, ALU.mult, ALU.mult,
                (f"tf{c}", "rs", "VAB"), (f"QLN{tb}",))
    wkv = load_w(k, w_in[:, O_KV:O_KV + 256], 16, 256)
    wkp = load_w(k, w_in[:, O_KPE:O_KPE + 64], 16, 64)
    for tb in range(NBK):
        cols = slice(tb * BS, (tb + 1) * BS)
        for c in range(2):
            proj_block(k, wkv[0], wkv[1], c * 128, 128, tb, k.PS[c], f"ps{c}")
            cp(P, "act", TF[c], k.PS[c][:, :], (f"ps{c}",), (f"tf{c}",))
            act(P, SQb[c % 2], TF[c], AF.Square, (f"tf{c}",), (f"sqb{c % 2}",))
            mm(P, k.PS[4][:, :], k.ONES[:], SQb[c % 2], c == 0, c == 1, ("ONES", f"sqb{c % 2}"), ("ps4",))
        rstd_from_ps(k, RS, k.PS[4][:, :], 256, ("ps4",), ("rs",))
        proj_block(k, wkp[0], wkp[1], 0, 64, tb, k.PS[2], "ps2")
        cp(P, "act", TF[4][0:64, :], k.PS[2][0:64, :], ("ps2",), ("tf4",))
        for c in range(2):
            stt(P, "dve", TF[c], TF[c], vab[:, 4 + c:5 + c], RS, ALU.mult, ALU.mult,
                (f"tf{c}", "rs", "VAB"), (f"tf{c}",))
        if tb == 0:
            for c in range(2):
                dma(P, "sp", O["o_ckv"][j, c * 128:(c + 1) * 128, :], TF[c], (f"tf{c}",), ("o_ckv",))
                cp(P, "act", CKV[:, c, 0:512], TF[c], (f"tf{c}",), ("CKVp",))
            dma(P, "sp", O["o_kpe"][j], TF[4][0:64, :], ("tf4",), ("o_kpe",))
            act(P, KPSQ[:, 0:512], TF[4][0:64, :], AF.Square, ("tf4",), ("KPSQp",))
            ts(P, "dve", KPEG[:, 0:512], TF[4][0:64, :], vab[0:64, 9:10], None, ALU.mult, None, ("tf4", "VAB"),
               ("KPEGp",))
        else:
            lc = slice((tb - 1) * BS, tb * BS)
            for c in range(2):
                dma(P, "sp", X["b_lat"][c * 128:(c + 1) * 128, lc], TF[c], (f"tf{c}",), ("b_lat",))
            dma(P, "sp", X["b_win"][0:64, lc], TF[4][0:64, :], ("tf4",), ("b_win",))
            ts(P, "dve", TF[5][0:64, :], TF[4][0:64, :], vab[0:64, 9:10], None, ALU.mult, None, ("tf4", "VAB"),
               ("tf5",))
            rope64(TF[2][0:64, :], TF[5][0:64, :], tb, "tf5", "tf2")
            dma(P, "sp", X["b_lat"][256:320, lc], TF[2][0:64, :], ("tf2",), ("b_lat",))
    P.dma("pool", lambda e: e.collective_compute("AllGather", ALU.bypass, replica_groups=RG,
                                                 ins=[X["b_lat"].opt()], outs=[X["g_lat"].opt()]),
          ("b_lat",), ("g_lat",), inc=1)
    P.dma("pool", lambda e: e.collective_compute("AllGather", ALU.bypass, replica_groups=RG,
                                                 ins=[X["b_win"].opt()], outs=[X["g_win"].opt()]),
          ("b_win",), ("g_win",), inc=1)
    for r in range(2):
        kc_ = slice(768 + r * 1024, 768 + (r + 1) * 1024)
        P.dma("pool", lambda e, r=r, kc_=kc_: e.dma_start(
            out=CKV[:, :, kc_], in_=X["g_lat"][r * 320:r * 320 + 256, :].rearrange("(c p) n -> p c n", p=128)),
            ("g_lat",), (f"CKVr{r}",))
        P.dma("pool", lambda e, r=r, kc_=kc_: e.dma_start(out=KPEG[:, kc_], in_=X["g_lat"][r * 320 + 256:r * 320 + 320, :]),
              ("g_lat",), (f"KPEGr{r}",))
        for hb in range(2):
            lc = slice(hb * BS, (hb + 1) * BS)
            kq = slice(768 + r * 1024 + hb * BS, 768 + r * 1024 + (hb + 1) * BS)
            dma(P, "sp", TF[3][0:64, :], X["g_win"][r * 128:r * 128 + 64, lc], ("g_win",), ("tf3",))
            act(P, KPSQ[:, kq], TF[3][0:64, :], AF.Square, ("tf3",), (f"KPSQr{r}",))
    ckpt(f"lat{j}")
    KRES = ("CKVctx", "CKVp", "CKVr0", "CKVr1")
    PRES = ("KPEGctx", "KPEGp", "KPEGr0", "KPEGr1")
    SRES = ("KPSQctx", "KPSQp", "KPSQr0", "KPSQr1")

    KB = [(i * 512, min(512, NKEY - i * 512)) for i in range(6)]
    for h in range(8):
        wqn = load_w(k, I["w_q_up"][j][:, h * 192:(h + 1) * 192], 4, 192)
        wkv_ = load_w(k, I["w_kv_up"][j][:, h * 256:(h + 1) * 256], 2, 256)
        for tb in range(NBK):
            cols = slice(tb * BS, (tb + 1) * BS)
            proj_block(k, wqn[0], wqn[1], 0, 128, tb, k.PS[0], "ps0", nk=4, rhs=QLN, rres=f"QLN{tb}")
            proj_block(k, wqn[0], wqn[1], 128, 64, tb, k.PS[1], "ps1", nk=4, rhs=QLN, rres=f"QLN{tb}")
            cp(P, "act", TF[0], k.PS[0][:, :], ("ps0",), ("tf0",))
            cp(P, "act", TF[1][0:64, :], k.PS[1][0:64, :], ("ps1",), ("tf1",))
            act(P, SQb[0], TF[0], AF.Square, ("tf0",), ("sqb0",))
            act(P, SQb[1][0:64, :], TF[1][0:64, :], AF.Square, ("tf1",), ("sqb1",))
            mm(P, k.PS[4][:, :], k.ONES[:], SQb[0], True, False, ("ONES", "sqb0"), ("ps4",))
            mm(P, k.PS[4][:, :], k.ONES[0:64, :], SQb[1][0:64, :], False, True, ("ONES", "sqb1"), ("ps4",))
            rstd_from_ps(k, RS, k.PS[4][:, :], 192, ("ps4",), ("rs",))
            stt(P, "dve", QTN[:, cols], TF[0], vab[:, 6:7], RS, ALU.mult, ALU.mult, ("tf0", "rs", "VAB"),
                (f"QTN{tb}",))
            stt(P, "dve", TF[2][0:64, :], TF[1][0:64, :], vab[0:64, 7:8], RS[0:64, :], ALU.mult, ALU.mult,
                ("tf1", "rs", "VAB"), ("tf2",))
            if tb == 0:
                cp(P, "dve", QTP[:, cols], TF[2][0:64, :], ("tf2",), (f"QTP{tb}",))
            else:
                rope64(QTP[:, cols], TF[2][0:64, :], tb, "tf2", f"QTP{tb}")
        for bi, (c0, n) in enumerate(KB):
            kc_ = slice(c0, c0 + n)
            ps = k.PS[bi % 2]
            pr = f"ps{bi % 2}"
            for c in range(2):
                mm(P, ps[:, 0:n], wkv_[0][:, c, 0:128], CKV[:, c, kc_], c == 0, c == 1, (wkv_[1],) + KRES, (pr,))
            cp(P, "act", TF[bi % 2][:, 0:n], ps[:, 0:n], (pr,), (f"tf{bi % 2}",))
            act(P, SQb[bi % 2][:, 0:n], TF[bi % 2][:, 0:n], AF.Square, (f"tf{bi % 2}",), (f"sqb{bi % 2}",))
            ps2 = k.PS[2 + bi % 2]
            pr2 = f"ps{2 + bi % 2}"
            mm(P, ps2[:, 0:n], k.ONES[:], SQb[bi % 2][:, 0:n], True, False, ("ONES", f"sqb{bi % 2}"), (pr2,))
            mm(P, ps2[:, 0:n], k.ONES[0:64, :], KPSQ[:, kc_], False, True, ("ONES",) + SRES, (pr2,))
            rstd_from_ps(k, RS[:, 0:n], ps2[:, 0:n], 192, (pr2,), ("rs",))
            stt(P, "dve", KTN[:, kc_], TF[bi % 2][:, 0:n], vab[:, 8:9], RS[:, 0:n], ALU.mult, ALU.mult,
                (f"tf{bi % 2}", "rs", "VAB"), ("KTN",))
            tt(P, "dve", KTP[:, kc_], KPEG[:, kc_], RS[0:64, 0:n], ALU.mult, PRES + ("rs",), ("KTP",))
        for g in range(6):
            nt_ = min(4, 22 - g * 4)
            ps = k.PS[g % 2]
            pr = f"ps{g % 2}"
            for t in range(nt_):
                kt = g * 4 + t
                for c in range(2):
                    mm(P, ps[:, t * 128:(t + 1) * 128], CKV[:, c, kt * 128:(kt + 1) * 128], wkv_[0][:, c, 128:256],
                       c == 0, c == 1, (wkv_[1],) + KRES, (pr,))
            cp(P, "act", VH[:, g * 4:g * 4 + nt_, :], ps[:, 0:nt_ * 128].rearrange("p (a b) -> p a b", b=128),
               (pr,), ("VH",))
        wg = load_w(k, w_in[:, O_AG + h * 128:O_AG + (h + 1) * 128], 16, 128)
        groups = [(0, 256, [0, 1]), (256, 256, [2, 3]), (512, 512, list(range(4, 22))),
                  (1024, 512, list(range(4, 22)))]
        for gi, (q0, qn, kts) in enumerate(groups):
            qs = slice(q0, q0 + qn)
            tb = q0 // BS
            Ops, Dps = k.PS[4], k.PS[5]
            def s_mm(ki):
                ks = slice(kts[ki] * 128, (kts[ki] + 1) * 128)
                sp_ = k.PS[ki % 2]
                spr = f"ps{ki % 2}"
                mm(P, sp_[:, 0:qn], KTN[:, ks], QTN[:, qs], True, False, ("KTN", f"QTN{tb}"), (spr,))
                mm(P, sp_[:, 0:qn], KTP[:, ks], QTP[:, qs], False, True, ("KTP", f"QTP{tb}"), (spr,))

            s_mm(0)
            for ki, kt in enumerate(kts):
                sp_ = k.PS[ki % 2]
                spr = f"ps{ki % 2}"
                if ki + 1 < len(kts):
                    s_mm(ki + 1)
                act(P, PT[ki % 2][:, 0:qn], sp_[:, 0:qn], AF.Exp, (spr,), (f"pt{ki % 2}",), scale=192 ** -0.5)
                mm(P, Ops[:, 0:qn], VH[:, kt, :], PT[ki % 2][:, 0:qn], ki == 0, ki == len(kts) - 1,
                   ("VH", f"pt{ki % 2}"), ("ps4",))
                mm(P, Dps[:, 0:qn], k.ONES[:], PT[ki % 2][:, 0:qn], ki == 0, ki == len(kts) - 1,
                   ("ONES", f"pt{ki % 2}"), ("ps5",))
            act(P, TF[4][:, 0:qn], Dps[:, 0:qn], AF.Ln, ("ps5",), ("tf4",))
            act(P, TF[4][:, 0:qn], TF[4][:, 0:qn], AF.Exp, ("tf4",), ("tf4",), scale=-1.0)
            tt(P, "dve", TF[5][:, 0:qn], Ops[:, 0:qn], TF[4][:, 0:qn], ALU.mult, ("ps4", "tf4"), ("tf5",))
            gp = k.PS[2 + gi % 2]
            gpr = f"ps{2 + gi % 2}"
            for kc in range(16):
                mm(P, gp[:, 0:qn], wg[0][:, kc, :], k.HT[:, kc, qs], kc == 0, kc == 15, (wg[1], f"HT{tb}"), (gpr,))
            act(P, TF[3][:, 0:qn], gp[:, 0:qn], AF.Silu, (gpr,), ("tf3",))
            tt(P, "dve", k.OT[:, h, qs], TF[5][:, 0:qn], TF[3][:, 0:qn], ALU.mult, ("tf5", "tf3"), (f"OT{tb}",))
    ckpt(f"mla{j}")
    hgrn(k, j)


def hgrn(k, j):
    P, I, O, X = k.P, k.I, k.O, k.X
    P.barrier()
    s = Scr(k, "hg")
    T1 = s.F("t1", 128, NT)
    T2 = s.F("t2", 128, NT)
    T3 = s.F("t3", 128, NT)
    T4 = s.F("t4", 128, NT)
    OPH = s.F("oph", 128, NT)
    OAC = OPH
    CS = s.F("cs", 128, 6, NCH)
    ST = s.F("st", 128, 128)
    ST1 = s.F("st1", 128, 128)
    GST = s.F("gst", 128, 2, 128)
    Qb = s.B("q", 128, NT)
    Kb = s.B("kb", 128, NT)
    QTl = s.B("qtl", 128, NT)
    KTl = s.B("ktl", 128, NT)
    KTM = s.B("ktm", 128, 12, 128)
    Gb = s.B("g", 128, NT)
    Vb = s.B("v", 128, 12, 128)
    AT = s.B("at", 128, NT // 2)
    STb = s.B("stb", 128, 128)
    SQb = s.B("sqb", 128, BS)
    w_in = I["w_in_ab"][j]
    identb = k.CSTB[:, 0:128]
    vab = k.VAB[:, j, :]
    SEGS = [(0, 256), (256, 256), (512, 1024)]

    for h in range(8):
        wqg = load_w(k, w_in[:, O_BQ + h * 128:O_BQ + (h + 1) * 128], 16, 128)
        for tb in range(NBK):
            cols = slice(tb * BS, (tb + 1) * BS)
            proj_block(k, wqg[0], wqg[1], 0, 128, tb, k.PS[tb % 2], f"ps{tb % 2}")
            act(P, Qb[:, cols], k.PS[tb % 2][:, :], AF.Silu, (f"ps{tb % 2}",), ("hq",))
        wgg = load_w(k, w_in[:, O_BG + h * 128:O_BG + (h + 1) * 128], 16, 128)
        for tb in range(NBK):
            cols = slice(tb * BS, (tb + 1) * BS)
            proj_block(k, wgg[0], wgg[1], 0, 128, tb, k.PS[2 + tb % 2], f"ps{2 + tb % 2}")
            act(P, Gb[:, cols], k.PS[2 + tb % 2][:, :], AF.Silu, (f"ps{2 + tb % 2}",), ("hg",))
        wv = load_w(k, w_in[:, O_BI + h * 128:O_BI + (h + 1) * 128], 16, 128)
        for g in range(3):
            ps = k.PS[g % 2]
            pr = f"ps{g % 2}"
            for t in range(4):
                tl = g * 4 + t
                for kc in range(16):
                    mm(P, ps[:, t * 128:(t + 1) * 128], k.HT[:, kc, tl * 128:(tl + 1) * 128], wv[0][:, kc, :],
                       kc == 0, kc == 15, (wv[1], f"HT{tl // 4}"), (pr,))
            cp(P, "act", Vb[:, g * 4:(g + 1) * 4, :], ps[:, :].rearrange("p (a b) -> p a b", b=128), (pr,), ("hv",))

        for ph in range(2):
            wf = load_w(k, w_in[:, (O_F1, O_F2)[ph] + h * 128:(O_F1, O_F2)[ph] + (h + 1) * 128], 16, 128)
            lb = k.LB[:, j, ph, h:h + 1]
            oml = k.OML[:, j, ph, h:h + 1]
            for tb in range(NBK):
                cols = slice(tb * BS, (tb + 1) * BS)
                proj_block(k, wf[0], wf[1], 0, 128, tb, k.PS[tb % 2], f"ps{tb % 2}")
                act(P, T1[:, cols], k.PS[tb % 2][:, :], AF.Sigmoid, (f"ps{tb % 2}",), ("t1",))
            ts(P, "dve", T1, T1, oml, lb, ALU.mult, ALU.add, ("t1", "LB", "OML"), ("t1",))
            act(P, T2, T1, AF.Ln, ("t1",), ("t2",))
            act(P, Kb, T1, AF.Identity, ("t1",), ("hk",), bias=1.0, scale=-1.0)
            P.op("dve", lambda e: e.tensor_tensor_scan(out=T1, data0=k.ONEF[:], data1=T2, initial=0.0,
                                                       op0=ALU.mult, op1=ALU.add), ("t2", "ONEF", "hk"), ("t1",))
            tt(P, "dve", T2, T1, T2, ALU.subtract, ("t1", "t2"), ("t2",))
            Bv = T1.rearrange("p (c t) -> p c t", t=HC)
            Xv = T2.rearrange("p (c t) -> p c t", t=HC)
            lo, hi, mid, din_, d2, d1 = (CS[:, i, :] for i in range(6))
            cp(P, "dve", lo, Xv[:, :, 0], ("t2",), ("cs",))
            cp(P, "dve", hi, Bv[:, :, HC - 1], ("t1", "cs"), ("cs",))
            if ph == 0:
                cp(P, "dve", mid, Bv[:, :, HC // 2], ("t1", "cs"), ("cs",))
                tt(P, "dve", T3.rearrange("p (c t) -> p c t", t=HC), Bv,
                   mid.unsqueeze(2).broadcast_to([128, NCH, HC]), ALU.subtract, ("t1", "cs"), ("t3",))
                tt(P, "dve", din_, mid, lo, ALU.subtract, ("cs",), ("cs",))
                tt(P, "dve", d2, hi, mid, ALU.subtract, ("cs",), ("cs",))
            else:
                cp(P, "dve", mid, Xv[:, :, HC // 2], ("t2", "cs"), ("cs",))
                tt(P, "dve", T3.rearrange("p (c t) -> p c t", t=HC),
                   mid.unsqueeze(2).broadcast_to([128, NCH, HC]), Xv, ALU.subtract, ("t2", "cs"), ("t3",))
                tt(P, "dve", din_, hi, mid, ALU.subtract, ("cs",), ("cs",))
                tt(P, "dve", d2, mid, lo, ALU.subtract, ("cs",), ("cs",))
            tt(P, "dve", d1, hi, lo, ALU.subtract, ("cs",), ("cs",))
            act(P, CS[:, 3:6, :], CS[:, 3:6, :], AF.Exp, ("cs",), ("cs",))
            act(P, T4, T3, AF.Exp, ("t3",), ("t4",))
            tt(P, "dve", QTl, Qb, T4, ALU.mult, ("hq", "t4"), ("qtl",))
            act(P, T4, T3, AF.Exp, ("t3", "qtl"), ("t4",), scale=-1.0)
            tt(P, "dve", KTl, Kb, T4, ALU.mult, ("hk", "t4"), ("ktl",))
            for tl in range(12):
                psb = k.PS[6][:, :].bitcast(BF16)
                o = psb[:, (tl % 4) * 128:(tl % 4 + 1) * 128]
                tr(P, o, KTl[:, tl * 128:(tl + 1) * 128], identb, ("ktl", "CSTB"), ("ps6",))
                if tl % 4 == 3:
                    cp(P, "act", KTM[:, tl - 3:tl + 1, :], psb[:, 0:512].rearrange("p (a b) -> p a b", b=128),
                       ("ps6",), ("ktm",))
            CPT = 128 // HC
            for g in range(3):
                ps = k.PS[g % 2]
                pr = f"ps{g % 2}"
                for t in range(4):
                    tl = g * 4 + t
                    for ci in range(CPT):
                        c0 = tl * 128 + ci * HC
                        mm(P, ps[ci * HC:(ci + 1) * HC, t * HC:(t + 1) * HC], KTl[:, c0:c0 + HC], QTl[:, c0:c0 + HC],
                           True, True, ("ktl", "qtl"), (pr,))
                mask = k.CST[:, 704 + ph * HC:704 + (ph + 1) * HC].bitcast(mybir.dt.uint32)
                atg = AT[:, g * 4 * HC:(g + 1) * 4 * HC]
                P.op("dve", lambda e, o=atg: e.memset(o, 0.0), (), ("at",))
                P.op("dve", lambda e, o=atg.rearrange("p (a b) -> p a b", b=HC),
                     d=ps[:, 0:4 * HC].rearrange("p (a b) -> p a b", b=HC),
                     m=mask.unsqueeze(1).broadcast_to([128, 4, HC]): e.copy_predicated(out=o, mask=m, data=d),
                     (pr, "CST", "at"), ("at",))
            DSS = T4[:, 0:1024].rearrange("p (r c v) -> p r c v", r=2, c=4)
            STBA = T3.bitcast(BF16).rearrange("p (c v) -> p c v", v=128)
            seg_groups = {0: [0], 1: [1], 2: [2, 3, 4, 5]}
            sorder = [2, 0, 1] if ph == 0 else [0, 1, 2]
            glist = []
            for si in sorder:
                gs = seg_groups[si] if ph == 0 else seg_groups[si][::-1]
                glist += [(si, g) for g in gs]

            def emit_dS(n, g):
                slot = n % 2
                for ci in range(CPT):
                    ps = k.PS[4 + 2 * slot + ci]
                    pr = f"ps{4 + 2 * slot + ci}"
                    for a_ in range(2):
                        c = g * 4 + a_ * 2 + ci
                        tl = c // CPT
                        mm(P, ps[:, a_ * 128:(a_ + 1) * 128], KTM[ci * HC:(ci + 1) * HC, tl, :],
                           Vb[ci * HC:(ci + 1) * HC, tl, :], True, True, ("ktm", "hv"), (pr,))
                for ci in range(CPT):
                    ps = k.PS[4 + 2 * slot + ci]
                    pr = f"ps{4 + 2 * slot + ci}"
                    tt(P, "dve", DSS[:, slot].rearrange("p (a b) v -> p a b v", b=2)[:, :, ci, :],
                       ps[:, 0:256].rearrange("p (c v) -> p c v", v=128),
                       d2[:, g * 4:(g + 1) * 4].rearrange("p (a b) -> p a b", b=2)[:, :, ci].unsqueeze(2)
                       .broadcast_to([128, 2, 128]), ALU.mult, (pr, "cs"), (f"dss{slot}", "t4"))

            def chain_group(n, g):
                slot = n % 2
                cs_ = list(range(g * 4, g * 4 + 4))
                if ph == 1:
                    cs_ = cs_[::-1]
                for c in cs_:
                    q = c - g * 4
                    ts(P, "dve", STBA[:, c, :], ST, din_[:, c:c + 1], None, ALU.mult, None, ("st", "cs"), ("t3",))
                    stt(P, "dve", ST, ST, d1[:, c:c + 1], DSS[:, slot, q, :], ALU.mult, ALU.add,
                        ("st", "cs", f"dss{slot}", "t4"), ("st",))

            emitted = [0]

            def ensure_dS(upto):
                while emitted[0] <= min(upto, len(glist) - 1):
                    emit_dS(emitted[0], glist[emitted[0]][1])
                    emitted[0] += 1

            ensure_dS(1)
            for n, (si, g) in enumerate(glist):
                first = (n == 0 or glist[n - 1][0] != si)
                last = (n == len(glist) - 1 or glist[n + 1][0] != si)
                if first:
                    if si < 2:
                        P.op("dve", lambda e: e.memset(ST, 0.0), ("st",), ("st",))
                    elif ph == 0:
                        dma(P, "sp", ST, I["s0"][j, h], ("st",), ("st",))
                    else:
                        dma(P, "sp", GST, X["g_st"][h].rearrange("(r p) n -> p r n", r=2), ("g_st", "gst"), ("gst",))
                        ts(P, "dve", ST1, GST[:, 0, :], k.SEL[:, 0:1], None, ALU.mult, None, ("gst", "SEL", "st1"),
                           ("st1",))
                        stt(P, "dve", ST, GST[:, 1, :], k.SEL[:, 1:2], ST1, ALU.mult, ALU.add,
                            ("gst", "st1", "SEL", "st"), ("st",))
                chain_group(n, g)
                ensure_dS(n + 2)
                if last:
                    if si < 2:
                        dma(P, "sp", O["o_st"][j, ph, si, h], ST, ("st",), ("o_st",))
                    elif ph == 0:
                        dma(P, "sp", X["b_st"][h], ST, ("st",), (f"b_st{h}",))
                        P.dma("pool", lambda e, h=h: e.collective_compute(
                            "AllGather", ALU.bypass, replica_groups=RG, ins=[X["b_st"][h].opt()],
                            outs=[X["g_st"][h].opt()]), (f"b_st{h}",), ("g_st",), inc=1)
            for bi, (b0, bl) in enumerate([(0, 256), (256, 256), (512, 512), (1024, 512)]):
                ops_b = k.PS[2 + bi % 2]
                opr_b = f"ps{2 + bi % 2}"
                for c in range(b0 // HC, (b0 + bl) // HC):
                    tl, ci = c // CPT, c % CPT
                    cc = slice(c * HC, (c + 1) * HC)
                    oc = slice(c * HC - b0, c * HC - b0 + HC)
                    mm(P, ops_b[:, oc], Vb[ci * HC:(ci + 1) * HC, tl, :], AT[ci * HC:(ci + 1) * HC, tl * HC:(tl + 1) * HC],
                       True, False, ("hv", "at"), (opr_b,))
                    mm(P, ops_b[:, oc], STBA[:, c, :], QTl[:, cc], False, True, ("t3", "qtl"), (opr_b,))
                if ph == 0:
                    cp(P, "act", OPH[:, b0:b0 + bl], ops_b[:, 0:bl], (opr_b,), ("oph",))
                else:
                    tt(P, "dve", OPH[:, b0:b0 + bl], ops_b[:, 0:bl], OPH[:, b0:b0 + bl], ALU.add,
                       (opr_b, "oph"), ("oph",))
        for tb in range(NBK):
            cols = slice(tb * BS, (tb + 1) * BS)
            act(P, SQb, OAC[:, cols], AF.Square, ("oph",), ("hsq",))
            mm(P, k.PS[0][:, :], k.ONES[:], SQb, True, True, ("ONES", "hsq"), ("ps0",))
            rstd_from_ps(k, T3[:, 0:BS], k.PS[0][:, :], 128, ("ps0",), ("t3",))
            stt(P, "dve", T4[:, 0:BS], OAC[:, cols], vab[:, 10:11], T3[:, 0:BS], ALU.mult, ALU.mult,
                ("oph", "t3", "VAB"), ("t4",))
            tt(P, "dve", k.OT[:, 8 + h, cols], T4[:, 0:BS], Gb[:, cols], ALU.mult, ("t4", "hg"), (f"OT{tb}",))


def c_layer(k, j):
    P, I, O, X = k.P, k.I, k.O, k.X
    P.barrier()
    s = Scr(k, "c")
    TFB = s.F("tfb", 128, 6, BS)
    TF = [TFB[:, i, :] for i in range(6)]
    GW = TFB[:, 0:4, :].rearrange("p a b -> p (a b)").rearrange("p (r n) -> p r n", r=2)
    GWR = ("tf0", "tf1", "tf2", "tf3")
    RS = s.F("rs", 128, BS)
    KT = s.B("kt", 128, 4, NT)
    VT = s.B("vt", 128, 12, 512)
    QT = s.B("qt", 128, 4, NT)
    KC = s.B("kc", 128, 4, 256)
    VCx = s.B("vcx", 128, 2, 512)
    KB_ = s.B("kbnd", 128, 4, 128)
    VB_ = s.B("vbnd", 128, 512)
    PT = [s.B(f"pt{i}", 128, BS) for i in range(2)]
    SQb = [s.B(f"sqb{i}", 128, BS) for i in range(2)]
    w_in = I["w_in_c"][j]
    RC = k.CST[:, 128:256]
    vc = k.VC[:, j, :]
    Mprev, Mnext, Manti = (k.CSTB[:, 256 + i * 128:256 + (i + 1) * 128] for i in range(3))
    scale = 128 ** -0.5

    P.dma("pool", lambda e: e.dma_start(out=KC[:], in_=I["kcT"][j].rearrange("g p n -> p g n")), (), ("KC",))
    P.dma("pool", lambda e: e.dma_start(out=VCx[:], in_=I["vc"][j].rearrange("(t p) n -> p t n", p=128)), (), ("VCx",))

    par_ctr = [0]

    def head_fm(gcol, out_bf, tb, outres, prompt_out=None):
        par = par_ctr[0]
        par_ctr[0] ^= 1
        A0, A1, A2 = TF[3 * par], TF[3 * par + 1], TF[3 * par + 2]
        n0, n1, n2 = f"tf{3 * par}", f"tf{3 * par + 1}", f"tf{3 * par + 2}"
        psp, sqn = f"ps{par}", f"sqb{par}"
        pst, prt = k.PS[4 + 2 * par], k.PS[5 + 2 * par]
        pstn, prtn = f"ps{4 + 2 * par}", f"ps{5 + 2 * par}"
        cp(P, "act", A0, k.PS[par][:, :], (psp,), (n0,))
        act(P, SQb[par], A0, AF.Square, (n0,), (sqn,))
        mm(P, pst[:, :], k.ONES[:], SQb[par], True, True, ("ONES", sqn), (pstn,))
        rstd_from_ps(k, A2, pst[:, :], 128, (pstn,), (n2,))
        stt(P, "dve", A1, A0, vc[:, gcol:gcol + 1], A2, ALU.mult, ALU.mult, (n0, n2, "VC"), (n1,))
        if tb == 0:
            if prompt_out is not None:
                dma(P, "sp", prompt_out, A1, (n1,), ("o_kc",))
            cp(P, "dve", out_bf, A1, (n1,), (outres,))
            return
        tc_ = slice((tb - 1) * BS, tb * BS)
        mm(P, prt[:, :], RC, A1, True, True, ("CST", n1), (prtn,))
        tt(P, "dve", A2, prt[:, :], k.RPC[:, 1, tc_], ALU.mult, (prtn, "RPC"), (n2,))
        tt(P, "dve", A1, A1, k.RPC[:, 0, tc_], ALU.mult, (n1, "RPC"), (n1,))
        tt(P, "dve", out_bf, A1, A2, ALU.add, (n1, n2), (outres,))

    def cur_ps():
        return k.PS[par_ctr[0]], f"ps{par_ctr[0]}"

    for g in range(2):
        wk = load_w(k, w_in[:, 2048 + g * 256:2048 + (g + 1) * 256], 16, 256)
        for hh in range(2):
            kh = g * 2 + hh
            for tb in range(NBK):
                proj_block(k, wk[0], wk[1], hh * 128, 128, tb, *cur_ps())
                head_fm(1, KT[:, kh, tb * BS:(tb + 1) * BS], tb, "KT",
                        prompt_out=O["o_kc"][j, kh] if tb == 0 else None)
    ckpt(f"c_k{j}")
    wvs = [load_w(k, w_in[:, 2560 + g * 256:2560 + (g + 1) * 256], 16, 256) for g in range(2)]
    for tl in range(12):
        ps = k.PS[tl % 2]
        pr = f"ps{tl % 2}"
        for g in range(2):
            for kc in range(16):
                mm(P, ps[:, g * 256:(g + 1) * 256], k.HT[:, kc, tl * 128:(tl + 1) * 128], wvs[g][0][:, kc, :],
                   kc == 0, kc == 15, (wvs[g][1], f"HT{tl // 4}"), (pr,))
        cp(P, "act", VT[:, tl, :], ps[:, :], (pr,), ("VT",))
        if tl < 4:
            cp(P, "dve", TF[4 + tl % 2], ps[:, :], (pr,), (f"tf{4 + tl % 2}",))
            dma(P, "sp", O["o_vc"][j, tl * 128:(tl + 1) * 128, :], TF[4 + tl % 2], (f"tf{4 + tl % 2}",), ("o_vc",))
    ckpt(f"c_v{j}")
    cp(P, "dve", GW[:, 0, 0:512].rearrange("p (a b) -> p a b", b=128), KT[:, :, NT - 128:NT], ("KT",) + GWR, GWR)
    cp(P, "dve", GW[:, 0, 512:1024], VT[:, 11, :], ("VT",) + GWR, GWR)
    dma(P, "sp", X["b_win"], GW[:, 0, :], GWR, ("b_win",))
    P.dma("pool", lambda e: e.collective_compute("AllGather", ALU.bypass, replica_groups=RG,
                                                 ins=[X["b_win"].opt()], outs=[X["g_win"].opt()]),
          ("b_win",), ("g_win",), inc=1)
    dma(P, "sp", GW, X["g_win"].rearrange("(r p) n -> p r n", r=2), ("g_win",) + GWR, GWR)
    ts(P, "dve", GW[:, 0, :], GW[:, 0, :], k.SEL[:, 0:1], None, ALU.mult, None, GWR + ("SEL",), GWR)
    stt(P, "dve", GW[:, 0, :], GW[:, 1, :], k.SEL[:, 1:2], GW[:, 0, :], ALU.mult, ALU.add, GWR + ("SEL",), GWR)
    cp(P, "dve", KB_, GW[:, 0, 0:512].rearrange("p (a b) -> p a b", b=128), GWR, ("KB",))
    cp(P, "dve", VB_, GW[:, 0, 512:1024], GWR, ("VB",))

    ckpt(f"c_kv{j}")
    for g in range(4):
        for pair in range(2):
            wq = load_w(k, w_in[:, g * 512 + pair * 256:g * 512 + (pair + 1) * 256], 16, 256)
            for hh in range(2):
                qh = pair * 2 + hh
                for tb in range(NBK):
                    cols = slice(tb * BS, (tb + 1) * BS)
                    proj_block(k, wq[0], wq[1], hh * 128, 128, tb, *cur_ps())
                    head_fm(0, QT[:, qh, cols], tb, f"QT{tb}")
        ckpt(f"c_q{j}_{g}")
        blocks = []
        for sq in range(2):
            for qb in range(2):
                q0 = sq * 256 + qb * 128
                keys = [(KT[:, g, sq * 256 + t * 128:sq * 256 + (t + 1) * 128], VT[:, sq * 2 + t, g * 128:(g + 1) * 128],
                         None, ("KT", "VT")) for t in range(2)]
                blocks.append((q0, keys))
        for qb in range(8):
            q0 = 512 + qb * 128
            keys = [(KC[:, g, t * 128:(t + 1) * 128], VCx[:, t, g * 128:(g + 1) * 128], None, ("KC", "VCx"))
                    for t in range(2)]
            if qb > 0:
                keys.append((KT[:, g, q0 - 128:q0], VT[:, 4 + qb - 1, g * 128:(g + 1) * 128], Mprev, ("KT", "VT")))
            keys.append((KT[:, g, q0:q0 + 128], VT[:, 4 + qb, g * 128:(g + 1) * 128], None, ("KT", "VT")))
            if qb < 7:
                keys.append((KT[:, g, q0 + 128:q0 + 256], VT[:, 4 + qb + 1, g * 128:(g + 1) * 128], Mnext, ("KT", "VT")))
            else:
                keys.append((KB_[:, g, :], VB_[:, g * 128:(g + 1) * 128], Manti, ("KB", "VB")))
            blocks.append((q0, keys))
        for bi, (q0, keys) in enumerate(blocks):
            tb = q0 // BS
            qs = slice(q0, q0 + 128)
            Ops, Dps = k.PS[4 + 2 * (bi % 2)], k.PS[5 + 2 * (bi % 2)]
            opr, dpr = f"ps{4 + 2 * (bi % 2)}", f"ps{5 + 2 * (bi % 2)}"
            def s_mm(ki):
                kap_, _, _, kres_ = keys[ki]
                mm(P, k.PS[2 + ki % 2][:, :].rearrange("p (a b) -> p a b", b=128), kap_, QT[:, :, qs], True, True,
                   kres_ + (f"QT{tb}",), (f"ps{2 + ki % 2}",))

            s_mm(0)
            for ki, (kap, vap, mask, kres) in enumerate(keys):
                sp_ = k.PS[2 + ki % 2]
                spr = f"ps{2 + ki % 2}"
                if ki + 1 < len(keys):
                    s_mm(ki + 1)
                act(P, PT[ki % 2], sp_[:, :], AF.Exp, (spr,), (f"pt{ki % 2}",), scale=scale)
                if mask is not None:
                    tt(P, "dve", PT[ki % 2].rearrange("p (a b) -> p a b", b=128),
                       PT[ki % 2].rearrange("p (a b) -> p a b", b=128),
                       mask.unsqueeze(1).broadcast_to([128, 4, 128]), ALU.mult, (f"pt{ki % 2}", "CSTB"),
                       (f"pt{ki % 2}",))
                mm(P, Ops[:, :], vap, PT[ki % 2], ki == 0, ki == len(keys) - 1, kres + (f"pt{ki % 2}",), (opr,))
                mm(P, Dps[:, :], k.ONES[:], PT[ki % 2], ki == 0, ki == len(keys) - 1, ("ONES", f"pt{ki % 2}"), (dpr,))
            tt(P, "dve", TF[4].rearrange("p (a b) -> p a b", b=128), Dps[:, :].rearrange("p (a b) -> p a b", b=128),
               k.ESINK[:, j, g * 4:(g + 1) * 4].unsqueeze(2).broadcast_to([128, 4, 128]), ALU.add,
               (dpr, "ESINK"), ("tf4",))
            act(P, TF[4], TF[4], AF.Ln, ("tf4",), ("tf4",))
            act(P, TF[4], TF[4], AF.Exp, ("tf4",), ("tf4",), scale=-1.0)
            tt(P, "dve", k.OT[:, g * 4:(g + 1) * 4, qs], Ops[:, :].rearrange("p (a b) -> p a b", b=128),
               TF[4].rearrange("p (a b) -> p a b", b=128), ALU.mult, (opr, "tf4"), (f"OT{tb}",))
        ckpt(f"c_att{j}_{g}")
        for pair in range(2):
            wg = load_w(k, w_in[:, 3072 + g * 512 + pair * 256:3072 + g * 512 + (pair + 1) * 256], 16, 256)
            for hh in range(2):
                qh = pair * 2 + hh
                for tb in range(NBK):
                    cols = slice(tb * BS, (tb + 1) * BS)
                    gp_ = tb % 2
                    proj_block(k, wg[0], wg[1], hh * 128, 128, tb, k.PS[gp_], f"ps{gp_}")
                    act(P, TF[4 + gp_], k.PS[gp_][:, :], AF.Silu, (f"ps{gp_}",), (f"tf{4 + gp_}",))
                    tt(P, "dve", k.OT[:, g * 4 + qh, cols], k.OT[:, g * 4 + qh, cols], TF[4 + gp_], ALU.mult,
                       (f"tf{4 + gp_}", f"OT{tb}"), (f"OT{tb}",))


_NC_CACHE = {}


def _f32(a):
    return np.ascontiguousarray(np.asarray(a, dtype=np.float32))


def _rope_tables(rot_dim, pos):
    n_freq = rot_dim // 4
    inv = (10000.0 ** (-np.arange(n_freq, dtype=np.float32) / n_freq)).astype(np.float32)
    row = np.floor(pos / 64).astype(np.float32)
    col = (pos % 64).astype(np.float32)
    ang = np.concatenate([row[:, None] * inv, col[:, None] * inv], axis=-1).astype(np.float32)
    cos, sin = np.cos(ang).astype(np.float32), np.sin(ang).astype(np.float32)
    idm = pos < 0
    cos[idm] = 1.0
    sin[idm] = 0.0
    c = np.concatenate([cos, cos], axis=-1).T
    s_ = np.concatenate([-sin, sin], axis=-1).T
    return np.stack([c, s_], axis=0).astype(np.float32)


def kernel(x_prompt, x_sample, cache_ckv, cache_kpe, state_hgrn_fwd, state_hgrn_bwd, cache_k_c, cache_v_c,
           c, c_ctx, mod_w_ab, mod_b_ab, norm_ab, w_in_ab, q_lora_norm, kv_lora_norm, w_q_up, w_kv_up,
           q_norm_ab, k_norm_ab, hgrn_lb_logits, hgrn_out_norm, w_out_ab, mod_w_c, mod_b_c, norm_c, w_in_c,
           q_norm_c, k_norm_c, sink_c, w_out_c):
    A = {n: _f32(v) for n, v in locals().items()}
    if "nc" not in _NC_CACHE:
        _NC_CACHE["nc"] = build_nc()
    nc = _NC_CACHE["nc"]

    cst = np.zeros((128, 832), np.float32)
    cst[:, 0:128] = np.eye(128)
    m = np.arange(128)
    cst[(m + 64) % 128, 128 + m] = 1.0
    jj, ii = np.meshgrid(np.arange(128), np.arange(128), indexing="ij")
    cst[:, 256:384] = (jj >= ii)
    cst[:, 384:512] = (jj <= ii)
    cst[:, 512:640] = (ii + jj >= 127)
    m64 = np.arange(64)
    cst[(m64 + 32) % 64, 640 + m64] = 1.0
    s32, t32 = np.meshgrid(np.arange(128) % HC, np.arange(HC), indexing="ij")
    cst[:, 704:704 + HC] = (s32 <= t32)
    cst[:, 704 + HC:704 + 2 * HC] = (s32 >= t32)

    def fm(v, n):
        return np.ascontiguousarray(v.reshape(n, 128).T)

    modb = np.stack([fm(A["mod_b_ab"][0], 48), fm(A["mod_b_c"][0], 48), fm(A["mod_b_ab"][1], 48),
                     fm(A["mod_b_c"][1], 48)])
    normT = np.stack([fm(A["norm_ab"][0], 16), fm(A["norm_c"][0], 16), fm(A["norm_ab"][1], 16),
                      fm(A["norm_c"][1], 16)])
    vecab = np.zeros((2, 128, 16), np.float32)
    for j in range(2):
        vecab[j, :, 0:4] = fm(A["q_lora_norm"][j], 4)
        vecab[j, :, 4:6] = fm(A["kv_lora_norm"][j], 2)
        vecab[j, :, 6] = A["q_norm_ab"][j][0:128]
        vecab[j, 0:64, 7] = A["q_norm_ab"][j][128:192]
        vecab[j, :, 8] = A["k_norm_ab"][j][0:128]
        vecab[j, 0:64, 9] = A["k_norm_ab"][j][128:192]
        vecab[j, :, 10] = A["hgrn_out_norm"][j]
    vecc = np.zeros((2, 128, 18), np.float32)
    for j in range(2):
        vecc[j, :, 0] = A["q_norm_c"][j]
        vecc[j, :, 1] = A["k_norm_c"][j]
        vecc[j, :, 2:18] = A["sink_c"][j][None, :]
    w_in_ab_sw = A["w_in_ab"].copy()
    w_in_ab_sw[:, :, O_F1:O_F1 + 1024] = A["w_in_ab"][:, :, O_F2:O_F2 + 1024]
    w_in_ab_sw[:, :, O_F2:O_F2 + 1024] = A["w_in_ab"][:, :, O_F1:O_F1 + 1024]

    in_maps = []
    for core in range(8):
        odd = core % 2
        b = core // 2
        xp = A["x_prompt"][2 * core:2 * core + 2]
        xs = A["x_sample"][b, 0:1024] if not odd else A["x_sample"][b, 1024:2048]
        pos = np.arange(1024, dtype=np.float32) if not odd else np.arange(1024, 2048, dtype=np.float32)
        if odd:
            xp = xp[:, ::-1]
            xs = xs[::-1]
            pos = pos[::-1]
        xt = np.concatenate([xp[0], xp[1], xs], axis=0)
        posall = np.concatenate([-np.ones(512, np.float32), pos])
        lbl = A["hgrn_lb_logits"]
        if odd:
            lbl = lbl[:, ::-1]
        lbl_fm = np.ascontiguousarray(lbl.reshape(2, 2, 8, 128).transpose(3, 0, 1, 2))
        condT = np.stack([fm(A["c_ctx"], 16), fm(A["c"][b], 16)], axis=-1)
        s0 = A["state_hgrn_bwd"][b] if odd else A["state_hgrn_fwd"][b]
        sel = np.zeros((128, 2), np.float32)
        sel[:, 1 - odd] = 1.0
        in_maps.append({
            "xT": np.ascontiguousarray(xt.T),
            "condT": np.ascontiguousarray(condT),
            "mod_w_ab": A["mod_w_ab"], "mod_w_c": A["mod_w_c"], "modb": modb, "normT": normT,
            "w_in_ab": w_in_ab_sw if odd else A["w_in_ab"],
            "w_q_up": A["w_q_up"], "w_kv_up": A["w_kv_up"], "w_out_ab": A["w_out_ab"],
            "w_in_c": A["w_in_c"], "w_out_c": A["w_out_c"],
            "vecab": vecab, "lbl": lbl_fm, "vecc": vecc,
            "ropeA": _rope_tables(64, pos), "ropeC": _rope_tables(128, pos),
            "cst": cst, "sel": sel,
            "ckvT": np.ascontiguousarray(A["cache_ckv"][b].transpose(0, 2, 1)),
            "kpeT": np.ascontiguousarray(A["cache_kpe"][b].transpose(0, 2, 1)),
            "s0": np.ascontiguousarray(s0),
            "kcT": np.ascontiguousarray(A["cache_k_c"][b].transpose(0, 2, 3, 1)),
            "vc": np.ascontiguousarray(A["cache_v_c"][b].reshape(2, 256, 512)),
        })
    res = run_bass_kernel_spmd(nc, in_maps, core_ids=list(range(8)))
    R = res.results

    y_prompt = np.zeros((16, 256, D), np.float32)
    y_sample = np.zeros((4, 2048, D), np.float32)
    new_ckv = np.zeros((16, 2, 256, 256), np.float32)
    new_kpe = np.zeros((16, 2, 256, 64), np.float32)
    new_sf = np.zeros((16, 2, 8, 128, 128), np.float32)
    new_sb = np.zeros((16, 2, 8, 128, 128), np.float32)
    new_kc = np.zeros((16, 2, 256, 4, 128), np.float32)
    new_vc = np.zeros((16, 2, 256, 4, 128), np.float32)
    for core in range(8):
        odd = core % 2
        b = core // 2
        r = R[core]
        y = r["yT"].T
        ckv = r["o_ckv"].transpose(0, 2, 1)
        kpe = r["o_kpe"].transpose(0, 2, 1)
        kc = r["o_kc"].transpose(0, 3, 1, 2)
        vcx = r["o_vc"].reshape(2, 512, 4, 128)
        st = r["o_st"]
        for sq in range(2):
            sl = slice(sq * 256, (sq + 1) * 256)
            bi = 2 * core + sq
            f = (lambda a: a[::-1]) if odd else (lambda a: a)
            y_prompt[bi] = f(y[sl])
            for j in range(2):
                new_ckv[bi, j] = f(ckv[j, sl])
                new_kpe[bi, j] = f(kpe[j, sl])
                new_kc[bi, j] = f(kc[j, sl])
                new_vc[bi, j] = f(vcx[j, sl])
                new_sf[bi, j] = st[j, 1 if odd else 0, sq]
                new_sb[bi, j] = st[j, 0 if odd else 1, sq]
        ys = y[512:1536]
        if odd:
            y_sample[b, 1024:2048] = ys[::-1]
        else:
            y_sample[b, 0:1024] = ys
    return (y_prompt, y_sample, new_ckv, new_kpe, new_sf, new_sb, new_kc, new_vc)
```

```python
import numpy as np
from contextlib import ExitStack
import concourse.bass as bass
import concourse.mybir as mybir
from concourse.bass_utils import run_bass_kernel_spmd

F32 = mybir.dt.float32
BF16 = mybir.dt.bfloat16
AF = mybir.ActivationFunctionType
ALU = mybir.AluOpType

D = 2048
NT = 1536
BS = 512
NBK = 3
NKEY = 2816
EPS = 1e-6
HC = 64
NCH = NT // HC
RG = [[0, 1], [2, 3], [4, 5], [6, 7]]
O_QL, O_KV, O_KPE, O_AG, O_BQ, O_F1, O_F2, O_BI, O_BG = 0, 512, 768, 832, 1856, 2880, 3904, 4928, 5952

EPOCH = 30000
N_DMA_SEMS = 20
SCRN = 15360


class Prog:
    COMPUTE = ("pe", "act", "dve", "pool")

    def __init__(self, nc):
        self.nc = nc
        self.streams = {e: [] for e in ("pe", "act", "dve", "pool", "sp")}
        self.count = {e: 0 for e in self.COMPUTE}
        self.known = {e: {} for e in self.streams}
        self.res = {}
        self.sem_names = set()
        self.dma_rr = {"sp": 0, "pool": 0}
        self.dma_val = {}
        self.last = {}

    def _need(self, eng, tok):
        key, val = tok
        if self.known[eng].get(key, 0) >= val:
            return
        self.known[eng][key] = val
        self.sem_names.add(key)
        self.streams[eng].append(("wait", key, val))

    def _wait_tok(self, eng, tok):
        if tok[0].startswith("pe_") and eng == "pe":
            return
        self._need(eng, tok)

    def _deps(self, eng, reads, writes):
        for r in reads:
            st = self.res.get(r)
            if st and st["w"] is not None:
                self._wait_tok(eng, st["w"])
        for w in writes:
            st = self.res.get(w)
            if st:
                if st["w"] is not None:
                    self._wait_tok(eng, st["w"])
                for t in st["r"].items():
                    self._wait_tok(eng, t)

    def _commit(self, tok, reads, writes):
        for r in reads:
            st = self.res.setdefault(r, {"w": None, "r": {}})
            st["r"][tok[0]] = max(st["r"].get(tok[0], 0), tok[1])
        for w in writes:
            self.res[w] = {"w": tok, "r": {}}
        self.last[tok[0]] = tok[1]

    def op(self, eng, fn, reads=(), writes=()):
        writes = tuple(writes) + tuple(r for r in reads if r.startswith("ps") and r not in writes)
        self._deps(eng, reads, writes)
        n = self.count[eng]
        self.count[eng] = n + 1
        key = f"{eng}_{n // EPOCH}"
        tok = (key, n % EPOCH + 1)
        self.sem_names.add(key)
        self.streams[eng].append(("op", fn, key, 1))
        self._commit(tok, reads, writes)
        return tok

    def dma(self, queue, fn, reads=(), writes=(), inc=16):
        self._deps(queue, reads, writes)
        i = self.dma_rr[queue]
        self.dma_rr[queue] = (i + 1) % N_DMA_SEMS
        key = f"d{queue}_{i}"
        prev = self.dma_val.get(key, 0)
        if prev:
            self._need(queue, (key, prev))
        val = prev + inc
        self.dma_val[key] = val
        self.sem_names.add(key)
        self.streams[queue].append(("op", fn, key, inc))
        tok = (key, val)
        self._commit(tok, reads, writes)
        return tok

    def barrier(self):
        toks = list(self.last.items())
        for eng in self.streams:
            for t in toks:
                self._need(eng, t)

    def emit(self, block, sems):
        def run(engine_obj, stream):
            for item in stream:
                if item[0] == "wait":
                    engine_obj.wait_ge(sems[item[1]], item[2])
                else:
                    _, fn, key, inc = item
                    fn(engine_obj).then_inc(sems[key], inc)

        @block.tensor
        def _(e):
            run(e, self.streams["pe"])

        @block.scalar
        def _(e):
            run(e, self.streams["act"])

        @block.vector
        def _(e):
            run(e, self.streams["dve"])

        @block.gpsimd
        def _(e):
            run(e, self.streams["pool"])

        @block.sync
        def _(e):
            run(e, self.streams["sp"])


def mm(P, out, lhsT, rhs, start, stop, reads, writes):
    return P.op("pe", lambda e, o=out, l=lhsT, r=rhs, s=start, t=stop:
                e.matmul(o, lhsT=l, rhs=r, start=s, stop=t), reads, writes)


def tr(P, out, in_, ident, reads, writes):
    return P.op("pe", lambda e, o=out, i=in_, d=ident: e.transpose(o, i, d), reads, writes)


def act(P, out, in_, func, reads, writes, bias=None, scale=None):
    kw = {}
    if bias is not None:
        kw["bias"] = bias
    if scale is not None:
        kw["scale"] = scale
    return P.op("act", lambda e, o=out, i=in_, f=func, k=kw: e.activation(out=o, in_=i, func=f, **k), reads, writes)


def tt(P, eng, out, in0, in1, op, reads, writes):
    return P.op(eng, lambda e, o=out, a=in0, b=in1, p=op: e.tensor_tensor(out=o, in0=a, in1=b, op=p), reads, writes)


def ts(P, eng, out, in0, s1, s2, op0, op1, reads, writes):
    if s2 is None:
        return P.op(eng, lambda e, o=out, a=in0, x=s1, p=op0:
                    e.tensor_single_scalar(out=o, in_=a, scalar=x, op=p), reads, writes)
    return P.op(eng, lambda e, o=out, a=in0, x=s1, y=s2, p=op0, q=op1:
                e.tensor_scalar(out=o, in0=a, scalar1=x, scalar2=y, op0=p, op1=q), reads, writes)


def stt(P, eng, out, in0, scalar, in1, op0, op1, reads, writes):
    return P.op(eng, lambda e, o=out, a=in0, s=scalar, b=in1, p=op0, q=op1:
                e.scalar_tensor_tensor(out=o, in0=a, scalar=s, in1=b, op0=p, op1=q), reads, writes)


def cp(P, eng, out, in_, reads, writes):
    if eng == "act":
        return P.op("act", lambda e, o=out, i=in_: e.copy(out=o, in_=i), reads, writes)
    return P.op(eng, lambda e, o=out, i=in_: e.tensor_copy(out=o, in_=i), reads, writes)


def dma(P, q, out, in_, reads, writes):
    return P.dma(q, lambda e, o=out, i=in_: e.dma_start(out=o, in_=i), reads, writes)


class K:
    pass


class _Stop(Exception):
    pass


import os as _os
_KSTOP = [_os.environ.get("KSTOP")]


def ckpt(name):
    if _KSTOP[0] == name:
        raise _Stop()


def build_nc():
    nc = bass.Bass("TRN2", target_bir_lowering=False)
    k = K()
    k.nc = nc

    def din(name, shape):
        return nc.dram_tensor(name, list(shape), F32, kind="ExternalInput").ap()

    def dout(name, shape):
        return nc.dram_tensor(name, list(shape), F32, kind="ExternalOutput").ap()

    def dint(name, shape, dt=F32):
        return nc.dram_tensor(name, list(shape), dt, kind="Internal").ap()

    I = {}
    I["xT"] = din("xT", [D, NT])
    I["condT"] = din("condT", [128, 16, 2])
    I["mod_w_ab"] = din("mod_w_ab", [2, D, 3 * D])
    I["mod_w_c"] = din("mod_w_c", [2, D, 3 * D])
    I["modb"] = din("modb", [4, 128, 48])
    I["normT"] = din("normT", [4, 128, 16])
    I["w_in_ab"] = din("w_in_ab", [2, D, 6976])
    I["w_q_up"] = din("w_q_up", [2, 512, 1536])
    I["w_kv_up"] = din("w_kv_up", [2, 256, 2048])
    I["w_out_ab"] = din("w_out_ab", [2, D, D])
    I["w_in_c"] = din("w_in_c", [2, D, 5120])
    I["w_out_c"] = din("w_out_c", [2, D, D])
    I["vecab"] = din("vecab", [2, 128, 16])
    I["lbl"] = din("lbl", [128, 2, 2, 8])
    I["vecc"] = din("vecc", [2, 128, 18])
    I["ropeA"] = din("ropeA", [2, 64, 1024])
    I["ropeC"] = din("ropeC", [2, 128, 1024])
    I["cst"] = din("cst", [128, 832])
    I["sel"] = din("sel", [128, 2])
    I["ckvT"] = din("ckvT", [2, 256, 256])
    I["kpeT"] = din("kpeT", [2, 64, 256])
    I["s0"] = din("s0", [2, 8, 128, 128])
    I["kcT"] = din("kcT", [2, 4, 128, 256])
    I["vc"] = din("vc", [2, 256, 512])
    O = {}
    O["yT"] = dout("yT", [D, NT])
    O["o_ckv"] = dout("o_ckv", [2, 256, 512])
    O["o_kpe"] = dout("o_kpe", [2, 64, 512])
    O["o_st"] = dout("o_st", [2, 2, 2, 8, 128, 128])
    O["o_kc"] = dout("o_kc", [2, 4, 128, 512])
    O["o_vc"] = dout("o_vc", [2, 512, 512])
    X = {}
    X["xs"] = [dint("xs0", [D, NT]), dint("xs1", [D, NT])]
    X["b_lat"] = dint("b_lat", [320, 1024])
    X["g_lat"] = dint("g_lat", [640, 1024])
    X["b_st"] = [dint(f"b_st{h}", [128, 128]) for h in range(8)]
    X["g_st"] = [dint(f"g_st{h}", [256, 128]) for h in range(8)]
    X["b_win"] = dint("b_win", [128, 1024])
    X["g_win"] = dint("g_win", [256, 1024])
    k.I, k.O, k.X = I, O, X

    with ExitStack() as es:
        def sb(name, shape, dt):
            return es.enter_context(nc.sbuf_tensor(name, list(shape), dt))

        k.HT = sb("HT", [128, 16, NT], BF16)
        k.OT = sb("OT", [128, 16, NT], BF16)
        k.W = [sb(f"W{i}", [128, 16, 256], BF16) for i in range(2)]
        k.SCR = sb("SCR", [128, SCRN], F32)
        k.MW = sb("MW", [128, 16, 128], BF16)
        k.CST = sb("CST", [128, 832], F32)
        k.CSTB = sb("CSTB", [128, 832], BF16)
        k.ONES = sb("ONES", [128, 128], BF16)
        k.ONEF = sb("ONEF", [128, NT], BF16)
        k.SEL = sb("SEL", [128, 2], F32)
        k.MOD = sb("MOD", [128, 4, 48, 2], F32)
        k.AMOD = sb("AMOD", [128, 4, 16, 2], F32)
        k.MODB = sb("MODB", [128, 4, 48], F32)
        k.NRM = sb("NRM", [128, 4, 16], F32)
        k.SC = sb("SC", [128, 16, 2], BF16)
        k.CONDF = sb("CONDF", [128, 16, 2], F32)
        k.VAB = sb("VAB", [128, 2, 16], F32)
        k.LBL = sb("LBL", [128, 2, 2, 8], F32)
        k.LB = sb("LB", [128, 2, 2, 8], F32)
        k.OML = sb("OML", [128, 2, 2, 8], F32)
        k.VC = sb("VC", [128, 2, 18], F32)
        k.ESINK = sb("ESINK", [128, 2, 16], F32)
        k.RPA = sb("RPA", [64, 2, 1024], F32)
        k.RPC = sb("RPC", [128, 2, 1024], F32)
        k.PS = [es.enter_context(nc.psum_tensor(f"ps{i}", [128, 512], F32)) for i in range(8)]
        k.PSB = k.PS[7]
        P = Prog(nc)
        k.P = P
        k.wi = 0
        k.modq = []
        k.mod_loaded = None
        k.mod_rate = 1
        try:
            program(k)
        except _Stop:
            P.barrier()
        sems = {s: es.enter_context(nc.semaphore(s)) for s in sorted(P.sem_names)}
        with nc.Block() as block:
            P.emit(block, sems)
    return nc


class Scr:
    def __init__(self, k, tag):
        self.k, self.tag, self.f = k, tag, 0

    def F(self, name, rows, *shape):
        n = int(np.prod(shape))
        ap = self.k.SCR[0:rows, self.f:self.f + n]
        self.f += n
        assert self.f <= SCRN, (self.tag, name, self.f)
        if len(shape) == 2:
            ap = ap.rearrange("p (a b) -> p a b", b=shape[1])
        elif len(shape) == 3:
            ap = ap.rearrange("p (a b c) -> p a b c", b=shape[1], c=shape[2])
        return ap

    def B(self, name, rows, *shape):
        n = int(np.prod(shape))
        nf = (n + 1) // 2
        ap = self.k.SCR[0:rows, self.f:self.f + nf].bitcast(BF16)[:, 0:n]
        self.f += nf
        assert self.f <= SCRN, (self.tag, name, self.f)
        if len(shape) == 2:
            ap = ap.rearrange("p (a b) -> p a b", b=shape[1])
        elif len(shape) == 3:
            ap = ap.rearrange("p (a b c) -> p a b c", b=shape[1], c=shape[2])
        return ap


def mod_consume(k):
    if k.mod_loaded is None:
        return
    P = k.P
    layer, n = k.mod_loaded
    for kc in range(16):
        mm(P, k.PS[7][:, 0:2], k.MW[:, kc, :], k.SC[:, kc, :], kc == 0, kc == 15, ("MW", "SC"), ("ps7",))
    ts(P, "dve", k.MOD[:, layer, n, :], k.PS[7][:, 0:2], k.MODB[:, layer, n:n + 1], None, ALU.add, None,
       ("ps7", "MODB"), (f"MOD{layer}",))
    k.mod_loaded = None


def mod_issue_load(k):
    if not k.modq:
        return
    P, I = k.P, k.I
    layer, n = k.modq.pop(0)
    wsrc = (I["mod_w_ab"] if layer % 2 == 0 else I["mod_w_c"])[layer // 2][:, n * 128:(n + 1) * 128]
    P.dma("pool", lambda e, s_=wsrc.rearrange("(c p) n -> p c n", p=128): e.dma_start(out=k.MW[:], in_=s_),
          (), ("MW",))
    k.mod_loaded = (layer, n)


def mod_step(k, times=1):
    for _ in range(times):
        mod_consume(k)
        mod_issue_load(k)


def mod_flush(k, layer):
    P = k.P
    while k.mod_loaded is not None or k.modq:
        mod_consume(k)
        mod_issue_load(k)
    ts(P, "dve", k.AMOD[:, layer], k.MOD[:, layer, 16:32, :], 1.0, None, ALU.add, None,
       (f"MOD{layer}",), (f"AMOD{layer}",))
    tt(P, "dve", k.AMOD[:, layer], k.AMOD[:, layer],
       k.NRM[:, layer, :].unsqueeze(2).broadcast_to([128, 16, 2]), ALU.mult, (f"AMOD{layer}", "NRM"),
       (f"AMOD{layer}",))


def load_w(k, src, nk, ncols):
    P = k.P
    i = k.wi
    k.wi = (i + 1) % 2
    buf = k.W[i][:, 0:nk, 0:ncols]
    rn = f"W{i}"
    P.dma("pool", lambda e, o=buf, s=src.rearrange("(c p) n -> p c n", p=128): e.dma_start(out=o, in_=s),
          reads=(), writes=(rn,))
    mod_step(k, k.mod_rate)
    return buf, rn


def rstd_from_ps(k, out, ps, n, reads, writes):
    P = k.P
    act(P, out, ps, AF.Ln, reads, writes, bias=EPS, scale=1.0 / n)
    act(P, out, out, AF.Exp, writes, writes, scale=-0.5)


def program(k):
    P, I, O, X = k.P, k.I, k.O, k.X
    dma(P, "sp", k.CST[:], I["cst"], (), ("CST",))
    P.dma("pool", lambda e: e.dma_start(out=k.CSTB[:], in_=I["cst"]), (), ("CSTB",))
    P.op("pool", lambda e: e.memset(k.ONES[:], 1.0), (), ("ONES",))
    P.op("pool", lambda e: e.memset(k.ONEF[:], 1.0), (), ("ONEF",))
    dma(P, "sp", k.SEL[:], I["sel"], (), ("SEL",))
    dma(P, "sp", k.CONDF[:], I["condT"], (), ("CONDF",))
    dma(P, "sp", k.MODB[:], I["modb"].rearrange("l p n -> p l n"), (), ("MODB",))
    dma(P, "sp", k.NRM[:], I["normT"].rearrange("l p n -> p l n"), (), ("NRM",))
    dma(P, "sp", k.VAB[:], I["vecab"].rearrange("l p n -> p l n"), (), ("VAB",))
    dma(P, "sp", k.LBL[:], I["lbl"], (), ("LBL",))
    dma(P, "sp", k.VC[:], I["vecc"].rearrange("l p n -> p l n"), (), ("VC",))
    dma(P, "sp", k.RPA[:], I["ropeA"].rearrange("l p n -> p l n"), (), ("RPA",))
    dma(P, "sp", k.RPC[:], I["ropeC"].rearrange("l p n -> p l n"), (), ("RPC",))
    act(P, k.SC[:], k.CONDF[:], AF.Silu, ("CONDF",), ("SC",))
    act(P, k.ESINK[:], k.VC[:, :, 2:18], AF.Exp, ("VC",), ("ESINK",))
    P.op("pool", lambda e: e.memset(k.LB[:, 0], 0.0), (), ("LB",))
    tt(P, "dve", k.LB[:, 1], k.LBL[:, 1], k.LBL[:, 0], ALU.subtract, ("LBL", "LB"), ("LB",))
    act(P, k.LB[:, 1], k.LB[:, 1], AF.Sigmoid, ("LB",), ("LB",))
    ts(P, "dve", k.OML[:], k.LB[:], -1.0, 1.0, ALU.mult, ALU.add, ("LB",), ("OML",))

    ckpt("const")
    modulation(k, 0)
    ckpt("mod")
    for layer in range(4):
        j = layer // 2
        if layer < 3:
            k.modq = [(layer + 1, n) for n in range(48)]
            k.mod_rate = 1 if layer % 2 == 0 else 2
        xin = I["xT"] if layer == 0 else X["xs"][(layer - 1) % 2]
        xout = O["yT"] if layer == 3 else X["xs"][layer % 2]
        norm_mod(k, layer, xin)
        ckpt(f"nm{layer}")
        if layer % 2 == 0:
            ab_layer(k, j)
            wo = I["w_out_ab"][j]
        else:
            c_layer(k, j)
            wo = I["w_out_c"][j]
        ckpt(f"mix{layer}")
        out_proj(k, layer, wo, xin, xout)
        if layer < 3:
            mod_flush(k, layer + 1)
        ckpt(f"out{layer}")
    P.barrier()


def modulation(k, layer):
    P, I = k.P, k.I
    wsrc = (I["mod_w_ab"] if layer % 2 == 0 else I["mod_w_c"])[layer // 2]
    for g in range(24):
        wb, rn = load_w(k, wsrc[:, g * 256:(g + 1) * 256], 16, 256)
        for h in range(2):
            n = g * 2 + h
            ps = k.PS[n % 2]
            pr = f"ps{n % 2}"
            for kc in range(16):
                mm(P, ps[:, 0:2], wb[:, kc, h * 128:(h + 1) * 128], k.SC[:, kc, :], kc == 0, kc == 15,
                   (rn, "SC"), (pr,))
            ts(P, "dve", k.MOD[:, layer, n, :], ps[:, 0:2], k.MODB[:, layer, n:n + 1], None, ALU.add, None,
               (pr, "MODB"), (f"MOD{layer}",))
    ts(P, "dve", k.AMOD[:, layer], k.MOD[:, layer, 16:32, :], 1.0, None, ALU.add, None,
       (f"MOD{layer}",), (f"AMOD{layer}",))
    tt(P, "dve", k.AMOD[:, layer], k.AMOD[:, layer],
       k.NRM[:, layer, :].unsqueeze(2).broadcast_to([128, 16, 2]), ALU.mult, (f"AMOD{layer}", "NRM"),
       (f"AMOD{layer}",))


def norm_mod(k, layer, xin):
    P = k.P
    P.barrier()
    s = Scr(k, "nm")
    XC = [s.F(f"xc{i}", 128, BS) for i in range(4)]
    SQ = [s.B(f"sq{i}", 128, BS) for i in range(2)]
    RS = s.F("rs", 128, BS)
    T = [s.F(f"t{i}", 128, BS) for i in range(2)]
    n = 0
    for tb in range(NBK):
        c = 0 if tb == 0 else 1
        cols = slice(tb * BS, (tb + 1) * BS)
        ps = k.PS[2 + tb % 2]
        pr = f"ps{2 + tb % 2}"
        for fc in range(16):
            xi = n % 4
            n += 1
            dma(P, "sp", XC[xi], xin[fc * 128:(fc + 1) * 128, cols], ("xin",), (f"nm_xc{xi}",))
            act(P, SQ[fc % 2], XC[xi], AF.Square, (f"nm_xc{xi}",), (f"nm_sq{fc % 2}",))
            mm(P, ps[:, :], k.ONES[:], SQ[fc % 2], fc == 0, fc == 15, ("ONES", f"nm_sq{fc % 2}"), (pr,))
        rstd_from_ps(k, RS, ps[:, :], D, (pr,), ("nm_rs",))
        for fc in range(16):
            xi = n % 4
            n += 1
            dma(P, "sp", XC[xi], xin[fc * 128:(fc + 1) * 128, cols], ("xin",), (f"nm_xc{xi}",))
            stt(P, "dve", T[fc % 2], XC[xi], k.AMOD[:, layer, fc, c:c + 1], RS, ALU.mult, ALU.mult,
                (f"nm_xc{xi}", "nm_rs", f"AMOD{layer}"), (f"nm_t{fc % 2}",))
            act(P, k.HT[:, fc, cols], T[fc % 2], AF.Identity, (f"nm_t{fc % 2}", f"MOD{layer}"), (f"HT{tb}",),
                bias=k.MOD[:, layer, fc, c:c + 1])


def out_proj(k, layer, wo, xin, xout):
    P = k.P
    P.barrier()
    s = Scr(k, "op")
    XC = [s.F(f"xc{i}", 128, BS) for i in range(4)]
    n = 0
    for g in range(8):
        wb, rn = load_w(k, wo[:, g * 256:(g + 1) * 256], 16, 256)
        for h in range(2):
            oc = g * 2 + h
            for tb in range(NBK):
                c = 0 if tb == 0 else 1
                cols = slice(tb * BS, (tb + 1) * BS)
                ps = k.PS[n % 4]
                pr = f"ps{n % 4}"
                xi = n % 4
                n += 1
                dma(P, "sp", XC[xi], xin[oc * 128:(oc + 1) * 128, cols], ("xin",), (f"op_xc{xi}",))
                for kc in range(16):
                    mm(P, ps[:, :], wb[:, kc, h * 128:(h + 1) * 128], k.OT[:, kc, cols], kc == 0, kc == 15,
                       (rn, f"OT{tb}"), (pr,))
                stt(P, "dve", XC[xi], ps[:, :], k.MOD[:, layer, 32 + oc, c:c + 1], XC[xi], ALU.mult, ALU.add,
                    (pr, f"op_xc{xi}", f"MOD{layer}"), (f"op_xc{xi}",))
                dma(P, "sp", xout[oc * 128:(oc + 1) * 128, cols], XC[xi], (f"op_xc{xi}",), ("xout",))
    P.res["xin"] = {"w": None, "r": {}}
    P.barrier()


def proj_block(k, wb, rn, c0, ncol, tb, ps, pr, nk=16, rhs=None, rres=None):
    P = k.P
    cols = slice(tb * BS, (tb + 1) * BS)
    for kc in range(nk):
        r = k.HT[:, kc, cols] if rhs is None else rhs[:, kc, cols]
        mm(P, ps[0:ncol, :], wb[:, kc, c0:c0 + ncol], r, kc == 0, kc == nk - 1,
           (rn, rres or f"HT{tb}"), (pr,))


def ab_layer(k, j):
    P, I, O, X = k.P, k.I, k.O, k.X
    P.barrier()
    s = Scr(k, "mla")
    QLN = k.OT[:, 8:12, :]
    CKV = k.OT[:, 12:16, :].rearrange("p a b -> p (a b)")[:, 0:2 * NKEY].rearrange("p (c n) -> p c n", c=2)
    KPEG = s.B("kpeg", 64, NKEY)
    KPSQ = s.B("kpsq", 64, NKEY)
    KTN = s.B("ktn", 128, NKEY)
    KTP = s.B("ktp", 64, NKEY)
    VH = s.B("vh", 128, 22, 128)
    QTN = s.B("qtn", 128, NT)
    QTP = s.B("qtp", 64, NT)
    PT = [s.B(f"pt{i}", 128, BS) for i in range(2)]
    SQb = [s.B(f"sqb{i}", 128, BS) for i in range(2)]
    TF = [s.F(f"tf{i}", 128, BS) for i in range(6)]
    RS = s.F("rs", 128, BS)
    w_in = I["w_in_ab"][j]
    RA = k.CST[0:64, 640:704]
    vab = k.VAB[:, j, :]

    def rope64(dst, x, tb, xres, dres):
        tc_ = slice((tb - 1) * BS, tb * BS)
        mm(P, k.PS[5][0:64, :], RA, x, True, True, ("CST", xres), ("ps5",))
        tt(P, "dve", TF[3][0:64, :], k.PS[5][0:64, :], k.RPA[:, 1, tc_], ALU.mult, ("ps5", "RPA"), ("tf3",))
        tt(P, "dve", x, x, k.RPA[:, 0, tc_], ALU.mult, (xres, "RPA"), (xres,))
        tt(P, "dve", dst, x, TF[3][0:64, :], ALU.add, (xres, "tf3"), (dres,))

    P.dma("pool", lambda e: e.dma_start(out=CKV[:, :, 512:768], in_=I["ckvT"][j].rearrange("(c p) n -> p c n", p=128)),
          (), ("CKVctx",))
    dma(P, "sp", TF[0][0:64, 0:256], I["kpeT"][j], (), ("tf0",))
    act(P, KPSQ[:, 512:768], TF[0][0:64, 0:256], AF.Square, ("tf0",), ("KPSQctx",))
    ts(P, "dve", KPEG[:, 512:768], TF[0][0:64, 0:256], vab[0:64, 9:10], None, ALU.mult, None, ("tf0", "VAB"),
       ("KPEGctx",))

    wq = []
    for g in range(2):
        wq.append(load_w(k, w_in[:, O_QL + g * 256:O_QL + (g + 1) * 256], 16, 256))
    for tb in range(NBK):
        cols = slice(tb * BS, (tb + 1) * BS)
        for c in range(4):
            wb, rn = wq[c // 2]
            proj_block(k, wb, rn, (c % 2) * 128, 128, tb, k.PS[c], f"ps{c}")
            cp(P, "act", TF[c], k.PS[c][:, :], (f"ps{c}",), (f"tf{c}",))
            act(P, SQb[c % 2], TF[c], AF.Square, (f"tf{c}",), (f"sqb{c % 2}",))
            mm(P, k.PS[4][:, :], k.ONES[:], SQb[c % 2], c == 0, c == 3, ("ONES", f"sqb{c % 2}"), ("ps4",))
        rstd_from_ps(k, RS, k.PS[4][:, :], 512, ("ps4",), ("rs",))
        for c in range(4):
            stt(P, "dve", QLN[:, c, cols], TF[c], vab[:, c:c + 1], RS, ALU.mult, ALU.mult,
                (f"tf{c}", "rs", "VAB"), (f"QLN{tb}",))
    wkv = load_w(k, w_in[:, O_KV:O_KV + 256], 16, 256)
    wkp = load_w(k, w_in[:, O_KPE:O_KPE + 64], 16, 64)
    for tb in range(NBK):
        cols = slice(tb * BS, (tb + 1) * BS)
        for c in range(2):
            proj_block(k, wkv[0], wkv[1], c * 128, 128, tb, k.PS[c], f"ps{c}")
            cp(P, "act", TF[c], k.PS[c][:, :], (f"ps{c}",), (f"tf{c}",))
            act(P, SQb[c % 2], TF[c], AF.Square, (f"tf{c}",), (f"sqb{c % 2}",))
            mm(P, k.PS[4][:, :], k.ONES[:], SQb[c % 2], c == 0, c == 1, ("ONES", f"sqb{c % 2}"), ("ps4",))
        rstd_from_ps(k, RS, k.PS[4][:, :], 256, ("ps4",), ("rs",))
        proj_block(k, wkp[0], wkp[1], 0, 64, tb, k.PS[2], "ps2")
        cp(P, "act", TF[4][0:64, :], k.PS[2][0:64, :], ("ps2",), ("tf4",))
        for c in range(2):
            stt(P, "dve", TF[c], TF[c], vab[:, 4 + c:5 + c], RS, ALU.mult, ALU.mult,
                (f"tf{c}", "rs", "VAB"), (f"tf{c}",))
        if tb == 0:
            for c in range(2):
                dma(P, "sp", O["o_ckv"][j, c * 128:(c + 1) * 128, :], TF[c], (f"tf{c}",), ("o_ckv",))
                cp(P, "act", CKV[:, c, 0:512], TF[c], (f"tf{c}",), ("CKVp",))
            dma(P, "sp", O["o_kpe"][j], TF[4][0:64, :], ("tf4",), ("o_kpe",))
            act(P, KPSQ[:, 0:512], TF[4][0:64, :], AF.Square, ("tf4",), ("KPSQp",))
            ts(P, "dve", KPEG[:, 0:512], TF[4][0:64, :], vab[0:64, 9:10], None, ALU.mult, None, ("tf4", "VAB"),
               ("KPEGp",))
        else:
            lc = slice((tb - 1) * BS, tb * BS)
            for c in range(2):
                dma(P, "sp", X["b_lat"][c * 128:(c + 1) * 128, lc], TF[c], (f"tf{c}",), ("b_lat",))
            dma(P, "sp", X["b_win"][0:64, lc], TF[4][0:64, :], ("tf4",), ("b_win",))
            ts(P, "dve", TF[5][0:64, :], TF[4][0:64, :], vab[0:64, 9:10], None, ALU.mult, None, ("tf4", "VAB"),
               ("tf5",))
            rope64(TF[2][0:64, :], TF[5][0:64, :], tb, "tf5", "tf2")
            dma(P, "sp", X["b_lat"][256:320, lc], TF[2][0:64, :], ("tf2",), ("b_lat",))
    P.dma("pool", lambda e: e.collective_compute("AllGather", ALU.bypass, replica_groups=RG,
                                                 ins=[X["b_lat"].opt()], outs=[X["g_lat"].opt()]),
          ("b_lat",), ("g_lat",), inc=1)
    P.dma("pool", lambda e: e.collective_compute("AllGather", ALU.bypass, replica_groups=RG,
                                                 ins=[X["b_win"].opt()], outs=[X["g_win"].opt()]),
          ("b_win",), ("g_win",), inc=1)
    for r in range(2):
        kc_ = slice(768 + r * 1024, 768 + (r + 1) * 1024)
        P.dma("pool", lambda e, r=r, kc_=kc_: e.dma_start(
            out=CKV[:, :, kc_], in_=X["g_lat"][r * 320:r * 320 + 256, :].rearrange("(c p) n -> p c n", p=128)),
            ("g_lat",), (f"CKVr{r}",))
        P.dma("pool", lambda e, r=r, kc_=kc_: e.dma_start(out=KPEG[:, kc_], in_=X["g_lat"][r * 320 + 256:r * 320 + 320, :]),
              ("g_lat",), (f"KPEGr{r}",))
        for hb in range(2):
            lc = slice(hb * BS, (hb + 1) * BS)
            kq = slice(768 + r * 1024 + hb * BS, 768 + r * 1024 + (hb + 1) * BS)
            dma(P, "sp", TF[3][0:64, :], X["g_win"][r * 128:r * 128 + 64, lc], ("g_win",), ("tf3",))
            act(P, KPSQ[:, kq], TF[3][0:64, :], AF.Square, ("tf3",), (f"KPSQr{r}",))
    ckpt(f"lat{j}")
    KRES = ("CKVctx", "CKVp", "CKVr0", "CKVr1")
    PRES = ("KPEGctx", "KPEGp", "KPEGr0", "KPEGr1")
    SRES = ("KPSQctx", "KPSQp", "KPSQr0", "KPSQr1")

    KB = [(i * 512, min(512, NKEY - i * 512)) for i in range(6)]
    for h in range(8):
        wqn = load_w(k, I["w_q_up"][j][:, h * 192:(h + 1) * 192], 4, 192)
        wkv_ = load_w(k, I["w_kv_up"][j][:, h * 256:(h + 1) * 256], 2, 256)
        for tb in range(NBK):
            cols = slice(tb * BS, (tb + 1) * BS)
            proj_block(k, wqn[0], wqn[1], 0, 128, tb, k.PS[0], "ps0", nk=4, rhs=QLN, rres=f"QLN{tb}")
            proj_block(k, wqn[0], wqn[1], 128, 64, tb, k.PS[1], "ps1", nk=4, rhs=QLN, rres=f"QLN{tb}")
            cp(P, "act", TF[0], k.PS[0][:, :], ("ps0",), ("tf0",))
            cp(P, "act", TF[1][0:64, :], k.PS[1][0:64, :], ("ps1",), ("tf1",))
            act(P, SQb[0], TF[0], AF.Square, ("tf0",), ("sqb0",))
            act(P, SQb[1][0:64, :], TF[1][0:64, :], AF.Square, ("tf1",), ("sqb1",))
            mm(P, k.PS[4][:, :], k.ONES[:], SQb[0], True, False, ("ONES", "sqb0"), ("ps4",))
            mm(P, k.PS[4][:, :], k.ONES[0:64, :], SQb[1][0:64, :], False, True, ("ONES", "sqb1"), ("ps4",))
            rstd_from_ps(k, RS, k.PS[4][:, :], 192, ("ps4",), ("rs",))
            stt(P, "dve", QTN[:, cols], TF[0], vab[:, 6:7], RS, ALU.mult, ALU.mult, ("tf0", "rs", "VAB"),
                (f"QTN{tb}",))
            stt(P, "dve", TF[2][0:64, :], TF[1][0:64, :], vab[0:64, 7:8], RS[0:64, :], ALU.mult, ALU.mult,
                ("tf1", "rs", "VAB"), ("tf2",))
            if tb == 0:
                cp(P, "dve", QTP[:, cols], TF[2][0:64, :], ("tf2",), (f"QTP{tb}",))
            else:
                rope64(QTP[:, cols], TF[2][0:64, :], tb, "tf2", f"QTP{tb}")
        for bi, (c0, n) in enumerate(KB):
            kc_ = slice(c0, c0 + n)
            ps = k.PS[bi % 2]
            pr = f"ps{bi % 2}"
            for c in range(2):
                mm(P, ps[:, 0:n], wkv_[0][:, c, 0:128], CKV[:, c, kc_], c == 0, c == 1, (wkv_[1],) + KRES, (pr,))
            cp(P, "act", TF[bi % 2][:, 0:n], ps[:, 0:n], (pr,), (f"tf{bi % 2}",))
            act(P, SQb[bi % 2][:, 0:n], TF[bi % 2][:, 0:n], AF.Square, (f"tf{bi % 2}",), (f"sqb{bi % 2}",))
            ps2 = k.PS[2 + bi % 2]
            pr2 = f"ps{2 + bi % 2}"
            mm(P, ps2[:, 0:n], k.ONES[:], SQb[bi % 2][:, 0:n], True, False, ("ONES", f"sqb{bi % 2}"), (pr2,))
            mm(P, ps2[:, 0:n], k.ONES[0:64, :], KPSQ[:, kc_], False, True, ("ONES",) + SRES, (pr2,))
            rstd_from_ps(k, RS[:, 0:n], ps2[:, 0:n], 192, (pr2,), ("rs",))
            stt(P, "dve", KTN[:, kc_], TF[bi % 2][:, 0:n], vab[:, 8:9], RS[:, 0:n], ALU.mult, ALU.mult,
                (f"tf{bi % 2}", "rs", "VAB"), ("KTN",))
            tt(P, "dve", KTP[:, kc_], KPEG[:, kc_], RS[0:64, 0:n], ALU.mult, PRES + ("rs",), ("KTP",))
        for g in range(6):
            nt_ = min(4, 22 - g * 4)
            ps = k.PS[g % 2]
            pr = f"ps{g % 2}"
            for t in range(nt_):
                kt = g * 4 + t
                for c in range(2):
                    mm(P, ps[:, t * 128:(t + 1) * 128], CKV[:, c, kt * 128:(kt + 1) * 128], wkv_[0][:, c, 128:256],
                       c == 0, c == 1, (wkv_[1],) + KRES, (pr,))
            cp(P, "act", VH[:, g * 4:g * 4 + nt_, :], ps[:, 0:nt_ * 128].rearrange("p (a b) -> p a b", b=128),
               (pr,), ("VH",))
        wg = load_w(k, w_in[:, O_AG + h * 128:O_AG + (h + 1) * 128], 16, 128)
        groups = [(0, 256, [0, 1]), (256, 256, [2, 3]), (512, 512, list(range(4, 22))),
                  (1024, 512, list(range(4, 22)))]
        for gi, (q0, qn, kts) in enumerate(groups):
            qs = slice(q0, q0 + qn)
            tb = q0 // BS
            Ops, Dps = k.PS[4], k.PS[5]
            def s_mm(ki):
                ks = slice(kts[ki] * 128, (kts[ki] + 1) * 128)
                sp_ = k.PS[ki % 2]
                spr = f"ps{ki % 2}"
                mm(P, sp_[:, 0:qn], KTN[:, ks], QTN[:, qs], True, False, ("KTN", f"QTN{tb}"), (spr,))
                mm(P, sp_[:, 0:qn], KTP[:, ks], QTP[:, qs], False, True, ("KTP", f"QTP{tb}"), (spr,))

            s_mm(0)
            for ki, kt in enumerate(kts):
                sp_ = k.PS[ki % 2]
                spr = f"ps{ki % 2}"
                if ki + 1 < len(kts):
                    s_mm(ki + 1)
                act(P, PT[ki % 2][:, 0:qn], sp_[:, 0:qn], AF.Exp, (spr,), (f"pt{ki % 2}",), scale=192 ** -0.5)
                mm(P, Ops[:, 0:qn], VH[:, kt, :], PT[ki % 2][:, 0:qn], ki == 0, ki == len(kts) - 1,
                   ("VH", f"pt{ki % 2}"), ("ps4",))
                mm(P, Dps[:, 0:qn], k.ONES[:], PT[ki % 2][:, 0:qn], ki == 0, ki == len(kts) - 1,
                   ("ONES", f"pt{ki % 2}"), ("ps5",))
            act(P, TF[4][:, 0:qn], Dps[:, 0:qn], AF.Ln, ("ps5",), ("tf4",))
            act(P, TF[4][:, 0:qn], TF[4][:, 0:qn], AF.Exp, ("tf4",), ("tf4",), scale=-1.0)
            tt(P, "dve", TF[5][:, 0:qn], Ops[:, 0:qn], TF[4][:, 0:qn], ALU.mult, ("ps4", "tf4"), ("tf5",))
            gp = k.PS[2 + gi % 2]
            gpr = f"ps{2 + gi % 2}"
            for kc in range(16):
                mm(P, gp[:, 0:qn], wg[0][:, kc, :], k.HT[:, kc, qs], kc == 0, kc == 15, (wg[1], f"HT{tb}"), (gpr,))
            act(P, TF[3][:, 0:qn], gp[:, 0:qn], AF.Silu, (gpr,), ("tf3",))
            tt(P, "dve", k.OT[:, h, qs], TF[5][:, 0:qn], TF[3][:, 0:qn], ALU.mult, ("tf5", "tf3"), (f"OT{tb}",))
    ckpt(f"mla{j}")
    hgrn(k, j)


def hgrn(k, j):
    P, I, O, X = k.P, k.I, k.O, k.X
    P.barrier()
    s = Scr(k, "hg")
    T1 = s.F("t1", 128, NT)
    T2 = s.F("t2", 128, NT)
    T3 = s.F("t3", 128, NT)
    T4 = s.F("t4", 128, NT)
    OPH = s.F("oph", 128, NT)
    OAC = OPH
    CS = s.F("cs", 128, 6, NCH)
    ST = s.F("st", 128, 128)
    ST1 = s.F("st1", 128, 128)
    GST = s.F("gst", 128, 2, 128)
    Qb = s.B("q", 128, NT)
    Kb = s.B("kb", 128, NT)
    QTl = s.B("qtl", 128, NT)
    KTl = s.B("ktl", 128, NT)
    KTM = s.B("ktm", 128, 12, 128)
    Gb = s.B("g", 128, NT)
    Vb = s.B("v", 128, 12, 128)
    AT = s.B("at", 128, NT // 2)
    STb = s.B("stb", 128, 128)
    SQb = s.B("sqb", 128, BS)
    w_in = I["w_in_ab"][j]
    identb = k.CSTB[:, 0:128]
    vab = k.VAB[:, j, :]
    SEGS = [(0, 256), (256, 256), (512, 1024)]

    for h in range(8):
        wqg = load_w(k, w_in[:, O_BQ + h * 128:O_BQ + (h + 1) * 128], 16, 128)
        for tb in range(NBK):
            cols = slice(tb * BS, (tb + 1) * BS)
            proj_block(k, wqg[0], wqg[1], 0, 128, tb, k.PS[tb % 2], f"ps{tb % 2}")
            act(P, Qb[:, cols], k.PS[tb % 2][:, :], AF.Silu, (f"ps{tb % 2}",), ("hq",))
        wgg = load_w(k, w_in[:, O_BG + h * 128:O_BG + (h + 1) * 128], 16, 128)
        for tb in range(NBK):
            cols = slice(tb * BS, (tb + 1) * BS)
            proj_block(k, wgg[0], wgg[1], 0, 128, tb, k.PS[2 + tb % 2], f"ps{2 + tb % 2}")
            act(P, Gb[:, cols], k.PS[2 + tb % 2][:, :], AF.Silu, (f"ps{2 + tb % 2}",), ("hg",))
        wv = load_w(k, w_in[:, O_BI + h * 128:O_BI + (h + 1) * 128], 16, 128)
        for g in range(3):
            ps = k.PS[g % 2]
            pr = f"ps{g % 2}"
            for t in range(4):
                tl = g * 4 + t
                for kc in range(16):
                    mm(P, ps[:, t * 128:(t + 1) * 128], k.HT[:, kc, tl * 128:(tl + 1) * 128], wv[0][:, kc, :],
                       kc == 0, kc == 15, (wv[1], f"HT{tl // 4}"), (pr,))
            cp(P, "act", Vb[:, g * 4:(g + 1) * 4, :], ps[:, :].rearrange("p (a b) -> p a b", b=128), (pr,), ("hv",))

        for ph in range(2):
            wf = load_w(k, w_in[:, (O_F1, O_F2)[ph] + h * 128:(O_F1, O_F2)[ph] + (h + 1) * 128], 16, 128)
            lb = k.LB[:, j, ph, h:h + 1]
            oml = k.OML[:, j, ph, h:h + 1]
            for tb in range(NBK):
                cols = slice(tb * BS, (tb + 1) * BS)
                proj_block(k, wf[0], wf[1], 0, 128, tb, k.PS[tb % 2], f"ps{tb % 2}")
                act(P, T1[:, cols], k.PS[tb % 2][:, :], AF.Sigmoid, (f"ps{tb % 2}",), ("t1",))
            ts(P, "dve", T1, T1, oml, lb, ALU.mult, ALU.add, ("t1", "LB", "OML"), ("t1",))
            act(P, T2, T1, AF.Ln, ("t1",), ("t2",))
            act(P, Kb, T1, AF.Identity, ("t1",), ("hk",), bias=1.0, scale=-1.0)
            P.op("dve", lambda e: e.tensor_tensor_scan(out=T1, data0=k.ONEF[:], data1=T2, initial=0.0,
                                                       op0=ALU.mult, op1=ALU.add), ("t2", "ONEF", "hk"), ("t1",))
            tt(P, "dve", T2, T1, T2, ALU.subtract, ("t1", "t2"), ("t2",))
            Bv = T1.rearrange("p (c t) -> p c t", t=HC)
            Xv = T2.rearrange("p (c t) -> p c t", t=HC)
            lo, hi, mid, din_, d2, d1 = (CS[:, i, :] for i in range(6))
            cp(P, "dve", lo, Xv[:, :, 0], ("t2",), ("cs",))
            cp(P, "dve", hi, Bv[:, :, HC - 1], ("t1", "cs"), ("cs",))
            if ph == 0:
                cp(P, "dve", mid, Bv[:, :, HC // 2], ("t1", "cs"), ("cs",))
                tt(P, "dve", T3.rearrange("p (c t) -> p c t", t=HC), Bv,
                   mid.unsqueeze(2).broadcast_to([128, NCH, HC]), ALU.subtract, ("t1", "cs"), ("t3",))
                tt(P, "dve", din_, mid, lo, ALU.subtract, ("cs",), ("cs",))
                tt(P, "dve", d2, hi, mid, ALU.subtract, ("cs",), ("cs",))
            else:
                cp(P, "dve", mid, Xv[:, :, HC // 2], ("t2", "cs"), ("cs",))
                tt(P, "dve", T3.rearrange("p (c t) -> p c t", t=HC),
                   mid.unsqueeze(2).broadcast_to([128, NCH, HC]), Xv, ALU.subtract, ("t2", "cs"), ("t3",))
                tt(P, "dve", din_, hi, mid, ALU.subtract, ("cs",), ("cs",))
                tt(P, "dve", d2, mid, lo, ALU.subtract, ("cs",), ("cs",))
            tt(P, "dve", d1, hi, lo, ALU.subtract, ("cs",), ("cs",))
            act(P, CS[:, 3:6, :], CS[:, 3:6, :], AF.Exp, ("cs",), ("cs",))
            act(P, T4, T3, AF.Exp, ("t3",), ("t4",))
            tt(P, "dve", QTl, Qb, T4, ALU.mult, ("hq", "t4"), ("qtl",))
            act(P, T4, T3, AF.Exp, ("t3", "qtl"), ("t4",), scale=-1.0)
            tt(P, "dve", KTl, Kb, T4, ALU.mult, ("hk", "t4"), ("ktl",))
            for tl in range(12):
                psb = k.PS[6][:, :].bitcast(BF16)
                o = psb[:, (tl % 4) * 128:(tl % 4 + 1) * 128]
                tr(P, o, KTl[:, tl * 128:(tl + 1) * 128], identb, ("ktl", "CSTB"), ("ps6",))
                if tl % 4 == 3:
                    cp(P, "act", KTM[:, tl - 3:tl + 1, :], psb[:, 0:512].rearrange("p (a b) -> p a b", b=128),
                       ("ps6",), ("ktm",))
            CPT = 128 // HC
            for g in range(3):
                ps = k.PS[g % 2]
                pr = f"ps{g % 2}"
                for t in range(4):
                    tl = g * 4 + t
                    for ci in range(CPT):
                        c0 = tl * 128 + ci * HC
                        mm(P, ps[ci * HC:(ci + 1) * HC, t * HC:(t + 1) * HC], KTl[:, c0:c0 + HC], QTl[:, c0:c0 + HC],
                           True, True, ("ktl", "qtl"), (pr,))
                mask = k.CST[:, 704 + ph * HC:704 + (ph + 1) * HC].bitcast(mybir.dt.uint32)
                atg = AT[:, g * 4 * HC:(g + 1) * 4 * HC]
                P.op("dve", lambda e, o=atg: e.memset(o, 0.0), (), ("at",))
                P.op("dve", lambda e, o=atg.rearrange("p (a b) -> p a b", b=HC),
                     d=ps[:, 0:4 * HC].rearrange("p (a b) -> p a b", b=HC),
                     m=mask.unsqueeze(1).broadcast_to([128, 4, HC]): e.copy_predicated(out=o, mask=m, data=d),
                     (pr, "CST", "at"), ("at",))
            DSS = T4[:, 0:1024].rearrange("p (r c v) -> p r c v", r=2, c=4)
            STBA = T3.bitcast(BF16).rearrange("p (c v) -> p c v", v=128)
            seg_groups = {0: [0], 1: [1], 2: [2, 3, 4, 5]}
            sorder = [2, 0, 1] if ph == 0 else [0, 1, 2]
            glist = []
            for si in sorder:
                gs = seg_groups[si] if ph == 0 else seg_groups[si][::-1]
                glist += [(si, g) for g in gs]

            def emit_dS(n, g):
                slot = n % 2
                for ci in range(CPT):
                    ps = k.PS[4 + 2 * slot + ci]
                    pr = f"ps{4 + 2 * slot + ci}"
                    for a_ in range(2):
                        c = g * 4 + a_ * 2 + ci
                        tl = c // CPT
                        mm(P, ps[:, a_ * 128:(a_ + 1) * 128], KTM[ci * HC:(ci + 1) * HC, tl, :],
                           Vb[ci * HC:(ci + 1) * HC, tl, :], True, True, ("ktm", "hv"), (pr,))
                for ci in range(CPT):
                    ps = k.PS[4 + 2 * slot + ci]
                    pr = f"ps{4 + 2 * slot + ci}"
                    tt(P, "dve", DSS[:, slot].rearrange("p (a b) v -> p a b v", b=2)[:, :, ci, :],
                       ps[:, 0:256].rearrange("p (c v) -> p c v", v=128),
                       d2[:, g * 4:(g + 1) * 4].rearrange("p (a b) -> p a b", b=2)[:, :, ci].unsqueeze(2)
                       .broadcast_to([128, 2, 128]), ALU.mult, (pr, "cs"), (f"dss{slot}", "t4"))

            def chain_group(n, g):
                slot = n % 2
                cs_ = list(range(g * 4, g * 4 + 4))
                if ph == 1:
                    cs_ = cs_[::-1]
                for c in cs_:
                    q = c - g * 4
                    ts(P, "dve", STBA[:, c, :], ST, din_[:, c:c + 1], None, ALU.mult, None, ("st", "cs"), ("t3",))
                    stt(P, "dve", ST, ST, d1[:, c:c + 1], DSS[:, slot, q, :], ALU.mult, ALU.add,
                        ("st", "cs", f"dss{slot}", "t4"), ("st",))

            emitted = [0]

            def ensure_dS(upto):
                while emitted[0] <= min(upto, len(glist) - 1):
                    emit_dS(emitted[0], glist[emitted[0]][1])
                    emitted[0] += 1

            ensure_dS(1)
            for n, (si, g) in enumerate(glist):
                first = (n == 0 or glist[n - 1][0] != si)
                last = (n == len(glist) - 1 or glist[n + 1][0] != si)
                if first:
                    if si < 2:
                        P.op("dve", lambda e: e.memset(ST, 0.0), ("st",), ("st",))
                    elif ph == 0:
                        dma(P, "sp", ST, I["s0"][j, h], ("st",), ("st",))
                    else:
                        dma(P, "sp", GST, X["g_st"][h].rearrange("(r p) n -> p r n", r=2), ("g_st", "gst"), ("gst",))
                        ts(P, "dve", ST1, GST[:, 0, :], k.SEL[:, 0:1], None, ALU.mult, None, ("gst", "SEL", "st1"),
                           ("st1",))
                        stt(P, "dve", ST, GST[:, 1, :], k.SEL[:, 1:2], ST1, ALU.mult, ALU.add,
                            ("gst", "st1", "SEL", "st"), ("st",))
                chain_group(n, g)
                ensure_dS(n + 2)
                if last:
                    if si < 2:
                        dma(P, "sp", O["o_st"][j, ph, si, h], ST, ("st",), ("o_st",))
                    elif ph == 0:
                        dma(P, "sp", X["b_st"][h], ST, ("st",), (f"b_st{h}",))
                        P.dma("pool", lambda e, h=h: e.collective_compute(
                            "AllGather", ALU.bypass, replica_groups=RG, ins=[X["b_st"][h].opt()],
                            outs=[X["g_st"][h].opt()]), (f"b_st{h}",), ("g_st",), inc=1)
            for bi, (b0, bl) in enumerate([(0, 256), (256, 256), (512, 512), (1024, 512)]):
                ops_b = k.PS[2 + bi % 2]
                opr_b = f"ps{2 + bi % 2}"
                for c in range(b0 // HC, (b0 + bl) // HC):
                    tl, ci = c // CPT, c % CPT
                    cc = slice(c * HC, (c + 1) * HC)
                    oc = slice(c * HC - b0, c * HC - b0 + HC)
                    mm(P, ops_b[:, oc], Vb[ci * HC:(ci + 1) * HC, tl, :], AT[ci * HC:(ci + 1) * HC, tl * HC:(tl + 1) * HC],
                       True, False, ("hv", "at"), (opr_b,))
                    mm(P, ops_b[:, oc], STBA[:, c, :], QTl[:, cc], False, True, ("t3", "qtl"), (opr_b,))
                if ph == 0:
                    cp(P, "act", OPH[:, b0:b0 + bl], ops_b[:, 0:bl], (opr_b,), ("oph",))
                else:
                    tt(P, "dve", OPH[:, b0:b0 + bl], ops_b[:, 0:bl], OPH[:, b0:b0 + bl], ALU.add,
                       (opr_b, "oph"), ("oph",))
        for tb in range(NBK):
            cols = slice(tb * BS, (tb + 1) * BS)
            act(P, SQb, OAC[:, cols], AF.Square, ("oph",), ("hsq",))
            mm(P, k.PS[0][:, :], k.ONES[:], SQb, True, True, ("ONES", "hsq"), ("ps0",))
            rstd_from_ps(k, T3[:, 0:BS], k.PS[0][:, :], 128, ("ps0",), ("t3",))
            stt(P, "dve", T4[:, 0:BS], OAC[:, cols], vab[:, 10:11], T3[:, 0:BS], ALU.mult, ALU.mult,
                ("oph", "t3", "VAB"), ("t4",))
            tt(P, "dve", k.OT[:, 8 + h, cols], T4[:, 0:BS], Gb[:, cols], ALU.mult, ("t4", "hg"), (f"OT{tb}",))


def c_layer(k, j):
    P, I, O, X = k.P, k.I, k.O, k.X
    P.barrier()
    s = Scr(k, "c")
    TFB = s.F("tfb", 128, 6, BS)
    TF = [TFB[:, i, :] for i in range(6)]
    GW = TFB[:, 0:4, :].rearrange("p a b -> p (a b)").rearrange("p (r n) -> p r n", r=2)
    GWR = ("tf0", "tf1", "tf2", "tf3")
    RS = s.F("rs", 128, BS)
    KT = s.B("kt", 128, 4, NT)
    VT = s.B("vt", 128, 12, 512)
    QT = s.B("qt", 128, 4, NT)
    KC = s.B("kc", 128, 4, 256)
    VCx = s.B("vcx", 128, 2, 512)
    KB_ = s.B("kbnd", 128, 4, 128)
    VB_ = s.B("vbnd", 128, 512)
    PT = [s.B(f"pt{i}", 128, BS) for i in range(2)]
    SQb = [s.B(f"sqb{i}", 128, BS) for i in range(2)]
    w_in = I["w_in_c"][j]
    RC = k.CST[:, 128:256]
    vc = k.VC[:, j, :]
    Mprev, Mnext, Manti = (k.CSTB[:, 256 + i * 128:256 + (i + 1) * 128] for i in range(3))
    scale = 128 ** -0.5

    P.dma("pool", lambda e: e.dma_start(out=KC[:], in_=I["kcT"][j].rearrange("g p n -> p g n")), (), ("KC",))
    P.dma("pool", lambda e: e.dma_start(out=VCx[:], in_=I["vc"][j].rearrange("(t p) n -> p t n", p=128)), (), ("VCx",))

    par_ctr = [0]

    def head_fm(gcol, out_bf, tb, outres, prompt_out=None):
        par = par_ctr[0]
        par_ctr[0] ^= 1
        A0, A1, A2 = TF[3 * par], TF[3 * par + 1], TF[3 * par + 2]
        n0, n1, n2 = f"tf{3 * par}", f"tf{3 * par + 1}", f"tf{3 * par + 2}"
        psp, sqn = f"ps{par}", f"sqb{par}"
        pst, prt = k.PS[4 + 2 * par], k.PS[5 + 2 * par]
        pstn, prtn = f"ps{4 + 2 * par}", f"ps{5 + 2 * par}"
        cp(P, "act", A0, k.PS[par][:, :], (psp,), (n0,))
        act(P, SQb[par], A0, AF.Square, (n0,), (sqn,))
        mm(P, pst[:, :], k.ONES[:], SQb[par], True, True, ("ONES", sqn), (pstn,))
        rstd_from_ps(k, A2, pst[:, :], 128, (pstn,), (n2,))
        stt(P, "dve", A1, A0, vc[:, gcol:gcol + 1], A2, ALU.mult, ALU.mult, (n0, n2, "VC"), (n1,))
        if tb == 0:
            if prompt_out is not None:
                dma(P, "sp", prompt_out, A1, (n1,), ("o_kc",))
            cp(P, "dve", out_bf, A1, (n1,), (outres,))
            return
        tc_ = slice((tb - 1) * BS, tb * BS)
        mm(P, prt[:, :], RC, A1, True, True, ("CST", n1), (prtn,))
        tt(P, "dve", A2, prt[:, :], k.RPC[:, 1, tc_], ALU.mult, (prtn, "RPC"), (n2,))
        tt(P, "dve", A1, A1, k.RPC[:, 0, tc_], ALU.mult, (n1, "RPC"), (n1,))
        tt(P, "dve", out_bf, A1, A2, ALU.add, (n1, n2), (outres,))

    def cur_ps():
        return k.PS[par_ctr[0]], f"ps{par_ctr[0]}"

    for g in range(2):
        wk = load_w(k, w_in[:, 2048 + g * 256:2048 + (g + 1) * 256], 16, 256)
        for hh in range(2):
            kh = g * 2 + hh
            for tb in range(NBK):
                proj_block(k, wk[0], wk[1], hh * 128, 128, tb, *cur_ps())
                head_fm(1, KT[:, kh, tb * BS:(tb + 1) * BS], tb, "KT",
                        prompt_out=O["o_kc"][j, kh] if tb == 0 else None)
    ckpt(f"c_k{j}")
    wvs = [load_w(k, w_in[:, 2560 + g * 256:2560 + (g + 1) * 256], 16, 256) for g in range(2)]
    for tl in range(12):
        ps = k.PS[tl % 2]
        pr = f"ps{tl % 2}"
        for g in range(2):
            for kc in range(16):
                mm(P, ps[:, g * 256:(g + 1) * 256], k.HT[:, kc, tl * 128:(tl + 1) * 128], wvs[g][0][:, kc, :],
                   kc == 0, kc == 15, (wvs[g][1], f"HT{tl // 4}"), (pr,))
        cp(P, "act", VT[:, tl, :], ps[:, :], (pr,), ("VT",))
        if tl < 4:
            cp(P, "dve", TF[4 + tl % 2], ps[:, :], (pr,), (f"tf{4 + tl % 2}",))
            dma(P, "sp", O["o_vc"][j, tl * 128:(tl + 1) * 128, :], TF[4 + tl % 2], (f"tf{4 + tl % 2}",), ("o_vc",))
    ckpt(f"c_v{j}")
    cp(P, "dve", GW[:, 0, 0:512].rearrange("p (a b) -> p a b", b=128), KT[:, :, NT - 128:NT], ("KT",) + GWR, GWR)
    cp(P, "dve", GW[:, 0, 512:1024], VT[:, 11, :], ("VT",) + GWR, GWR)
    dma(P, "sp", X["b_win"], GW[:, 0, :], GWR, ("b_win",))
    P.dma("pool", lambda e: e.collective_compute("AllGather", ALU.bypass, replica_groups=RG,
                                                 ins=[X["b_win"].opt()], outs=[X["g_win"].opt()]),
          ("b_win",), ("g_win",), inc=1)
    dma(P, "sp", GW, X["g_win"].rearrange("(r p) n -> p r n", r=2), ("g_win",) + GWR, GWR)
    ts(P, "dve", GW[:, 0, :], GW[:, 0, :], k.SEL[:, 0:1], None, ALU.mult, None, GWR + ("SEL",), GWR)
    stt(P, "dve", GW[:, 0, :], GW[:, 1, :], k.SEL[:, 1:2], GW[:, 0, :], ALU.mult, ALU.add, GWR + ("SEL",), GWR)
    cp(P, "dve", KB_, GW[:, 0, 0:512].rearrange("p (a b) -> p a b", b=128), GWR, ("KB",))
    cp(P, "dve", VB_, GW[:, 0, 512:1024], GWR, ("VB",))

    ckpt(f"c_kv{j}")
    for g in range(4):
        for pair in range(2):
            wq = load_w(k, w_in[:, g * 512 + pair * 256:g * 512 + (pair + 1) * 256], 16, 256)
            for hh in range(2):
                qh = pair * 2 + hh
                for tb in range(NBK):
                    cols = slice(tb * BS, (tb + 1) * BS)
                    proj_block(k, wq[0], wq[1], hh * 128, 128, tb, *cur_ps())
                    head_fm(0, QT[:, qh, cols], tb, f"QT{tb}")
        ckpt(f"c_q{j}_{g}")
        blocks = []
        for sq in range(2):
            for qb in range(2):
                q0 = sq * 256 + qb * 128
                keys = [(KT[:, g, sq * 256 + t * 128:sq * 256 + (t + 1) * 128], VT[:, sq * 2 + t, g * 128:(g + 1) * 128],
                         None, ("KT", "VT")) for t in range(2)]
                blocks.append((q0, keys))
        for qb in range(8):
            q0 = 512 + qb * 128
            keys = [(KC[:, g, t * 128:(t + 1) * 128], VCx[:, t, g * 128:(g + 1) * 128], None, ("KC", "VCx"))
                    for t in range(2)]
            if qb > 0:
                keys.append((KT[:, g, q0 - 128:q0], VT[:, 4 + qb - 1, g * 128:(g + 1) * 128], Mprev, ("KT", "VT")))
            keys.append((KT[:, g, q0:q0 + 128], VT[:, 4 + qb, g * 128:(g + 1) * 128], None, ("KT", "VT")))
            if qb < 7:
                keys.append((KT[:, g, q0 + 128:q0 + 256], VT[:, 4 + qb + 1, g * 128:(g + 1) * 128], Mnext, ("KT", "VT")))
            else:
                keys.append((KB_[:, g, :], VB_[:, g * 128:(g + 1) * 128], Manti, ("KB", "VB")))
            blocks.append((q0, keys))
        for bi, (q0, keys) in enumerate(blocks):
            tb = q0 // BS
            qs = slice(q0, q0 + 128)
            Ops, Dps = k.PS[4 + 2 * (bi % 2)], k.PS[5 + 2 * (bi % 2)]
            opr, dpr = f"ps{4 + 2 * (bi % 2)}", f"ps{5 + 2 * (bi % 2)}"
            def s_mm(ki):
                kap_, _, _, kres_ = keys[ki]
                mm(P, k.PS[2 + ki % 2][:, :].rearrange("p (a b) -> p a b", b=128), kap_, QT[:, :, qs], True, True,
                   kres_ + (f"QT{tb}",), (f"ps{2 + ki % 2}",))

            s_mm(0)
            for ki, (kap, vap, mask, kres) in enumerate(keys):
                sp_ = k.PS[2 + ki % 2]
                spr = f"ps{2 + ki % 2}"
                if ki + 1 < len(keys):
                    s_mm(ki + 1)
                act(P, PT[ki % 2], sp_[:, :], AF.Exp, (spr,), (f"pt{ki % 2}",), scale=scale)
                if mask is not None:
                    tt(P, "dve", PT[ki % 2].rearrange("p (a b) -> p a b", b=128),
                       PT[ki % 2].rearrange("p (a b) -> p a b", b=128),
                       mask.unsqueeze(1).broadcast_to([128, 4, 128]), ALU.mult, (f"pt{ki % 2}", "CSTB"),
                       (f"pt{ki % 2}",))
                mm(P, Ops[:, :], vap, PT[ki % 2], ki == 0, ki == len(keys) - 1, kres + (f"pt{ki % 2}",), (opr,))
                mm(P, Dps[:, :], k.ONES[:], PT[ki % 2], ki == 0, ki == len(keys) - 1, ("ONES", f"pt{ki % 2}"), (dpr,))
            tt(P, "dve", TF[4].rearrange("p (a b) -> p a b", b=128), Dps[:, :].rearrange("p (a b) -> p a b", b=128),
               k.ESINK[:, j, g * 4:(g + 1) * 4].unsqueeze(2).broadcast_to([128, 4, 128]), ALU.add,
               (dpr, "ESINK"), ("tf4",))
            act(P, TF[4], TF[4], AF.Ln, ("tf4",), ("tf4",))
            act(P, TF[4], TF[4], AF.Exp, ("tf4",), ("tf4",), scale=-1.0)
            tt(P, "dve", k.OT[:, g * 4:(g + 1) * 4, qs], Ops[:, :].rearrange("p (a b) -> p a b", b=128),
               TF[4].rearrange("p (a b) -> p a b", b=128), ALU.mult, (opr, "tf4"), (f"OT{tb}",))
        ckpt(f"c_att{j}_{g}")
        for pair in range(2):
            wg = load_w(k, w_in[:, 3072 + g * 512 + pair * 256:3072 + g * 512 + (pair + 1) * 256], 16, 256)
            for hh in range(2):
                qh = pair * 2 + hh
                for tb in range(NBK):
                    cols = slice(tb * BS, (tb + 1) * BS)
                    gp_ = tb % 2
                    proj_block(k, wg[0], wg[1], hh * 128, 128, tb, k.PS[gp_], f"ps{gp_}")
                    act(P, TF[4 + gp_], k.PS[gp_][:, :], AF.Silu, (f"ps{gp_}",), (f"tf{4 + gp_}",))
                    tt(P, "dve", k.OT[:, g * 4 + qh, cols], k.OT[:, g * 4 + qh, cols], TF[4 + gp_], ALU.mult,
                       (f"tf{4 + gp_}", f"OT{tb}"), (f"OT{tb}",))


_NC_CACHE = {}


def _f32(a):
    return np.ascontiguousarray(np.asarray(a, dtype=np.float32))


def _rope_tables(rot_dim, pos):
    n_freq = rot_dim // 4
    inv = (10000.0 ** (-np.arange(n_freq, dtype=np.float32) / n_freq)).astype(np.float32)
    row = np.floor(pos / 64).astype(np.float32)
    col = (pos % 64).astype(np.float32)
    ang = np.concatenate([row[:, None] * inv, col[:, None] * inv], axis=-1).astype(np.float32)
    cos, sin = np.cos(ang).astype(np.float32), np.sin(ang).astype(np.float32)
    idm = pos < 0
    cos[idm] = 1.0
    sin[idm] = 0.0
    c = np.concatenate([cos, cos], axis=-1).T
    s_ = np.concatenate([-sin, sin], axis=-1).T
    return np.stack([c, s_], axis=0).astype(np.float32)


def kernel(x_prompt, x_sample, cache_ckv, cache_kpe, state_hgrn_fwd, state_hgrn_bwd, cache_k_c, cache_v_c,
           c, c_ctx, mod_w_ab, mod_b_ab, norm_ab, w_in_ab, q_lora_norm, kv_lora_norm, w_q_up, w_kv_up,
           q_norm_ab, k_norm_ab, hgrn_lb_logits, hgrn_out_norm, w_out_ab, mod_w_c, mod_b_c, norm_c, w_in_c,
           q_norm_c, k_norm_c, sink_c, w_out_c):
    A = {n: _f32(v) for n, v in locals().items()}
    if "nc" not in _NC_CACHE:
        _NC_CACHE["nc"] = build_nc()
    nc = _NC_CACHE["nc"]

    cst = np.zeros((128, 832), np.float32)
    cst[:, 0:128] = np.eye(128)
    m = np.arange(128)
    cst[(m + 64) % 128, 128 + m] = 1.0
    jj, ii = np.meshgrid(np.arange(128), np.arange(128), indexing="ij")
    cst[:, 256:384] = (jj >= ii)
    cst[:, 384:512] = (jj <= ii)
    cst[:, 512:640] = (ii + jj >= 127)
    m64 = np.arange(64)
    cst[(m64 + 32) % 64, 640 + m64] = 1.0
    s32, t32 = np.meshgrid(np.arange(128) % HC, np.arange(HC), indexing="ij")
    cst[:, 704:704 + HC] = (s32 <= t32)
    cst[:, 704 + HC:704 + 2 * HC] = (s32 >= t32)

    def fm(v, n):
        return np.ascontiguousarray(v.reshape(n, 128).T)

    modb = np.stack([fm(A["mod_b_ab"][0], 48), fm(A["mod_b_c"][0], 48), fm(A["mod_b_ab"][1], 48),
                     fm(A["mod_b_c"][1], 48)])
    normT = np.stack([fm(A["norm_ab"][0], 16), fm(A["norm_c"][0], 16), fm(A["norm_ab"][1], 16),
                      fm(A["norm_c"][1], 16)])
    vecab = np.zeros((2, 128, 16), np.float32)
    for j in range(2):
        vecab[j, :, 0:4] = fm(A["q_lora_norm"][j], 4)
        vecab[j, :, 4:6] = fm(A["kv_lora_norm"][j], 2)
        vecab[j, :, 6] = A["q_norm_ab"][j][0:128]
        vecab[j, 0:64, 7] = A["q_norm_ab"][j][128:192]
        vecab[j, :, 8] = A["k_norm_ab"][j][0:128]
        vecab[j, 0:64, 9] = A["k_norm_ab"][j][128:192]
        vecab[j, :, 10] = A["hgrn_out_norm"][j]
    vecc = np.zeros((2, 128, 18), np.float32)
    for j in range(2):
        vecc[j, :, 0] = A["q_norm_c"][j]
        vecc[j, :, 1] = A["k_norm_c"][j]
        vecc[j, :, 2:18] = A["sink_c"][j][None, :]
    w_in_ab_sw = A["w_in_ab"].copy()
    w_in_ab_sw[:, :, O_F1:O_F1 + 1024] = A["w_in_ab"][:, :, O_F2:O_F2 + 1024]
    w_in_ab_sw[:, :, O_F2:O_F2 + 1024] = A["w_in_ab"][:, :, O_F1:O_F1 + 1024]

    in_maps = []
    for core in range(8):
        odd = core % 2
        b = core // 2
        xp = A["x_prompt"][2 * core:2 * core + 2]
        xs = A["x_sample"][b, 0:1024] if not odd else A["x_sample"][b, 1024:2048]
        pos = np.arange(1024, dtype=np.float32) if not odd else np.arange(1024, 2048, dtype=np.float32)
        if odd:
            xp = xp[:, ::-1]
            xs = xs[::-1]
            pos = pos[::-1]
        xt = np.concatenate([xp[0], xp[1], xs], axis=0)
        posall = np.concatenate([-np.ones(512, np.float32), pos])
        lbl = A["hgrn_lb_logits"]
        if odd:
            lbl = lbl[:, ::-1]
        lbl_fm = np.ascontiguousarray(lbl.reshape(2, 2, 8, 128).transpose(3, 0, 1, 2))
        condT = np.stack([fm(A["c_ctx"], 16), fm(A["c"][b], 16)], axis=-1)
        s0 = A["state_hgrn_bwd"][b] if odd else A["state_hgrn_fwd"][b]
        sel = np.zeros((128, 2), np.float32)
        sel[:, 1 - odd] = 1.0
        in_maps.append({
            "xT": np.ascontiguousarray(xt.T),
            "condT": np.ascontiguousarray(condT),
            "mod_w_ab": A["mod_w_ab"], "mod_w_c": A["mod_w_c"], "modb": modb, "normT": normT,
            "w_in_ab": w_in_ab_sw if odd else A["w_in_ab"],
            "w_q_up": A["w_q_up"], "w_kv_up": A["w_kv_up"], "w_out_ab": A["w_out_ab"],
            "w_in_c": A["w_in_c"], "w_out_c": A["w_out_c"],
            "vecab": vecab, "lbl": lbl_fm, "vecc": vecc,
            "ropeA": _rope_tables(64, pos), "ropeC": _rope_tables(128, pos),
            "cst": cst, "sel": sel,
            "ckvT": np.ascontiguousarray(A["cache_ckv"][b].transpose(0, 2, 1)),
            "kpeT": np.ascontiguousarray(A["cache_kpe"][b].transpose(0, 2, 1)),
            "s0": np.ascontiguousarray(s0),
            "kcT": np.ascontiguousarray(A["cache_k_c"][b].transpose(0, 2, 3, 1)),
            "vc": np.ascontiguousarray(A["cache_v_c"][b].reshape(2, 256, 512)),
        })
    res = run_bass_kernel_spmd(nc, in_maps, core_ids=list(range(8)))
    R = res.results

    y_prompt = np.zeros((16, 256, D), np.float32)
    y_sample = np.zeros((4, 2048, D), np.float32)
    new_ckv = np.zeros((16, 2, 256, 256), np.float32)
    new_kpe = np.zeros((16, 2, 256, 64), np.float32)
    new_sf = np.zeros((16, 2, 8, 128, 128), np.float32)
    new_sb = np.zeros((16, 2, 8, 128, 128), np.float32)
    new_kc = np.zeros((16, 2, 256, 4, 128), np.float32)
    new_vc = np.zeros((16, 2, 256, 4, 128), np.float32)
    for core in range(8):
        odd = core % 2
        b = core // 2
        r = R[core]
        y = r["yT"].T
        ckv = r["o_ckv"].transpose(0, 2, 1)
        kpe = r["o_kpe"].transpose(0, 2, 1)
        kc = r["o_kc"].transpose(0, 3, 1, 2)
        vcx = r["o_vc"].reshape(2, 512, 4, 128)
        st = r["o_st"]
        for sq in range(2):
            sl = slice(sq * 256, (sq + 1) * 256)
            bi = 2 * core + sq
            f = (lambda a: a[::-1]) if odd else (lambda a: a)
            y_prompt[bi] = f(y[sl])
            for j in range(2):
                new_ckv[bi, j] = f(ckv[j, sl])
                new_kpe[bi, j] = f(kpe[j, sl])
                new_kc[bi, j] = f(kc[j, sl])
                new_vc[bi, j] = f(vcx[j, sl])
                new_sf[bi, j] = st[j, 1 if odd else 0, sq]
                new_sb[bi, j] = st[j, 0 if odd else 1, sq]
        ys = y[512:1536]
        if odd:
            y_sample[b, 1024:2048] = ys[::-1]
        else:
            y_sample[b, 0:1024] = ys
    return (y_prompt, y_sample, new_ckv, new_kpe, new_sf, new_sb, new_kc, new_vc)
```

```python
import numpy as np
from contextlib import ExitStack
import concourse.bass as bass
import concourse.mybir as mybir
from concourse.bass_utils import run_bass_kernel_spmd

F32 = mybir.dt.float32
BF16 = mybir.dt.bfloat16
AF = mybir.ActivationFunctionType
ALU = mybir.AluOpType

D = 2048
NT = 1536
BS = 512
NBK = 3
NKEY = 2816
EPS = 1e-6
HC = 64
NCH = NT // HC
RG = [[0, 1], [2, 3], [4, 5], [6, 7]]
O_QL, O_KV, O_KPE, O_AG, O_BQ, O_F1, O_F2, O_BI, O_BG = 0, 512, 768, 832, 1856, 2880, 3904, 4928, 5952

EPOCH = 30000
N_DMA_SEMS = 20
SCRN = 15360


class Prog:
    COMPUTE = ("pe", "act", "dve", "pool")

    def __init__(self, nc):
        self.nc = nc
        self.streams = {e: [] for e in ("pe", "act", "dve", "pool", "sp")}
        self.count = {e: 0 for e in self.COMPUTE}
        self.known = {e: {} for e in self.streams}
        self.res = {}
        self.sem_names = set()
        self.dma_rr = {"sp": 0, "pool": 0}
        self.dma_val = {}
        self.last = {}

    def _need(self, eng, tok):
        key, val = tok
        if self.known[eng].get(key, 0) >= val:
            return
        self.known[eng][key] = val
        self.sem_names.add(key)
        self.streams[eng].append(("wait", key, val))

    def _wait_tok(self, eng, tok):
        if tok[0].startswith("pe_") and eng == "pe":
            return
        self._need(eng, tok)

    def _deps(self, eng, reads, writes):
        for r in reads:
            st = self.res.get(r)
            if st and st["w"] is not None:
                self._wait_tok(eng, st["w"])
        for w in writes:
            st = self.res.get(w)
            if st:
                if st["w"] is not None:
                    self._wait_tok(eng, st["w"])
                for t in st["r"].items():
                    self._wait_tok(eng, t)

    def _commit(self, tok, reads, writes):
        for r in reads:
            st = self.res.setdefault(r, {"w": None, "r": {}})
            st["r"][tok[0]] = max(st["r"].get(tok[0], 0), tok[1])
        for w in writes:
            self.res[w] = {"w": tok, "r": {}}
        self.last[tok[0]] = tok[1]

    def op(self, eng, fn, reads=(), writes=()):
        writes = tuple(writes) + tuple(r for r in reads if r.startswith("ps") and r not in writes)
        self._deps(eng, reads, writes)
        n = self.count[eng]
        self.count[eng] = n + 1
        key = f"{eng}_{n // EPOCH}"
        tok = (key, n % EPOCH + 1)
        self.sem_names.add(key)
        self.streams[eng].append(("op", fn, key, 1))
        self._commit(tok, reads, writes)
        return tok

    def dma(self, queue, fn, reads=(), writes=(), inc=16):
        self._deps(queue, reads, writes)
        i = self.dma_rr[queue]
        self.dma_rr[queue] = (i + 1) % N_DMA_SEMS
        key = f"d{queue}_{i}"
        prev = self.dma_val.get(key, 0)
        if prev:
            self._need(queue, (key, prev))
        val = prev + inc
        self.dma_val[key] = val
        self.sem_names.add(key)
        self.streams[queue].append(("op", fn, key, inc))
        tok = (key, val)
        self._commit(tok, reads, writes)
        return tok

    def barrier(self):
        toks = list(self.last.items())
        for eng in self.streams:
            for t in toks:
                self._need(eng, t)

    def emit(self, block, sems):
        def run(engine_obj, stream):
            for item in stream:
                if item[0] == "wait":
                    engine_obj.wait_ge(sems[item[1]], item[2])
                else:
                    _, fn, key, inc = item
                    fn(engine_obj).then_inc(sems[key], inc)

        @block.tensor
        def _(e):
            run(e, self.streams["pe"])

        @block.scalar
        def _(e):
            run(e, self.streams["act"])

        @block.vector
        def _(e):
            run(e, self.streams["dve"])

        @block.gpsimd
        def _(e):
            run(e, self.streams["pool"])

        @block.sync
        def _(e):
            run(e, self.streams["sp"])


def mm(P, out, lhsT, rhs, start, stop, reads, writes):
    return P.op("pe", lambda e, o=out, l=lhsT, r=rhs, s=start, t=stop:
                e.matmul(o, lhsT=l, rhs=r, start=s, stop=t), reads, writes)


def tr(P, out, in_, ident, reads, writes):
    return P.op("pe", lambda e, o=out, i=in_, d=ident: e.transpose(o, i, d), reads, writes)


def act(P, out, in_, func, reads, writes, bias=None, scale=None):
    kw = {}
    if bias is not None:
        kw["bias"] = bias
    if scale is not None:
        kw["scale"] = scale
    return P.op("act", lambda e, o=out, i=in_, f=func, k=kw: e.activation(out=o, in_=i, func=f, **k), reads, writes)


def tt(P, eng, out, in0, in1, op, reads, writes):
    return P.op(eng, lambda e, o=out, a=in0, b=in1, p=op: e.tensor_tensor(out=o, in0=a, in1=b, op=p), reads, writes)


def ts(P, eng, out, in0, s1, s2, op0, op1, reads, writes):
    if s2 is None:
        return P.op(eng, lambda e, o=out, a=in0, x=s1, p=op0:
                    e.tensor_single_scalar(out=o, in_=a, scalar=x, op=p), reads, writes)
    return P.op(eng, lambda e, o=out, a=in0, x=s1, y=s2, p=op0, q=op1:
                e.tensor_scalar(out=o, in0=a, scalar1=x, scalar2=y, op0=p, op1=q), reads, writes)


def stt(P, eng, out, in0, scalar, in1, op0, op1, reads, writes):
    return P.op(eng, lambda e, o=out, a=in0, s=scalar, b=in1, p=op0, q=op1:
                e.scalar_tensor_tensor(out=o, in0=a, scalar=s, in1=b, op0=p, op1=q), reads, writes)


def cp(P, eng, out, in_, reads, writes):
    if eng == "act":
        return P.op("act", lambda e, o=out, i=in_: e.copy(out=o, in_=i), reads, writes)
    return P.op(eng, lambda e, o=out, i=in_: e.tensor_copy(out=o, in_=i), reads, writes)


def dma(P, q, out, in_, reads, writes):
    return P.dma(q, lambda e, o=out, i=in_: e.dma_start(out=o, in_=i), reads, writes)


class K:
    pass


class _Stop(Exception):
    pass


import os as _os
_KSTOP = [_os.environ.get("KSTOP")]


def ckpt(name):
    if _KSTOP[0] == name:
        raise _Stop()


def build_nc():
    nc = bass.Bass("TRN2", target_bir_lowering=False)
    k = K()
    k.nc = nc

    def din(name, shape):
        return nc.dram_tensor(name, list(shape), F32, kind="ExternalInput").ap()

    def dout(name, shape):
        return nc.dram_tensor(name, list(shape), F32, kind="ExternalOutput").ap()

    def dint(name, shape, dt=F32):
        return nc.dram_tensor(name, list(shape), dt, kind="Internal").ap()

    I = {}
    I["xT"] = din("xT", [D, NT])
    I["condT"] = din("condT", [128, 16, 2])
    I["mod_w_ab"] = din("mod_w_ab", [2, D, 3 * D])
    I["mod_w_c"] = din("mod_w_c", [2, D, 3 * D])
    I["modb"] = din("modb", [4, 128, 48])
    I["normT"] = din("normT", [4, 128, 16])
    I["w_in_ab"] = din("w_in_ab", [2, D, 6976])
    I["w_q_up"] = din("w_q_up", [2, 512, 1536])
    I["w_kv_up"] = din("w_kv_up", [2, 256, 2048])
    I["w_out_ab"] = din("w_out_ab", [2, D, D])
    I["w_in_c"] = din("w_in_c", [2, D, 5120])
    I["w_out_c"] = din("w_out_c", [2, D, D])
    I["vecab"] = din("vecab", [2, 128, 16])
    I["lbl"] = din("lbl", [128, 2, 2, 8])
    I["vecc"] = din("vecc", [2, 128, 18])
    I["ropeA"] = din("ropeA", [2, 64, 1024])
    I["ropeC"] = din("ropeC", [2, 128, 1024])
    I["cst"] = din("cst", [128, 832])
    I["sel"] = din("sel", [128, 2])
    I["ckvT"] = din("ckvT", [2, 256, 256])
    I["kpeT"] = din("kpeT", [2, 64, 256])
    I["s0"] = din("s0", [2, 8, 128, 128])
    I["kcT"] = din("kcT", [2, 4, 128, 256])
    I["vc"] = din("vc", [2, 256, 512])
    O = {}
    O["yT"] = dout("yT", [D, NT])
    O["o_ckv"] = dout("o_ckv", [2, 256, 512])
    O["o_kpe"] = dout("o_kpe", [2, 64, 512])
    O["o_st"] = dout("o_st", [2, 2, 2, 8, 128, 128])
    O["o_kc"] = dout("o_kc", [2, 4, 128, 512])
    O["o_vc"] = dout("o_vc", [2, 512, 512])
    X = {}
    X["xs"] = [dint("xs0", [D, NT]), dint("xs1", [D, NT])]
    X["b_lat"] = dint("b_lat", [320, 1024])
    X["g_lat"] = dint("g_lat", [640, 1024])
    X["b_st"] = [dint(f"b_st{h}", [128, 128]) for h in range(8)]
    X["g_st"] = [dint(f"g_st{h}", [256, 128]) for h in range(8)]
    X["b_win"] = dint("b_win", [128, 1024])
    X["g_win"] = dint("g_win", [256, 1024])
    k.I, k.O, k.X = I, O, X

    with ExitStack() as es:
        def sb(name, shape, dt):
            return es.enter_context(nc.sbuf_tensor(name, list(shape), dt))

        k.HT = sb("HT", [128, 16, NT], BF16)
        k.OT = sb("OT", [128, 16, NT], BF16)
        k.W = [sb(f"W{i}", [128, 16, 256], BF16) for i in range(2)]
        k.SCR = sb("SCR", [128, SCRN], F32)
        k.MW = sb("MW", [128, 16, 128], BF16)
        k.CST = sb("CST", [128, 832], F32)
        k.CSTB = sb("CSTB", [128, 832], BF16)
        k.ONES = sb("ONES", [128, 128], BF16)
        k.ONEF = sb("ONEF", [128, NT], BF16)
        k.SEL = sb("SEL", [128, 2], F32)
        k.MOD = sb("MOD", [128, 4, 48, 2], F32)
        k.AMOD = sb("AMOD", [128, 4, 16, 2], F32)
        k.MODB = sb("MODB", [128, 4, 48], F32)
        k.NRM = sb("NRM", [128, 4, 16], F32)
        k.SC = sb("SC", [128, 16, 2], BF16)
        k.CONDF = sb("CONDF", [128, 16, 2], F32)
        k.VAB = sb("VAB", [128, 2, 16], F32)
        k.LBL = sb("LBL", [128, 2, 2, 8], F32)
        k.LB = sb("LB", [128, 2, 2, 8], F32)
        k.OML = sb("OML", [128, 2, 2, 8], F32)
        k.VC = sb("VC", [128, 2, 18], F32)
        k.ESINK = sb("ESINK", [128, 2, 16], F32)
        k.RPA = sb("RPA", [64, 2, 1024], F32)
        k.RPC = sb("RPC", [128, 2, 1024], F32)
        k.PS = [es.enter_context(nc.psum_tensor(f"ps{i}", [128, 512], F32)) for i in range(8)]
        k.PSB = k.PS[7]
        P = Prog(nc)
        k.P = P
        k.wi = 0
        k.modq = []
        k.mod_loaded = None
        k.mod_rate = 1
        try:
            program(k)
        except _Stop:
            P.barrier()
        sems = {s: es.enter_context(nc.semaphore(s)) for s in sorted(P.sem_names)}
        with nc.Block() as block:
            P.emit(block, sems)
    return nc


class Scr:
    def __init__(self, k, tag):
        self.k, self.tag, self.f = k, tag, 0

    def F(self, name, rows, *shape):
        n = int(np.prod(shape))
        ap = self.k.SCR[0:rows, self.f:self.f + n]
        self.f += n
        assert self.f <= SCRN, (self.tag, name, self.f)
        if len(shape) == 2:
            ap = ap.rearrange("p (a b) -> p a b", b=shape[1])
        elif len(shape) == 3:
            ap = ap.rearrange("p (a b c) -> p a b c", b=shape[1], c=shape[2])
        return ap

    def B(self, name, rows, *shape):
        n = int(np.prod(shape))
        nf = (n + 1) // 2
        ap = self.k.SCR[0:rows, self.f:self.f + nf].bitcast(BF16)[:, 0:n]
        self.f += nf
        assert self.f <= SCRN, (self.tag, name, self.f)
        if len(shape) == 2:
            ap = ap.rearrange("p (a b) -> p a b", b=shape[1])
        elif len(shape) == 3:
            ap = ap.rearrange("p (a b c) -> p a b c", b=shape[1], c=shape[2])
        return ap


def mod_consume(k):
    if k.mod_loaded is None:
        return
    P = k.P
    layer, n = k.mod_loaded
    for kc in range(16):
        mm(P, k.PS[7][:, 0:2], k.MW[:, kc, :], k.SC[:, kc, :], kc == 0, kc == 15, ("MW", "SC"), ("ps7",))
    ts(P, "dve", k.MOD[:, layer, n, :], k.PS[7][:, 0:2], k.MODB[:, layer, n:n + 1], None, ALU.add, None,
       ("ps7", "MODB"), (f"MOD{layer}",))
    k.mod_loaded = None


def mod_issue_load(k):
    if not k.modq:
        return
    P, I = k.P, k.I
    layer, n = k.modq.pop(0)
    wsrc = (I["mod_w_ab"] if layer % 2 == 0 else I["mod_w_c"])[layer // 2][:, n * 128:(n + 1) * 128]
    P.dma("pool", lambda e, s_=wsrc.rearrange("(c p) n -> p c n", p=128): e.dma_start(out=k.MW[:], in_=s_),
          (), ("MW",))
    k.mod_loaded = (layer, n)


def mod_step(k, times=1):
    for _ in range(times):
        mod_consume(k)
        mod_issue_load(k)


def mod_flush(k, layer):
    P = k.P
    while k.mod_loaded is not None or k.modq:
        mod_consume(k)
        mod_issue_load(k)
    ts(P, "dve", k.AMOD[:, layer], k.MOD[:, layer, 16:32, :], 1.0, None, ALU.add, None,
       (f"MOD{layer}",), (f"AMOD{layer}",))
    tt(P, "dve", k.AMOD[:, layer], k.AMOD[:, layer],
       k.NRM[:, layer, :].unsqueeze(2).broadcast_to([128, 16, 2]), ALU.mult, (f"AMOD{layer}", "NRM"),
       (f"AMOD{layer}",))


def load_w(k, src, nk, ncols):
    P = k.P
    i = k.wi
    k.wi = (i + 1) % 2
    buf = k.W[i][:, 0:nk, 0:ncols]
    rn = f"W{i}"
    P.dma("pool", lambda e, o=buf, s=src.rearrange("(c p) n -> p c n", p=128): e.dma_start(out=o, in_=s),
          reads=(), writes=(rn,))
    mod_step(k, k.mod_rate)
    return buf, rn


def rstd_from_ps(k, out, ps, n, reads, writes):
    P = k.P
    act(P, out, ps, AF.Ln, reads, writes, bias=EPS, scale=1.0 / n)
    act(P, out, out, AF.Exp, writes, writes, scale=-0.5)


def program(k):
    P, I, O, X = k.P, k.I, k.O, k.X
    dma(P, "sp", k.CST[:], I["cst"], (), ("CST",))
    P.dma("pool", lambda e: e.dma_start(out=k.CSTB[:], in_=I["cst"]), (), ("CSTB",))
    P.op("pool", lambda e: e.memset(k.ONES[:], 1.0), (), ("ONES",))
    P.op("pool", lambda e: e.memset(k.ONEF[:], 1.0), (), ("ONEF",))
    dma(P, "sp", k.SEL[:], I["sel"], (), ("SEL",))
    dma(P, "sp", k.CONDF[:], I["condT"], (), ("CONDF",))
    dma(P, "sp", k.MODB[:], I["modb"].rearrange("l p n -> p l n"), (), ("MODB",))
    dma(P, "sp", k.NRM[:], I["normT"].rearrange("l p n -> p l n"), (), ("NRM",))
    dma(P, "sp", k.VAB[:], I["vecab"].rearrange("l p n -> p l n"), (), ("VAB",))
    dma(P, "sp", k.LBL[:], I["lbl"], (), ("LBL",))
    dma(P, "sp", k.VC[:], I["vecc"].rearrange("l p n -> p l n"), (), ("VC",))
    dma(P, "sp", k.RPA[:], I["ropeA"].rearrange("l p n -> p l n"), (), ("RPA",))
    dma(P, "sp", k.RPC[:], I["ropeC"].rearrange("l p n -> p l n"), (), ("RPC",))
    act(P, k.SC[:], k.CONDF[:], AF.Silu, ("CONDF",), ("SC",))
    act(P, k.ESINK[:], k.VC[:, :, 2:18], AF.Exp, ("VC",), ("ESINK",))
    P.op("pool", lambda e: e.memset(k.LB[:, 0], 0.0), (), ("LB",))
    tt(P, "dve", k.LB[:, 1], k.LBL[:, 1], k.LBL[:, 0], ALU.subtract, ("LBL", "LB"), ("LB",))
    act(P, k.LB[:, 1], k.LB[:, 1], AF.Sigmoid, ("LB",), ("LB",))
    ts(P, "dve", k.OML[:], k.LB[:], -1.0, 1.0, ALU.mult, ALU.add, ("LB",), ("OML",))

    ckpt("const")
    modulation(k, 0)
    ckpt("mod")
    for layer in range(4):
        j = layer // 2
        if layer < 3:
            k.modq = [(layer + 1, n) for n in range(48)]
            k.mod_rate = 1 if layer % 2 == 0 else 2
        xin = I["xT"] if layer == 0 else X["xs"][(layer - 1) % 2]
        xout = O["yT"] if layer == 3 else X["xs"][layer % 2]
        norm_mod(k, layer, xin)
        ckpt(f"nm{layer}")
        if layer % 2 == 0:
            ab_layer(k, j)
            wo = I["w_out_ab"][j]
        else:
            c_layer(k, j)
            wo = I["w_out_c"][j]
        ckpt(f"mix{layer}")
        out_proj(k, layer, wo, xin, xout)
        if layer < 3:
            mod_flush(k, layer + 1)
        ckpt(f"out{layer}")
    P.barrier()


def modulation(k, layer):
    P, I = k.P, k.I
    wsrc = (I["mod_w_ab"] if layer % 2 == 0 else I["mod_w_c"])[layer // 2]
    for g in range(24):
        wb, rn = load_w(k, wsrc[:, g * 256:(g + 1) * 256], 16, 256)
        for h in range(2):
            n = g * 2 + h
            ps = k.PS[n % 2]
            pr = f"ps{n % 2}"
            for kc in range(16):
                mm(P, ps[:, 0:2], wb[:, kc, h * 128:(h + 1) * 128], k.SC[:, kc, :], kc == 0, kc == 15,
                   (rn, "SC"), (pr,))
            ts(P, "dve", k.MOD[:, layer, n, :], ps[:, 0:2], k.MODB[:, layer, n:n + 1], None, ALU.add, None,
               (pr, "MODB"), (f"MOD{layer}",))
    ts(P, "dve", k.AMOD[:, layer], k.MOD[:, layer, 16:32, :], 1.0, None, ALU.add, None,
       (f"MOD{layer}",), (f"AMOD{layer}",))
    tt(P, "dve", k.AMOD[:, layer], k.AMOD[:, layer],
       k.NRM[:, layer, :].unsqueeze(2).broadcast_to([128, 16, 2]), ALU.mult, (f"AMOD{layer}", "NRM"),
       (f"AMOD{layer}",))


def norm_mod(k, layer, xin):
    P = k.P
    P.barrier()
    s = Scr(k, "nm")
    XC = [s.F(f"xc{i}", 128, BS) for i in range(4)]
    SQ = [s.B(f"sq{i}", 128, BS) for i in range(2)]
    RS = s.F("rs", 128, BS)
    T = [s.F(f"t{i}", 128, BS) for i in range(2)]
    n = 0
    for tb in range(NBK):
        c = 0 if tb == 0 else 1
        cols = slice(tb * BS, (tb + 1) * BS)
        ps = k.PS[2 + tb % 2]
        pr = f"ps{2 + tb % 2}"
        for fc in range(16):
            xi = n % 4
            n += 1
            dma(P, "sp", XC[xi], xin[fc * 128:(fc + 1) * 128, cols], ("xin",), (f"nm_xc{xi}",))
            act(P, SQ[fc % 2], XC[xi], AF.Square, (f"nm_xc{xi}",), (f"nm_sq{fc % 2}",))
            mm(P, ps[:, :], k.ONES[:], SQ[fc % 2], fc == 0, fc == 15, ("ONES", f"nm_sq{fc % 2}"), (pr,))
        rstd_from_ps(k, RS, ps[:, :], D, (pr,), ("nm_rs",))
        for fc in range(16):
            xi = n % 4
            n += 1
            dma(P, "sp", XC[xi], xin[fc * 128:(fc + 1) * 128, cols], ("xin",), (f"nm_xc{xi}",))
            stt(P, "dve", T[fc % 2], XC[xi], k.AMOD[:, layer, fc, c:c + 1], RS, ALU.mult, ALU.mult,
                (f"nm_xc{xi}", "nm_rs", f"AMOD{layer}"), (f"nm_t{fc % 2}",))
            act(P, k.HT[:, fc, cols], T[fc % 2], AF.Identity, (f"nm_t{fc % 2}", f"MOD{layer}"), (f"HT{tb}",),
                bias=k.MOD[:, layer, fc, c:c + 1])


def out_proj(k, layer, wo, xin, xout):
    P = k.P
    P.barrier()
    s = Scr(k, "op")
    XC = [s.F(f"xc{i}", 128, BS) for i in range(4)]
    n = 0
    for g in range(8):
        wb, rn = load_w(k, wo[:, g * 256:(g + 1) * 256], 16, 256)
        for h in range(2):
            oc = g * 2 + h
            for tb in range(NBK):
                c = 0 if tb == 0 else 1
                cols = slice(tb * BS, (tb + 1) * BS)
                ps = k.PS[n % 4]
                pr = f"ps{n % 4}"
                xi = n % 4
                n += 1
                dma(P, "sp", XC[xi], xin[oc * 128:(oc + 1) * 128, cols], ("xin",), (f"op_xc{xi}",))
                for kc in range(16):
                    mm(P, ps[:, :], wb[:, kc, h * 128:(h + 1) * 128], k.OT[:, kc, cols], kc == 0, kc == 15,
                       (rn, f"OT{tb}"), (pr,))
                stt(P, "dve", XC[xi], ps[:, :], k.MOD[:, layer, 32 + oc, c:c + 1], XC[xi], ALU.mult, ALU.add,
                    (pr, f"op_xc{xi}", f"MOD{layer}"), (f"op_xc{xi}",))
                dma(P, "sp", xout[oc * 128:(oc + 1) * 128, cols], XC[xi], (f"op_xc{xi}",), ("xout",))
    P.res["xin"] = {"w": None, "r": {}}
    P.barrier()


def proj_block(k, wb, rn, c0, ncol, tb, ps, pr, nk=16, rhs=None, rres=None):
    P = k.P
    cols = slice(tb * BS, (tb + 1) * BS)
    for kc in range(nk):
        r = k.HT[:, kc, cols] if rhs is None else rhs[:, kc, cols]
        mm(P, ps[0:ncol, :], wb[:, kc, c0:c0 + ncol], r, kc == 0, kc == nk - 1,
           (rn, rres or f"HT{tb}"), (pr,))


def ab_layer(k, j):
    P, I, O, X = k.P, k.I, k.O, k.X
    P.barrier()
    s = Scr(k, "mla")
    QLN = k.OT[:, 8:12, :]
    CKV = k.OT[:, 12:16, :].rearrange("p a b -> p (a b)")[:, 0:2 * NKEY].rearrange("p (c n) -> p c n", c=2)
    KPEG = s.B("kpeg", 64, NKEY)
    KPSQ = s.B("kpsq", 64, NKEY)
    KTN = s.B("ktn", 128, NKEY)
    KTP = s.B("ktp", 64, NKEY)
    VH = s.B("vh", 128, 22, 128)
    QTN = s.B("qtn", 128, NT)
    QTP = s.B("qtp", 64, NT)
    PT = [s.B(f"pt{i}", 128, BS) for i in range(2)]
    SQb = [s.B(f"sqb{i}", 128, BS) for i in range(2)]
    TF = [s.F(f"tf{i}", 128, BS) for i in range(6)]
    RS = s.F("rs", 128, BS)
    w_in = I["w_in_ab"][j]
    RA = k.CST[0:64, 640:704]
    vab = k.VAB[:, j, :]

    def rope64(dst, x, tb, xres, dres):
        tc_ = slice((tb - 1) * BS, tb * BS)
        mm(P, k.PS[5][0:64, :], RA, x, True, True, ("CST", xres), ("ps5",))
        tt(P, "dve", TF[3][0:64, :], k.PS[5][0:64, :], k.RPA[:, 1, tc_], ALU.mult, ("ps5", "RPA"), ("tf3",))
        tt(P, "dve", x, x, k.RPA[:, 0, tc_], ALU.mult, (xres, "RPA"), (xres,))
        tt(P, "dve", dst, x, TF[3][0:64, :], ALU.add, (xres, "tf3"), (dres,))

    P.dma("pool", lambda e: e.dma_start(out=CKV[:, :, 512:768], in_=I["ckvT"][j].rearrange("(c p) n -> p c n", p=128)),
          (), ("CKVctx",))
    dma(P, "sp", TF[0][0:64, 0:256], I["kpeT"][j], (), ("tf0",))
    act(P, KPSQ[:, 512:768], TF[0][0:64, 0:256], AF.Square, ("tf0",), ("KPSQctx",))
    ts(P, "dve", KPEG[:, 512:768], TF[0][0:64, 0:256], vab[0:64, 9:10], None, ALU.mult, None, ("tf0", "VAB"),
       ("KPEGctx",))

    wq = []
    for g in range(2):
        wq.append(load_w(k, w_in[:, O_QL + g * 256:O_QL + (g + 1) * 256], 16, 256))
    for tb in range(NBK):
        cols = slice(tb * BS, (tb + 1) * BS)
        for c in range(4):
            wb, rn = wq[c // 2]
            proj_block(k, wb, rn, (c % 2) * 128, 128, tb, k.PS[c], f"ps{c}")
            cp(P, "act", TF[c], k.PS[c][:, :], (f"ps{c}",), (f"tf{c}",))
            act(P, SQb[c % 2], TF[c], AF.Square, (f"tf{c}",), (f"sqb{c % 2}",))
            mm(P, k.PS[4][:, :], k.ONES[:], SQb[c % 2], c == 0, c == 3, ("ONES", f"sqb{c % 2}"), ("ps4",))
        rstd_from_ps(k, RS, k.PS[4][:, :], 512, ("ps4",), ("rs",))
        for c in range(4):
            stt(P, "dve", QLN[:, c, cols], TF[c], vab[:, c:c + 1], RS, ALU.mult, ALU.mult,
                (f"tf{c}", "rs", "VAB"), (f"QLN{tb}",))
    wkv = load_w(k, w_in[:, O_KV:O_KV + 256], 16, 256)
    wkp = load_w(k, w_in[:, O_KPE:O_KPE + 64], 16, 64)
    for tb in range(NBK):
        cols = slice(tb * BS, (tb + 1) * BS)
        for c in range(2):
            proj_block(k, wkv[0], wkv[1], c * 128, 128, tb, k.PS[c], f"ps{c}")
            cp(P, "act", TF[c], k.PS[c][:, :], (f"ps{c}",), (f"tf{c}",))
            act(P, SQb[c % 2], TF[c], AF.Square, (f"tf{c}",), (f"sqb{c % 2}",))
            mm(P, k.PS[4][:, :], k.ONES[:], SQb[c % 2], c == 0, c == 1, ("ONES", f"sqb{c % 2}"), ("ps4",))
        rstd_from_ps(k, RS, k.PS[4][:, :], 256, ("ps4",), ("rs",))
        proj_block(k, wkp[0], wkp[1], 0, 64, tb, k.PS[2], "ps2")
        cp(P, "act", TF[4][0:64, :], k.PS[2][0:64, :], ("ps2",), ("tf4",))
        for c in range(2):
            stt(P, "dve", TF[c], TF[c], vab[:, 4 + c:5 + c], RS, ALU.mult, ALU.mult,
                (f"tf{c}", "rs", "VAB"), (f"tf{c}",))
        if tb == 0:
            for c in range(2):
                dma(P, "sp", O["o_ckv"][j, c * 128:(c + 1) * 128, :], TF[c], (f"tf{c}",), ("o_ckv",))
                cp(P, "act", CKV[:, c, 0:512], TF[c], (f"tf{c}",), ("CKVp",))
            dma(P, "sp", O["o_kpe"][j], TF[4][0:64, :], ("tf4",), ("o_kpe",))
            act(P, KPSQ[:, 0:512], TF[4][0:64, :], AF.Square, ("tf4",), ("KPSQp",))
            ts(P, "dve", KPEG[:, 0:512], TF[4][0:64, :], vab[0:64, 9:10], None, ALU.mult, None, ("tf4", "VAB"),
               ("KPEGp",))
        else:
            lc = slice((tb - 1) * BS, tb * BS)
            for c in range(2):
                dma(P, "sp", X["b_lat"][c * 128:(c + 1) * 128, lc], TF[c], (f"tf{c}",), ("b_lat",))
            dma(P, "sp", X["b_win"][0:64, lc], TF[4][0:64, :], ("tf4",), ("b_win",))
            ts(P, "dve", TF[5][0:64, :], TF[4][0:64, :], vab[0:64, 9:10], None, ALU.mult, None, ("tf4", "VAB"),
               ("tf5",))
            rope64(TF[2][0:64, :], TF[5][0:64, :], tb, "tf5", "tf2")
            dma(P, "sp", X["b_lat"][256:320, lc], TF[2][0:64, :], ("tf2",), ("b_lat",))
    P.dma("pool", lambda e: e.collective_compute("AllGather", ALU.bypass, replica_groups=RG,
                                                 ins=[X["b_lat"].opt()], outs=[X["g_lat"].opt()]),
          ("b_lat",), ("g_lat",), inc=1)
    P.dma("pool", lambda e: e.collective_compute("AllGather", ALU.bypass, replica_groups=RG,
                                                 ins=[X["b_win"].opt()], outs=[X["g_win"].opt()]),
          ("b_win",), ("g_win",), inc=1)
    for r in range(2):
        kc_ = slice(768 + r * 1024, 768 + (r + 1) * 1024)
        P.dma("pool", lambda e, r=r, kc_=kc_: e.dma_start(
            out=CKV[:, :, kc_], in_=X["g_lat"][r * 320:r * 320 + 256, :].rearrange("(c p) n -> p c n", p=128)),
            ("g_lat",), (f"CKVr{r}",))
        P.dma("pool", lambda e, r=r, kc_=kc_: e.dma_start(out=KPEG[:, kc_], in_=X["g_lat"][r * 320 + 256:r * 320 + 320, :]),
              ("g_lat",), (f"KPEGr{r}",))
        for hb in range(2):
            lc = slice(hb * BS, (hb + 1) * BS)
            kq = slice(768 + r * 1024 + hb * BS, 768 + r * 1024 + (hb + 1) * BS)
            dma(P, "sp", TF[3][0:64, :], X["g_win"][r * 128:r * 128 + 64, lc], ("g_win",), ("tf3",))
            act(P, KPSQ[:, kq], TF[3][0:64, :], AF.Square, ("tf3",), (f"KPSQr{r}",))
    ckpt(f"lat{j}")
    KRES = ("CKVctx", "CKVp", "CKVr0", "CKVr1")
    PRES = ("KPEGctx", "KPEGp", "KPEGr0", "KPEGr1")
    SRES = ("KPSQctx", "KPSQp", "KPSQr0", "KPSQr1")

    KB = [(i * 512, min(512, NKEY - i * 512)) for i in range(6)]
    for h in range(8):
        wqn = load_w(k, I["w_q_up"][j][:, h * 192:(h + 1) * 192], 4, 192)
        wkv_ = load_w(k, I["w_kv_up"][j][:, h * 256:(h + 1) * 256], 2, 256)
        for tb in range(NBK):
            cols = slice(tb * BS, (tb + 1) * BS)
            proj_block(k, wqn[0], wqn[1], 0, 128, tb, k.PS[0], "ps0", nk=4, rhs=QLN, rres=f"QLN{tb}")
            proj_block(k, wqn[0], wqn[1], 128, 64, tb, k.PS[1], "ps1", nk=4, rhs=QLN, rres=f"QLN{tb}")
            cp(P, "act", TF[0], k.PS[0][:, :], ("ps0",), ("tf0",))
            cp(P, "act", TF[1][0:64, :], k.PS[1][0:64, :], ("ps1",), ("tf1",))
            act(P, SQb[0], TF[0], AF.Square, ("tf0",), ("sqb0",))
            act(P, SQb[1][0:64, :], TF[1][0:64, :], AF.Square, ("tf1",), ("sqb1",))
            mm(P, k.PS[4][:, :], k.ONES[:], SQb[0], True, False, ("ONES", "sqb0"), ("ps4",))
            mm(P, k.PS[4][:, :], k.ONES[0:64, :], SQb[1][0:64, :], False, True, ("ONES", "sqb1"), ("ps4",))
            rstd_from_ps(k, RS, k.PS[4][:, :], 192, ("ps4",), ("rs",))
            stt(P, "dve", QTN[:, cols], TF[0], vab[:, 6:7], RS, ALU.mult, ALU.mult, ("tf0", "rs", "VAB"),
                (f"QTN{tb}",))
            stt(P, "dve", TF[2][0:64, :], TF[1][0:64, :], vab[0:64, 7:8], RS[0:64, :], ALU.mult, ALU.mult,
                ("tf1", "rs", "VAB"), ("tf2",))
            if tb == 0:
                cp(P, "dve", QTP[:, cols], TF[2][0:64, :], ("tf2",), (f"QTP{tb}",))
            else:
                rope64(QTP[:, cols], TF[2][0:64, :], tb, "tf2", f"QTP{tb}")
        for bi, (c0, n) in enumerate(KB):
            kc_ = slice(c0, c0 + n)
            ps = k.PS[bi % 2]
            pr = f"ps{bi % 2}"
            for c in range(2):
                mm(P, ps[:, 0:n], wkv_[0][:, c, 0:128], CKV[:, c, kc_], c == 0, c == 1, (wkv_[1],) + KRES, (pr,))
            cp(P, "act", TF[bi % 2][:, 0:n], ps[:, 0:n], (pr,), (f"tf{bi % 2}",))
            act(P, SQb[bi % 2][:, 0:n], TF[bi % 2][:, 0:n], AF.Square, (f"tf{bi % 2}",), (f"sqb{bi % 2}",))
            ps2 = k.PS[2 + bi % 2]
            pr2 = f"ps{2 + bi % 2}"
            mm(P, ps2[:, 0:n], k.ONES[:], SQb[bi % 2][:, 0:n], True, False, ("ONES", f"sqb{bi % 2}"), (pr2,))
            mm(P, ps2[:, 0:n], k.ONES[0:64, :], KPSQ[:, kc_], False, True, ("ONES",) + SRES, (pr2,))
            rstd_from_ps(k, RS[:, 0:n], ps2[:, 0:n], 192, (pr2,), ("rs",))
            stt(P, "dve", KTN[:, kc_], TF[bi % 2][:, 0:n], vab[:, 8:9], RS[:, 0:n], ALU.mult, ALU.mult,
                (f"tf{bi % 2}", "rs", "VAB"), ("KTN",))
            tt(P, "dve", KTP[:, kc_], KPEG[:, kc_], RS[0:64, 0:n], ALU.mult, PRES + ("rs",), ("KTP",))
        for g in range(6):
            nt_ = min(4, 22 - g * 4)
            ps = k.PS[g % 2]
            pr = f"ps{g % 2}"
            for t in range(nt_):
                kt = g * 4 + t
                for c in range(2):
                    mm(P, ps[:, t * 128:(t + 1) * 128], CKV[:, c, kt * 128:(kt + 1) * 128], wkv_[0][:, c, 128:256],
                       c == 0, c == 1, (wkv_[1],) + KRES, (pr,))
            cp(P, "act", VH[:, g * 4:g * 4 + nt_, :], ps[:, 0:nt_ * 128].rearrange("p (a b) -> p a b", b=128),
               (pr,), ("VH",))
        wg = load_w(k, w_in[:, O_AG + h * 128:O_AG + (h + 1) * 128], 16, 128)
        groups = [(0, 256, [0, 1]), (256, 256, [2, 3]), (512, 512, list(range(4, 22))),
                  (1024, 512, list(range(4, 22)))]
        for gi, (q0, qn, kts) in enumerate(groups):
            qs = slice(q0, q0 + qn)
            tb = q0 // BS
            Ops, Dps = k.PS[4], k.PS[5]
            def s_mm(ki):
                ks = slice(kts[ki] * 128, (kts[ki] + 1) * 128)
                sp_ = k.PS[ki % 2]
                spr = f"ps{ki % 2}"
                mm(P, sp_[:, 0:qn], KTN[:, ks], QTN[:, qs], True, False, ("KTN", f"QTN{tb}"), (spr,))
                mm(P, sp_[:, 0:qn], KTP[:, ks], QTP[:, qs], False, True, ("KTP", f"QTP{tb}"), (spr,))

            s_mm(0)
            for ki, kt in enumerate(kts):
                sp_ = k.PS[ki % 2]
                spr = f"ps{ki % 2}"
                if ki + 1 < len(kts):
                    s_mm(ki + 1)
                act(P, PT[ki % 2][:, 0:qn], sp_[:, 0:qn], AF.Exp, (spr,), (f"pt{ki % 2}",), scale=192 ** -0.5)
                mm(P, Ops[:, 0:qn], VH[:, kt, :], PT[ki % 2][:, 0:qn], ki == 0, ki == len(kts) - 1,
                   ("VH", f"pt{ki % 2}"), ("ps4",))
                mm(P, Dps[:, 0:qn], k.ONES[:], PT[ki % 2][:, 0:qn], ki == 0, ki == len(kts) - 1,
                   ("ONES", f"pt{ki % 2}"), ("ps5",))
            act(P, TF[4][:, 0:qn], Dps[:, 0:qn], AF.Ln, ("ps5",), ("tf4",))
            act(P, TF[4][:, 0:qn], TF[4][:, 0:qn], AF.Exp, ("tf4",), ("tf4",), scale=-1.0)
            tt(P, "dve", TF[5][:, 0:qn], Ops[:, 0:qn], TF[4][:, 0:qn], ALU.mult, ("ps4", "tf4"), ("tf5",))
            gp = k.PS[2 + gi % 2]
            gpr = f"ps{2 + gi % 2}"
            for kc in range(16):
                mm(P, gp[:, 0:qn], wg[0][:, kc, :], k.HT[:, kc, qs], kc == 0, kc == 15, (wg[1], f"HT{tb}"), (gpr,))
            act(P, TF[3][:, 0:qn], gp[:, 0:qn], AF.Silu, (gpr,), ("tf3",))
            tt(P, "dve", k.OT[:, h, qs], TF[5][:, 0:qn], TF[3][:, 0:qn], ALU.mult, ("tf5", "tf3"), (f"OT{tb}",))
    ckpt(f"mla{j}")
    hgrn(k, j)


def hgrn(k, j):
    P, I, O, X = k.P, k.I, k.O, k.X
    P.barrier()
    s = Scr(k, "hg")
    T1 = s.F("t1", 128, NT)
    T2 = s.F("t2", 128, NT)
    T3 = s.F("t3", 128, NT)
    T4 = s.F("t4", 128, NT)
    OPH = s.F("oph", 128, NT)
    OAC = OPH
    CS = s.F("cs", 128, 6, NCH)
    ST = s.F("st", 128, 128)
    ST1 = s.F("st1", 128, 128)
    GST = s.F("gst", 128, 2, 128)
    Qb = s.B("q", 128, NT)
    Kb = s.B("kb", 128, NT)
    QTl = s.B("qtl", 128, NT)
    KTl = s.B("ktl", 128, NT)
    KTM = s.B("ktm", 128, 12, 128)
    Gb = s.B("g", 128, NT)
    Vb = s.B("v", 128, 12, 128)
    AT = s.B("at", 128, NT // 2)
    STb = s.B("stb", 128, 128)
    SQb = s.B("sqb", 128, BS)
    w_in = I["w_in_ab"][j]
    identb = k.CSTB[:, 0:128]
    vab = k.VAB[:, j, :]
    SEGS = [(0, 256), (256, 256), (512, 1024)]

    for h in range(8):
        wqg = load_w(k, w_in[:, O_BQ + h * 128:O_BQ + (h + 1) * 128], 16, 128)
        for tb in range(NBK):
            cols = slice(tb * BS, (tb + 1) * BS)
            proj_block(k, wqg[0], wqg[1], 0, 128, tb, k.PS[tb % 2], f"ps{tb % 2}")
            act(P, Qb[:, cols], k.PS[tb % 2][:, :], AF.Silu, (f"ps{tb % 2}",), ("hq",))
        wv = load_w(k, w_in[:, O_BI + h * 128:O_BI + (h + 1) * 128], 16, 128)
        for g in range(3):
            ps = k.PS[g % 2]
            pr = f"ps{g % 2}"
            for t in range(4):
                tl = g * 4 + t
                for kc in range(16):
                    mm(P, ps[:, t * 128:(t + 1) * 128], k.HT[:, kc, tl * 128:(tl + 1) * 128], wv[0][:, kc, :],
                       kc == 0, kc == 15, (wv[1], f"HT{tl // 4}"), (pr,))
            cp(P, "act", Vb[:, g * 4:(g + 1) * 4, :], ps[:, :].rearrange("p (a b) -> p a b", b=128), (pr,), ("hv",))

        for ph in range(2):
            wf = load_w(k, w_in[:, (O_F1, O_F2)[ph] + h * 128:(O_F1, O_F2)[ph] + (h + 1) * 128], 16, 128)
            lb = k.LB[:, j, ph, h:h + 1]
            oml = k.OML[:, j, ph, h:h + 1]
            for tb in range(NBK):
                cols = slice(tb * BS, (tb + 1) * BS)
                proj_block(k, wf[0], wf[1], 0, 128, tb, k.PS[tb % 2], f"ps{tb % 2}")
                act(P, T1[:, cols], k.PS[tb % 2][:, :], AF.Sigmoid, (f"ps{tb % 2}",), ("t1",))
            if ph == 0:
                wgg = load_w(k, w_in[:, O_BG + h * 128:O_BG + (h + 1) * 128], 16, 128)
                for tb in range(2):
                    proj_block(k, wgg[0], wgg[1], 0, 128, tb, k.PS[2 + tb % 2], f"ps{2 + tb % 2}")
            ts(P, "dve", T1, T1, oml, lb, ALU.mult, ALU.add, ("t1", "LB", "OML"), ("t1",))
            act(P, T2, T1, AF.Ln, ("t1",), ("t2",))
            if ph == 0:
                for tb in range(2):
                    act(P, Gb[:, tb * BS:(tb + 1) * BS], k.PS[2 + tb % 2][:, :], AF.Silu, (f"ps{2 + tb % 2}",), ("hg",))
                proj_block(k, wgg[0], wgg[1], 0, 128, 2, k.PS[2], "ps2")
                act(P, Gb[:, 2 * BS:3 * BS], k.PS[2][:, :], AF.Silu, ("ps2",), ("hg",))
            act(P, Kb, T1, AF.Identity, ("t1",), ("hk",), bias=1.0, scale=-1.0)
            P.op("dve", lambda e: e.tensor_tensor_scan(out=T1, data0=k.ONEF[:], data1=T2, initial=0.0,
                                                       op0=ALU.mult, op1=ALU.add), ("t2", "ONEF", "hk"), ("t1",))
            tt(P, "dve", T2, T1, T2, ALU.subtract, ("t1", "t2"), ("t2",))
            Bv = T1.rearrange("p (c t) -> p c t", t=HC)
            Xv = T2.rearrange("p (c t) -> p c t", t=HC)
            lo, hi, mid, din_, d2, d1 = (CS[:, i, :] for i in range(6))
            cp(P, "dve", lo, Xv[:, :, 0], ("t2",), ("cs",))
            cp(P, "dve", hi, Bv[:, :, HC - 1], ("t1", "cs"), ("cs",))
            if ph == 0:
                cp(P, "dve", mid, Bv[:, :, HC // 2], ("t1", "cs"), ("cs",))
                tt(P, "dve", T3.rearrange("p (c t) -> p c t", t=HC), Bv,
                   mid.unsqueeze(2).broadcast_to([128, NCH, HC]), ALU.subtract, ("t1", "cs"), ("t3",))
                tt(P, "dve", din_, mid, lo, ALU.subtract, ("cs",), ("cs",))
                tt(P, "dve", d2, hi, mid, ALU.subtract, ("cs",), ("cs",))
            else:
                cp(P, "dve", mid, Xv[:, :, HC // 2], ("t2", "cs"), ("cs",))
                tt(P, "dve", T3.rearrange("p (c t) -> p c t", t=HC),
                   mid.unsqueeze(2).broadcast_to([128, NCH, HC]), Xv, ALU.subtract, ("t2", "cs"), ("t3",))
                tt(P, "dve", din_, hi, mid, ALU.subtract, ("cs",), ("cs",))
                tt(P, "dve", d2, mid, lo, ALU.subtract, ("cs",), ("cs",))
            tt(P, "dve", d1, hi, lo, ALU.subtract, ("cs",), ("cs",))
            act(P, CS[:, 3:6, :], CS[:, 3:6, :], AF.Exp, ("cs",), ("cs",))
            act(P, T4, T3, AF.Exp, ("t3",), ("t4",))
            tt(P, "dve", QTl, Qb, T4, ALU.mult, ("hq", "t4"), ("qtl",))
            act(P, T4, T3, AF.Exp, ("t3", "qtl"), ("t4",), scale=-1.0)
            tt(P, "dve", KTl, Kb, T4, ALU.mult, ("hk", "t4"), ("ktl",))
            for tl in range(12):
                psb = k.PS[6][:, :].bitcast(BF16)
                o = psb[:, (tl % 4) * 128:(tl % 4 + 1) * 128]
                tr(P, o, KTl[:, tl * 128:(tl + 1) * 128], identb, ("ktl", "CSTB"), ("ps6",))
                if tl % 4 == 3:
                    cp(P, "act", KTM[:, tl - 3:tl + 1, :], psb[:, 0:512].rearrange("p (a b) -> p a b", b=128),
                       ("ps6",), ("ktm",))
            CPT = 128 // HC
            for g in range(3):
                ps = k.PS[g % 2]
                pr = f"ps{g % 2}"
                for t in range(4):
                    tl = g * 4 + t
                    for ci in range(CPT):
                        c0 = tl * 128 + ci * HC
                        mm(P, ps[ci * HC:(ci + 1) * HC, t * HC:(t + 1) * HC], KTl[:, c0:c0 + HC], QTl[:, c0:c0 + HC],
                           True, True, ("ktl", "qtl"), (pr,))
                mask = k.CST[:, 704 + ph * HC:704 + (ph + 1) * HC].bitcast(mybir.dt.uint32)
                atg = AT[:, g * 4 * HC:(g + 1) * 4 * HC]
                P.op("dve", lambda e, o=atg: e.memset(o, 0.0), (), ("at",))
                P.op("dve", lambda e, o=atg.rearrange("p (a b) -> p a b", b=HC),
                     d=ps[:, 0:4 * HC].rearrange("p (a b) -> p a b", b=HC),
                     m=mask.unsqueeze(1).broadcast_to([128, 4, HC]): e.copy_predicated(out=o, mask=m, data=d),
                     (pr, "CST", "at"), ("at",))
            DSS = T4[:, 0:1024].rearrange("p (r c v) -> p r c v", r=2, c=4)
            STBA = T3.bitcast(BF16).rearrange("p (c v) -> p c v", v=128)
            seg_groups = {0: [0], 1: [1], 2: [2, 3, 4, 5]}
            sorder = [2, 0, 1] if ph == 0 else [0, 1, 2]
            glist = []
            for si in sorder:
                gs = seg_groups[si] if ph == 0 else seg_groups[si][::-1]
                glist += [(si, g) for g in gs]

            def emit_dS(n, g):
                slot = n % 2
                for ci in range(CPT):
                    ps = k.PS[4 + 2 * slot + ci]
                    pr = f"ps{4 + 2 * slot + ci}"
                    for a_ in range(2):
                        c = g * 4 + a_ * 2 + ci
                        tl = c // CPT
                        mm(P, ps[:, a_ * 128:(a_ + 1) * 128], KTM[ci * HC:(ci + 1) * HC, tl, :],
                           Vb[ci * HC:(ci + 1) * HC, tl, :], True, True, ("ktm", "hv"), (pr,))
                for ci in range(CPT):
                    ps = k.PS[4 + 2 * slot + ci]
                    pr = f"ps{4 + 2 * slot + ci}"
                    tt(P, "dve", DSS[:, slot].rearrange("p (a b) v -> p a b v", b=2)[:, :, ci, :],
                       ps[:, 0:256].rearrange("p (c v) -> p c v", v=128),
                       d2[:, g * 4:(g + 1) * 4].rearrange("p (a b) -> p a b", b=2)[:, :, ci].unsqueeze(2)
                       .broadcast_to([128, 2, 128]), ALU.mult, (pr, "cs"), (f"dss{slot}", "t4"))

            def chain_group(n, g):
                slot = n % 2
                cs_ = list(range(g * 4, g * 4 + 4))
                if ph == 1:
                    cs_ = cs_[::-1]
                for c in cs_:
                    q = c - g * 4
                    ts(P, "dve", STBA[:, c, :], ST, din_[:, c:c + 1], None, ALU.mult, None, ("st", "cs"), ("t3",))
                    stt(P, "dve", ST, ST, d1[:, c:c + 1], DSS[:, slot, q, :], ALU.mult, ALU.add,
                        ("st", "cs", f"dss{slot}", "t4"), ("st",))

            emitted = [0]

            def ensure_dS(upto):
                while emitted[0] <= min(upto, len(glist) - 1):
                    emit_dS(emitted[0], glist[emitted[0]][1])
                    emitted[0] += 1

            ensure_dS(1)
            for n, (si, g) in enumerate(glist):
                first = (n == 0 or glist[n - 1][0] != si)
                last = (n == len(glist) - 1 or glist[n + 1][0] != si)
                if first:
                    if si < 2:
                        P.op("dve", lambda e: e.memset(ST, 0.0), ("st",), ("st",))
                    elif ph == 0:
                        dma(P, "sp", ST, I["s0"][j, h], ("st",), ("st",))
                    else:
                        dma(P, "sp", GST, X["g_st"][h].rearrange("(r p) n -> p r n", r=2), ("g_st", "gst"), ("gst",))
                        ts(P, "dve", ST1, GST[:, 0, :], k.SEL[:, 0:1], None, ALU.mult, None, ("gst", "SEL", "st1"),
                           ("st1",))
                        stt(P, "dve", ST, GST[:, 1, :], k.SEL[:, 1:2], ST1, ALU.mult, ALU.add,
                            ("gst", "st1", "SEL", "st"), ("st",))
                chain_group(n, g)
                ensure_dS(n + 2)
                if last:
                    if si < 2:
                        dma(P, "sp", O["o_st"][j, ph, si, h], ST, ("st",), ("o_st",))
                    elif ph == 0:
                        dma(P, "sp", X["b_st"][h], ST, ("st",), (f"b_st{h}",))
                        P.dma("pool", lambda e, h=h: e.collective_compute(
                            "AllGather", ALU.bypass, replica_groups=RG, ins=[X["b_st"][h].opt()],
                            outs=[X["g_st"][h].opt()]), (f"b_st{h}",), ("g_st",), inc=1)
            for bi, (b0, bl) in enumerate([(0, 256), (256, 256), (512, 512), (1024, 512)]):
                ops_b = k.PS[2 + bi % 2]
                opr_b = f"ps{2 + bi % 2}"
                for c in range(b0 // HC, (b0 + bl) // HC):
                    tl, ci = c // CPT, c % CPT
                    cc = slice(c * HC, (c + 1) * HC)
                    oc = slice(c * HC - b0, c * HC - b0 + HC)
                    mm(P, ops_b[:, oc], Vb[ci * HC:(ci + 1) * HC, tl, :], AT[ci * HC:(ci + 1) * HC, tl * HC:(tl + 1) * HC],
                       True, False, ("hv", "at"), (opr_b,))
                    mm(P, ops_b[:, oc], STBA[:, c, :], QTl[:, cc], False, True, ("t3", "qtl"), (opr_b,))
                if ph == 0:
                    cp(P, "act", OPH[:, b0:b0 + bl], ops_b[:, 0:bl], (opr_b,), ("oph",))
                else:
                    tt(P, "dve", OPH[:, b0:b0 + bl], ops_b[:, 0:bl], OPH[:, b0:b0 + bl], ALU.add,
                       (opr_b, "oph"), ("oph",))
        for tb in range(NBK):
            cols = slice(tb * BS, (tb + 1) * BS)
            act(P, SQb, OAC[:, cols], AF.Square, ("oph",), ("hsq",))
            mm(P, k.PS[0][:, :], k.ONES[:], SQb, True, True, ("ONES", "hsq"), ("ps0",))
            rstd_from_ps(k, T3[:, 0:BS], k.PS[0][:, :], 128, ("ps0",), ("t3",))
            stt(P, "dve", T4[:, 0:BS], OAC[:, cols], vab[:, 10:11], T3[:, 0:BS], ALU.mult, ALU.mult,
                ("oph", "t3", "VAB"), ("t4",))
            tt(P, "dve", k.OT[:, 8 + h, cols], T4[:, 0:BS], Gb[:, cols], ALU.mult, ("t4", "hg"), (f"OT{tb}",))


def c_layer(k, j):
    P, I, O, X = k.P, k.I, k.O, k.X
    P.barrier()
    s = Scr(k, "c")
    TFB = s.F("tfb", 128, 6, BS)
    TF = [TFB[:, i, :] for i in range(6)]
    GW = TFB[:, 0:4, :].rearrange("p a b -> p (a b)").rearrange("p (r n) -> p r n", r=2)
    GWR = ("tf0", "tf1", "tf2", "tf3")
    RS = s.F("rs", 128, BS)
    KT = s.B("kt", 128, 4, NT)
    VT = s.B("vt", 128, 12, 512)
    QT = s.B("qt", 128, 4, NT)
    KC = s.B("kc", 128, 4, 256)
    VCx = s.B("vcx", 128, 2, 512)
    KB_ = s.B("kbnd", 128, 4, 128)
    VB_ = s.B("vbnd", 128, 512)
    PT = [s.B(f"pt{i}", 128, BS) for i in range(2)]
    SQb = [s.B(f"sqb{i}", 128, BS) for i in range(2)]
    w_in = I["w_in_c"][j]
    RC = k.CST[:, 128:256]
    vc = k.VC[:, j, :]
    Mprev, Mnext, Manti = (k.CSTB[:, 256 + i * 128:256 + (i + 1) * 128] for i in range(3))
    scale = 128 ** -0.5

    P.dma("pool", lambda e: e.dma_start(out=KC[:], in_=I["kcT"][j].rearrange("g p n -> p g n")), (), ("KC",))
    P.dma("pool", lambda e: e.dma_start(out=VCx[:], in_=I["vc"][j].rearrange("(t p) n -> p t n", p=128)), (), ("VCx",))

    par_ctr = [0]

    def head_fm(gcol, out_bf, tb, outres, prompt_out=None):
        par = par_ctr[0]
        par_ctr[0] ^= 1
        A0, A1, A2 = TF[3 * par], TF[3 * par + 1], TF[3 * par + 2]
        n0, n1, n2 = f"tf{3 * par}", f"tf{3 * par + 1}", f"tf{3 * par + 2}"
        psp, sqn = f"ps{par}", f"sqb{par}"
        pst, prt = k.PS[4 + 2 * par], k.PS[5 + 2 * par]
        pstn, prtn = f"ps{4 + 2 * par}", f"ps{5 + 2 * par}"
        cp(P, "act", A0, k.PS[par][:, :], (psp,), (n0,))
        act(P, SQb[par], A0, AF.Square, (n0,), (sqn,))
        mm(P, pst[:, :], k.ONES[:], SQb[par], True, True, ("ONES", sqn), (pstn,))
        rstd_from_ps(k, A2, pst[:, :], 128, (pstn,), (n2,))
        stt(P, "dve", A1, A0, vc[:, gcol:gcol + 1], A2, ALU.mult, ALU.mult, (n0, n2, "VC"), (n1,))
        if tb == 0:
            if prompt_out is not None:
                dma(P, "sp", prompt_out, A1, (n1,), ("o_kc",))
            cp(P, "dve", out_bf, A1, (n1,), (outres,))
            return
        tc_ = slice((tb - 1) * BS, tb * BS)
        mm(P, prt[:, :], RC, A1, True, True, ("CST", n1), (prtn,))
        tt(P, "dve", A2, prt[:, :], k.RPC[:, 1, tc_], ALU.mult, (prtn, "RPC"), (n2,))
        tt(P, "dve", A1, A1, k.RPC[:, 0, tc_], ALU.mult, (n1, "RPC"), (n1,))
        tt(P, "dve", out_bf, A1, A2, ALU.add, (n1, n2), (outres,))

    def cur_ps():
        return k.PS[par_ctr[0]], f"ps{par_ctr[0]}"

    for g in range(2):
        wk = load_w(k, w_in[:, 2048 + g * 256:2048 + (g + 1) * 256], 16, 256)
        for hh in range(2):
            kh = g * 2 + hh
            for tb in range(NBK):
                proj_block(k, wk[0], wk[1], hh * 128, 128, tb, *cur_ps())
                head_fm(1, KT[:, kh, tb * BS:(tb + 1) * BS], tb, "KT",
                        prompt_out=O["o_kc"][j, kh] if tb == 0 else None)
    ckpt(f"c_k{j}")
    wvs = [load_w(k, w_in[:, 2560 + g * 256:2560 + (g + 1) * 256], 16, 256) for g in range(2)]
    for tl in range(12):
        ps = k.PS[tl % 2]
        pr = f"ps{tl % 2}"
        for g in range(2):
            for kc in range(16):
                mm(P, ps[:, g * 256:(g + 1) * 256], k.HT[:, kc, tl * 128:(tl + 1) * 128], wvs[g][0][:, kc, :],
                   kc == 0, kc == 15, (wvs[g][1], f"HT{tl // 4}"), (pr,))
        cp(P, "act", VT[:, tl, :], ps[:, :], (pr,), ("VT",))
        if tl < 4:
            cp(P, "dve", TF[4 + tl % 2], ps[:, :], (pr,), (f"tf{4 + tl % 2}",))
            dma(P, "sp", O["o_vc"][j, tl * 128:(tl + 1) * 128, :], TF[4 + tl % 2], (f"tf{4 + tl % 2}",), ("o_vc",))
    ckpt(f"c_v{j}")
    cp(P, "dve", GW[:, 0, 0:512].rearrange("p (a b) -> p a b", b=128), KT[:, :, NT - 128:NT], ("KT",) + GWR, GWR)
    cp(P, "dve", GW[:, 0, 512:1024], VT[:, 11, :], ("VT",) + GWR, GWR)
    dma(P, "sp", X["b_win"], GW[:, 0, :], GWR, ("b_win",))
    P.dma("pool", lambda e: e.collective_compute("AllGather", ALU.bypass, replica_groups=RG,
                                                 ins=[X["b_win"].opt()], outs=[X["g_win"].opt()]),
          ("b_win",), ("g_win",), inc=1)
    dma(P, "sp", GW, X["g_win"].rearrange("(r p) n -> p r n", r=2), ("g_win",) + GWR, GWR)
    ts(P, "dve", GW[:, 0, :], GW[:, 0, :], k.SEL[:, 0:1], None, ALU.mult, None, GWR + ("SEL",), GWR)
    stt(P, "dve", GW[:, 0, :], GW[:, 1, :], k.SEL[:, 1:2], GW[:, 0, :], ALU.mult, ALU.add, GWR + ("SEL",), GWR)
    cp(P, "dve", KB_, GW[:, 0, 0:512].rearrange("p (a b) -> p a b", b=128), GWR, ("KB",))
    cp(P, "dve", VB_, GW[:, 0, 512:1024], GWR, ("VB",))

    ckpt(f"c_kv{j}")
    for g in range(4):
        for pair in range(2):
            wq = load_w(k, w_in[:, g * 512 + pair * 256:g * 512 + (pair + 1) * 256], 16, 256)
            for hh in range(2):
                qh = pair * 2 + hh
                for tb in range(NBK):
                    cols = slice(tb * BS, (tb + 1) * BS)
                    proj_block(k, wq[0], wq[1], hh * 128, 128, tb, *cur_ps())
                    head_fm(0, QT[:, qh, cols], tb, f"QT{tb}")
        ckpt(f"c_q{j}_{g}")
        blocks = []
        for sq in range(2):
            for qb in range(2):
                q0 = sq * 256 + qb * 128
                keys = [(KT[:, g, sq * 256 + t * 128:sq * 256 + (t + 1) * 128], VT[:, sq * 2 + t, g * 128:(g + 1) * 128],
                         None, ("KT", "VT")) for t in range(2)]
                blocks.append((q0, keys))
        for qb in range(8):
            q0 = 512 + qb * 128
            keys = [(KC[:, g, t * 128:(t + 1) * 128], VCx[:, t, g * 128:(g + 1) * 128], None, ("KC", "VCx"))
                    for t in range(2)]
            if qb > 0:
                keys.append((KT[:, g, q0 - 128:q0], VT[:, 4 + qb - 1, g * 128:(g + 1) * 128], Mprev, ("KT", "VT")))
            keys.append((KT[:, g, q0:q0 + 128], VT[:, 4 + qb, g * 128:(g + 1) * 128], None, ("KT", "VT")))
            if qb < 7:
                keys.append((KT[:, g, q0 + 128:q0 + 256], VT[:, 4 + qb + 1, g * 128:(g + 1) * 128], Mnext, ("KT", "VT")))
            else:
                keys.append((KB_[:, g, :], VB_[:, g * 128:(g + 1) * 128], Manti, ("KB", "VB")))
            blocks.append((q0, keys))
        for bi, (q0, keys) in enumerate(blocks):
            tb = q0 // BS
            qs = slice(q0, q0 + 128)
            Ops, Dps = k.PS[4 + 2 * (bi % 2)], k.PS[5 + 2 * (bi % 2)]
            opr, dpr = f"ps{4 + 2 * (bi % 2)}", f"ps{5 + 2 * (bi % 2)}"
            def s_mm(ki):
                kap_, _, _, kres_ = keys[ki]
                mm(P, k.PS[2 + ki % 2][:, :].rearrange("p (a b) -> p a b", b=128), kap_, QT[:, :, qs], True, True,
                   kres_ + (f"QT{tb}",), (f"ps{2 + ki % 2}",))

            s_mm(0)
            for ki, (kap, vap, mask, kres) in enumerate(keys):
                sp_ = k.PS[2 + ki % 2]
                spr = f"ps{2 + ki % 2}"
                if ki + 1 < len(keys):
                    s_mm(ki + 1)
                act(P, PT[ki % 2], sp_[:, :], AF.Exp, (spr,), (f"pt{ki % 2}",), scale=scale)
                if mask is not None:
                    tt(P, "dve", PT[ki % 2].rearrange("p (a b) -> p a b", b=128),
                       PT[ki % 2].rearrange("p (a b) -> p a b", b=128),
                       mask.unsqueeze(1).broadcast_to([128, 4, 128]), ALU.mult, (f"pt{ki % 2}", "CSTB"),
                       (f"pt{ki % 2}",))
                mm(P, Ops[:, :], vap, PT[ki % 2], ki == 0, ki == len(keys) - 1, kres + (f"pt{ki % 2}",), (opr,))
                mm(P, Dps[:, :], k.ONES[:], PT[ki % 2], ki == 0, ki == len(keys) - 1, ("ONES", f"pt{ki % 2}"), (dpr,))
            tt(P, "dve", TF[4].rearrange("p (a b) -> p a b", b=128), Dps[:, :].rearrange("p (a b) -> p a b", b=128),
               k.ESINK[:, j, g * 4:(g + 1) * 4].unsqueeze(2).broadcast_to([128, 4, 128]), ALU.add,
               (dpr, "ESINK"), ("tf4",))
            act(P, TF[4], TF[4], AF.Ln, ("tf4",), ("tf4",))
            act(P, TF[4], TF[4], AF.Exp, ("tf4",), ("tf4",), scale=-1.0)
            tt(P, "dve", k.OT[:, g * 4:(g + 1) * 4, qs], Ops[:, :].rearrange("p (a b) -> p a b", b=128),
               TF[4].rearrange("p (a b) -> p a b", b=128), ALU.mult, (opr, "tf4"), (f"OT{tb}",))
        ckpt(f"c_att{j}_{g}")
        for pair in range(2):
            wg = load_w(k, w_in[:, 3072 + g * 512 + pair * 256:3072 + g * 512 + (pair + 1) * 256], 16, 256)
            for hh in range(2):
                qh = pair * 2 + hh
                for tb in range(NBK):
                    cols = slice(tb * BS, (tb + 1) * BS)
                    gp_ = tb % 2
                    proj_block(k, wg[0], wg[1], hh * 128, 128, tb, k.PS[gp_], f"ps{gp_}")
                    act(P, TF[4 + gp_], k.PS[gp_][:, :], AF.Silu, (f"ps{gp_}",), (f"tf{4 + gp_}",))
                    tt(P, "dve", k.OT[:, g * 4 + qh, cols], k.OT[:, g * 4 + qh, cols], TF[4 + gp_], ALU.mult,
                       (f"tf{4 + gp_}", f"OT{tb}"), (f"OT{tb}",))


_NC_CACHE = {}


def _f32(a):
    return np.ascontiguousarray(np.asarray(a, dtype=np.float32))


def _rope_tables(rot_dim, pos):
    n_freq = rot_dim // 4
    inv = (10000.0 ** (-np.arange(n_freq, dtype=np.float32) / n_freq)).astype(np.float32)
    row = np.floor(pos / 64).astype(np.float32)
    col = (pos % 64).astype(np.float32)
    ang = np.concatenate([row[:, None] * inv, col[:, None] * inv], axis=-1).astype(np.float32)
    cos, sin = np.cos(ang).astype(np.float32), np.sin(ang).astype(np.float32)
    idm = pos < 0
    cos[idm] = 1.0
    sin[idm] = 0.0
    c = np.concatenate([cos, cos], axis=-1).T
    s_ = np.concatenate([-sin, sin], axis=-1).T
    return np.stack([c, s_], axis=0).astype(np.float32)


def kernel(x_prompt, x_sample, cache_ckv, cache_kpe, state_hgrn_fwd, state_hgrn_bwd, cache_k_c, cache_v_c,
           c, c_ctx, mod_w_ab, mod_b_ab, norm_ab, w_in_ab, q_lora_norm, kv_lora_norm, w_q_up, w_kv_up,
           q_norm_ab, k_norm_ab, hgrn_lb_logits, hgrn_out_norm, w_out_ab, mod_w_c, mod_b_c, norm_c, w_in_c,
           q_norm_c, k_norm_c, sink_c, w_out_c):
    A = {n: _f32(v) for n, v in locals().items()}
    if "nc" not in _NC_CACHE:
        _NC_CACHE["nc"] = build_nc()
    nc = _NC_CACHE["nc"]

    cst = np.zeros((128, 832), np.float32)
    cst[:, 0:128] = np.eye(128)
    m = np.arange(128)
    cst[(m + 64) % 128, 128 + m] = 1.0
    jj, ii = np.meshgrid(np.arange(128), np.arange(128), indexing="ij")
    cst[:, 256:384] = (jj >= ii)
    cst[:, 384:512] = (jj <= ii)
    cst[:, 512:640] = (ii + jj >= 127)
    m64 = np.arange(64)
    cst[(m64 + 32) % 64, 640 + m64] = 1.0
    s32, t32 = np.meshgrid(np.arange(128) % HC, np.arange(HC), indexing="ij")
    cst[:, 704:704 + HC] = (s32 <= t32)
    cst[:, 704 + HC:704 + 2 * HC] = (s32 >= t32)

    def fm(v, n):
        return np.ascontiguousarray(v.reshape(n, 128).T)

    modb = np.stack([fm(A["mod_b_ab"][0], 48), fm(A["mod_b_c"][0], 48), fm(A["mod_b_ab"][1], 48),
                     fm(A["mod_b_c"][1], 48)])
    normT = np.stack([fm(A["norm_ab"][0], 16), fm(A["norm_c"][0], 16), fm(A["norm_ab"][1], 16),
                      fm(A["norm_c"][1], 16)])
    vecab = np.zeros((2, 128, 16), np.float32)
    for j in range(2):
        vecab[j, :, 0:4] = fm(A["q_lora_norm"][j], 4)
        vecab[j, :, 4:6] = fm(A["kv_lora_norm"][j], 2)
        vecab[j, :, 6] = A["q_norm_ab"][j][0:128]
        vecab[j, 0:64, 7] = A["q_norm_ab"][j][128:192]
        vecab[j, :, 8] = A["k_norm_ab"][j][0:128]
        vecab[j, 0:64, 9] = A["k_norm_ab"][j][128:192]
        vecab[j, :, 10] = A["hgrn_out_norm"][j]
    vecc = np.zeros((2, 128, 18), np.float32)
    for j in range(2):
        vecc[j, :, 0] = A["q_norm_c"][j]
        vecc[j, :, 1] = A["k_norm_c"][j]
        vecc[j, :, 2:18] = A["sink_c"][j][None, :]
    w_in_ab_sw = A["w_in_ab"].copy()
    w_in_ab_sw[:, :, O_F1:O_F1 + 1024] = A["w_in_ab"][:, :, O_F2:O_F2 + 1024]
    w_in_ab_sw[:, :, O_F2:O_F2 + 1024] = A["w_in_ab"][:, :, O_F1:O_F1 + 1024]

    in_maps = []
    for core in range(8):
        odd = core % 2
        b = core // 2
        xp = A["x_prompt"][2 * core:2 * core + 2]
        xs = A["x_sample"][b, 0:1024] if not odd else A["x_sample"][b, 1024:2048]
        pos = np.arange(1024, dtype=np.float32) if not odd else np.arange(1024, 2048, dtype=np.float32)
        if odd:
            xp = xp[:, ::-1]
            xs = xs[::-1]
            pos = pos[::-1]
        xt = np.concatenate([xp[0], xp[1], xs], axis=0)
        posall = np.concatenate([-np.ones(512, np.float32), pos])
        lbl = A["hgrn_lb_logits"]
        if odd:
            lbl = lbl[:, ::-1]
        lbl_fm = np.ascontiguousarray(lbl.reshape(2, 2, 8, 128).transpose(3, 0, 1, 2))
        condT = np.stack([fm(A["c_ctx"], 16), fm(A["c"][b], 16)], axis=-1)
        s0 = A["state_hgrn_bwd"][b] if odd else A["state_hgrn_fwd"][b]
        sel = np.zeros((128, 2), np.float32)
        sel[:, 1 - odd] = 1.0
        in_maps.append({
            "xT": np.ascontiguousarray(xt.T),
            "condT": np.ascontiguousarray(condT),
            "mod_w_ab": A["mod_w_ab"], "mod_w_c": A["mod_w_c"], "modb": modb, "normT": normT,
            "w_in_ab": w_in_ab_sw if odd else A["w_in_ab"],
            "w_q_up": A["w_q_up"], "w_kv_up": A["w_kv_up"], "w_out_ab": A["w_out_ab"],
            "w_in_c": A["w_in_c"], "w_out_c": A["w_out_c"],
            "vecab": vecab, "lbl": lbl_fm, "vecc": vecc,
            "ropeA": _rope_tables(64, pos), "ropeC": _rope_tables(128, pos),
            "cst": cst, "sel": sel,
            "ckvT": np.ascontiguousarray(A["cache_ckv"][b].transpose(0, 2, 1)),
            "kpeT": np.ascontiguousarray(A["cache_kpe"][b].transpose(0, 2, 1)),
            "s0": np.ascontiguousarray(s0),
            "kcT": np.ascontiguousarray(A["cache_k_c"][b].transpose(0, 2, 3, 1)),
            "vc": np.ascontiguousarray(A["cache_v_c"][b].reshape(2, 256, 512)),
        })
    res = run_bass_kernel_spmd(nc, in_maps, core_ids=list(range(8)))
    R = res.results

    y_prompt = np.zeros((16, 256, D), np.float32)
    y_sample = np.zeros((4, 2048, D), np.float32)
    new_ckv = np.zeros((16, 2, 256, 256), np.float32)
    new_kpe = np.zeros((16, 2, 256, 64), np.float32)
    new_sf = np.zeros((16, 2, 8, 128, 128), np.float32)
    new_sb = np.zeros((16, 2, 8, 128, 128), np.float32)
    new_kc = np.zeros((16, 2, 256, 4, 128), np.float32)
    new_vc = np.zeros((16, 2, 256, 4, 128), np.float32)
    for core in range(8):
        odd = core % 2
        b = core // 2
        r = R[core]
        y = r["yT"].T
        ckv = r["o_ckv"].transpose(0, 2, 1)
        kpe = r["o_kpe"].transpose(0, 2, 1)
        kc = r["o_kc"].transpose(0, 3, 1, 2)
        vcx = r["o_vc"].reshape(2, 512, 4, 128)
        st = r["o_st"]
        for sq in range(2):
            sl = slice(sq * 256, (sq + 1) * 256)
            bi = 2 * core + sq
            f = (lambda a: a[::-1]) if odd else (lambda a: a)
            y_prompt[bi] = f(y[sl])
            for j in range(2):
                new_ckv[bi, j] = f(ckv[j, sl])
                new_kpe[bi, j] = f(kpe[j, sl])
                new_kc[bi, j] = f(kc[j, sl])
                new_vc[bi, j] = f(vcx[j, sl])
                new_sf[bi, j] = st[j, 1 if odd else 0, sq]
                new_sb[bi, j] = st[j, 0 if odd else 1, sq]
        ys = y[512:1536]
        if odd:
            y_sample[b, 1024:2048] = ys[::-1]
        else:
            y_sample[b, 0:1024] = ys
    return (y_prompt, y_sample, new_ckv, new_kpe, new_sf, new_sb, new_kc, new_vc)
```

```python
import numpy as np
from contextlib import ExitStack
import concourse.bass as bass
import concourse.mybir as mybir
from concourse.bass_utils import run_bass_kernel_spmd

F32 = mybir.dt.float32
BF16 = mybir.dt.bfloat16
AF = mybir.ActivationFunctionType
ALU = mybir.AluOpType

D = 2048
NT = 1536
BS = 512
NBK = 3
NKEY = 2816
EPS = 1e-6
HC = 64
NCH = NT // HC
RG = [[0, 1], [2, 3], [4, 5], [6, 7]]
O_QL, O_KV, O_KPE, O_AG, O_BQ, O_F1, O_F2, O_BI, O_BG = 0, 512, 768, 832, 1856, 2880, 3904, 4928, 5952

EPOCH = 30000
N_DMA_SEMS = 20
SCRN = 15360


class Prog:
    COMPUTE = ("pe", "act", "dve", "pool")

    def __init__(self, nc):
        self.nc = nc
        self.streams = {e: [] for e in ("pe", "act", "dve", "pool", "sp")}
        self.count = {e: 0 for e in self.COMPUTE}
        self.known = {e: {} for e in self.streams}
        self.res = {}
        self.sem_names = set()
        self.dma_rr = {"sp": 0, "pool": 0}
        self.dma_val = {}
        self.last = {}

    def _need(self, eng, tok):
        key, val = tok
        if self.known[eng].get(key, 0) >= val:
            return
        self.known[eng][key] = val
        self.sem_names.add(key)
        self.streams[eng].append(("wait", key, val))

    def _wait_tok(self, eng, tok):
        if tok[0].startswith("pe_") and eng == "pe":
            return
        self._need(eng, tok)

    def _deps(self, eng, reads, writes):
        for r in reads:
            st = self.res.get(r)
            if st and st["w"] is not None:
                self._wait_tok(eng, st["w"])
        for w in writes:
            st = self.res.get(w)
            if st:
                if st["w"] is not None:
                    self._wait_tok(eng, st["w"])
                for t in st["r"].items():
                    self._wait_tok(eng, t)

    def _commit(self, tok, reads, writes):
        for r in reads:
            st = self.res.setdefault(r, {"w": None, "r": {}})
            st["r"][tok[0]] = max(st["r"].get(tok[0], 0), tok[1])
        for w in writes:
            self.res[w] = {"w": tok, "r": {}}
        self.last[tok[0]] = tok[1]

    def op(self, eng, fn, reads=(), writes=()):
        writes = tuple(writes) + tuple(r for r in reads if r.startswith("ps") and r not in writes)
        self._deps(eng, reads, writes)
        n = self.count[eng]
        self.count[eng] = n + 1
        key = f"{eng}_{n // EPOCH}"
        tok = (key, n % EPOCH + 1)
        self.sem_names.add(key)
        self.streams[eng].append(("op", fn, key, 1))
        self._commit(tok, reads, writes)
        return tok

    def dma(self, queue, fn, reads=(), writes=(), inc=16):
        self._deps(queue, reads, writes)
        if inc == 1:
            i = self.dma_rr.get("cc", 0)
            self.dma_rr["cc"] = (i + 1) % 4
            key = f"cc_{i}"
        else:
            i = self.dma_rr[queue]
            self.dma_rr[queue] = (i + 1) % N_DMA_SEMS
            key = f"d{queue}_{i}"
        prev = self.dma_val.get(key, 0)
        if prev:
            self._need(queue, (key, prev))
        val = prev + inc
        self.dma_val[key] = val
        self.sem_names.add(key)
        self.streams[queue].append(("op", fn, key, inc))
        tok = (key, val)
        self._commit(tok, reads, writes)
        return tok

    def barrier(self):
        toks = list(self.last.items())
        for eng in self.streams:
            for t in toks:
                self._need(eng, t)

    def emit(self, block, sems):
        def run(engine_obj, stream):
            for item in stream:
                if item[0] == "wait":
                    engine_obj.wait_ge(sems[item[1]], item[2])
                else:
                    _, fn, key, inc = item
                    fn(engine_obj).then_inc(sems[key], inc)

        @block.tensor
        def _(e):
            run(e, self.streams["pe"])

        @block.scalar
        def _(e):
            run(e, self.streams["act"])

        @block.vector
        def _(e):
            run(e, self.streams["dve"])

        @block.gpsimd
        def _(e):
            run(e, self.streams["pool"])

        @block.sync
        def _(e):
            run(e, self.streams["sp"])


def mm(P, out, lhsT, rhs, start, stop, reads, writes):
    return P.op("pe", lambda e, o=out, l=lhsT, r=rhs, s=start, t=stop:
                e.matmul(o, lhsT=l, rhs=r, start=s, stop=t), reads, writes)


def tr(P, out, in_, ident, reads, writes):
    return P.op("pe", lambda e, o=out, i=in_, d=ident: e.transpose(o, i, d), reads, writes)


def act(P, out, in_, func, reads, writes, bias=None, scale=None):
    kw = {}
    if bias is not None:
        kw["bias"] = bias
    if scale is not None:
        kw["scale"] = scale
    return P.op("act", lambda e, o=out, i=in_, f=func, k=kw: e.activation(out=o, in_=i, func=f, **k), reads, writes)


def tt(P, eng, out, in0, in1, op, reads, writes):
    return P.op(eng, lambda e, o=out, a=in0, b=in1, p=op: e.tensor_tensor(out=o, in0=a, in1=b, op=p), reads, writes)


def ts(P, eng, out, in0, s1, s2, op0, op1, reads, writes):
    if s2 is None:
        return P.op(eng, lambda e, o=out, a=in0, x=s1, p=op0:
                    e.tensor_single_scalar(out=o, in_=a, scalar=x, op=p), reads, writes)
    return P.op(eng, lambda e, o=out, a=in0, x=s1, y=s2, p=op0, q=op1:
                e.tensor_scalar(out=o, in0=a, scalar1=x, scalar2=y, op0=p, op1=q), reads, writes)


def stt(P, eng, out, in0, scalar, in1, op0, op1, reads, writes):
    return P.op(eng, lambda e, o=out, a=in0, s=scalar, b=in1, p=op0, q=op1:
                e.scalar_tensor_tensor(out=o, in0=a, scalar=s, in1=b, op0=p, op1=q), reads, writes)


def cp(P, eng, out, in_, reads, writes):
    if eng == "act":
        return P.op("act", lambda e, o=out, i=in_: e.copy(out=o, in_=i), reads, writes)
    return P.op(eng, lambda e, o=out, i=in_: e.tensor_copy(out=o, in_=i), reads, writes)


def dma(P, q, out, in_, reads, writes):
    return P.dma(q, lambda e, o=out, i=in_: e.dma_start(out=o, in_=i), reads, writes)


class K:
    pass


class _Stop(Exception):
    pass


import os as _os
_KSTOP = [_os.environ.get("KSTOP")]


def ckpt(name):
    if _KSTOP[0] == name:
        raise _Stop()


def build_nc():
    nc = bass.Bass("TRN2", target_bir_lowering=False)
    k = K()
    k.nc = nc

    def din(name, shape):
        return nc.dram_tensor(name, list(shape), F32, kind="ExternalInput").ap()

    def dout(name, shape):
        return nc.dram_tensor(name, list(shape), F32, kind="ExternalOutput").ap()

    def dint(name, shape, dt=F32):
        return nc.dram_tensor(name, list(shape), dt, kind="Internal").ap()

    I = {}
    I["xT"] = din("xT", [D, NT])
    I["condT"] = din("condT", [128, 16, 2])
    I["mod_w_ab"] = din("mod_w_ab", [2, D, 3 * D])
    I["mod_w_c"] = din("mod_w_c", [2, D, 3 * D])
    I["modb"] = din("modb", [4, 128, 48])
    I["normT"] = din("normT", [4, 128, 16])
    I["w_in_ab"] = din("w_in_ab", [2, D, 6976])
    I["w_q_up"] = din("w_q_up", [2, 512, 1536])
    I["w_kv_up"] = din("w_kv_up", [2, 256, 2048])
    I["w_out_ab"] = din("w_out_ab", [2, D, D])
    I["w_in_c"] = din("w_in_c", [2, D, 5120])
    I["w_out_c"] = din("w_out_c", [2, D, D])
    I["vecab"] = din("vecab", [2, 128, 16])
    I["lbl"] = din("lbl", [128, 2, 2, 8])
    I["vecc"] = din("vecc", [2, 128, 18])
    I["ropeA"] = din("ropeA", [2, 64, 1024])
    I["ropeC"] = din("ropeC", [2, 128, 1024])
    I["cst"] = din("cst", [128, 832])
    I["sel"] = din("sel", [128, 2])
    I["ckvT"] = din("ckvT", [2, 256, 256])
    I["kpeT"] = din("kpeT", [2, 64, 256])
    I["s0"] = din("s0", [2, 8, 128, 128])
    I["kcT"] = din("kcT", [2, 4, 128, 256])
    I["vc"] = din("vc", [2, 256, 512])
    O = {}
    O["yT"] = dout("yT", [D, NT])
    O["o_ckv"] = dout("o_ckv", [2, 256, 512])
    O["o_kpe"] = dout("o_kpe", [2, 64, 512])
    O["o_st"] = dout("o_st", [2, 2, 2, 8, 128, 128])
    O["o_kc"] = dout("o_kc", [2, 4, 128, 512])
    O["o_vc"] = dout("o_vc", [2, 512, 512])
    X = {}
    X["xs"] = [dint("xs0", [D, NT]), dint("xs1", [D, NT])]
    X["b_lat"] = dint("b_lat", [320, 1024])
    X["g_lat"] = dint("g_lat", [640, 1024])
    X["b_st"] = [dint(f"b_st{h}", [128, 128]) for h in range(8)]
    X["g_st"] = [dint(f"g_st{h}", [256, 128]) for h in range(8)]
    X["b_win"] = dint("b_win", [128, 1024])
    X["g_win"] = dint("g_win", [256, 1024])
    k.I, k.O, k.X = I, O, X

    with ExitStack() as es:
        def sb(name, shape, dt):
            return es.enter_context(nc.sbuf_tensor(name, list(shape), dt))

        k.HT = sb("HT", [128, 16, NT], BF16)
        k.OT = sb("OT", [128, 16, NT], BF16)
        k.W = [sb(f"W{i}", [128, 16, 256], BF16) for i in range(2)]
        k.SCR = sb("SCR", [128, SCRN], F32)
        k.MW = sb("MW", [128, 16, 128], BF16)
        k.CST = sb("CST", [128, 832], F32)
        k.CSTB = sb("CSTB", [128, 832], BF16)
        k.ONES = sb("ONES", [128, 128], BF16)
        k.ONEF = sb("ONEF", [128, NT], BF16)
        k.SEL = sb("SEL", [128, 2], F32)
        k.MOD = sb("MOD", [128, 4, 48, 2], F32)
        k.AMOD = sb("AMOD", [128, 4, 16, 2], F32)
        k.MODB = sb("MODB", [128, 4, 48], F32)
        k.NRM = sb("NRM", [128, 4, 16], F32)
        k.SC = sb("SC", [128, 16, 2], BF16)
        k.CONDF = sb("CONDF", [128, 16, 2], F32)
        k.VAB = sb("VAB", [128, 2, 16], F32)
        k.LBL = sb("LBL", [128, 2, 2, 8], F32)
        k.LB = sb("LB", [128, 2, 2, 8], F32)
        k.OML = sb("OML", [128, 2, 2, 8], F32)
        k.VC = sb("VC", [128, 2, 18], F32)
        k.ESINK = sb("ESINK", [128, 2, 16], F32)
        k.RPA = sb("RPA", [64, 2, 1024], F32)
        k.RPC = sb("RPC", [128, 2, 1024], F32)
        k.PS = [es.enter_context(nc.psum_tensor(f"ps{i}", [128, 512], F32)) for i in range(8)]
        k.PSB = k.PS[7]
        P = Prog(nc)
        k.P = P
        k.wi = 0
        k.modq = []
        k.mod_loaded = None
        k.mod_rate = 1
        try:
            program(k)
        except _Stop:
            P.barrier()
        sems = {s: es.enter_context(nc.semaphore(s)) for s in sorted(P.sem_names)}
        with nc.Block() as block:
            P.emit(block, sems)
    return nc


class Scr:
    def __init__(self, k, tag):
        self.k, self.tag, self.f = k, tag, 0

    def F(self, name, rows, *shape):
        n = int(np.prod(shape))
        ap = self.k.SCR[0:rows, self.f:self.f + n]
        self.f += n
        assert self.f <= SCRN, (self.tag, name, self.f)
        if len(shape) == 2:
            ap = ap.rearrange("p (a b) -> p a b", b=shape[1])
        elif len(shape) == 3:
            ap = ap.rearrange("p (a b c) -> p a b c", b=shape[1], c=shape[2])
        return ap

    def B(self, name, rows, *shape):
        n = int(np.prod(shape))
        nf = (n + 1) // 2
        ap = self.k.SCR[0:rows, self.f:self.f + nf].bitcast(BF16)[:, 0:n]
        self.f += nf
        assert self.f <= SCRN, (self.tag, name, self.f)
        if len(shape) == 2:
            ap = ap.rearrange("p (a b) -> p a b", b=shape[1])
        elif len(shape) == 3:
            ap = ap.rearrange("p (a b c) -> p a b c", b=shape[1], c=shape[2])
        return ap


def mod_consume(k):
    if k.mod_loaded is None:
        return
    P = k.P
    layer, n = k.mod_loaded
    for kc in range(16):
        mm(P, k.PS[7][:, 0:2], k.MW[:, kc, :], k.SC[:, kc, :], kc == 0, kc == 15, ("MW", "SC"), ("ps7",))
    ts(P, "dve", k.MOD[:, layer, n, :], k.PS[7][:, 0:2], k.MODB[:, layer, n:n + 1], None, ALU.add, None,
       ("ps7", "MODB"), (f"MOD{layer}",))
    k.mod_loaded = None


def mod_issue_load(k):
    if not k.modq:
        return
    P, I = k.P, k.I
    layer, n = k.modq.pop(0)
    wsrc = (I["mod_w_ab"] if layer % 2 == 0 else I["mod_w_c"])[layer // 2][:, n * 128:(n + 1) * 128]
    P.dma("pool", lambda e, s_=wsrc.rearrange("(c p) n -> p c n", p=128): e.dma_start(out=k.MW[:], in_=s_),
          (), ("MW",))
    k.mod_loaded = (layer, n)


def mod_step(k, times=1):
    for _ in range(times):
        mod_consume(k)
        mod_issue_load(k)


def mod_flush(k, layer):
    P = k.P
    while k.mod_loaded is not None or k.modq:
        mod_consume(k)
        mod_issue_load(k)
    ts(P, "dve", k.AMOD[:, layer], k.MOD[:, layer, 16:32, :], 1.0, None, ALU.add, None,
       (f"MOD{layer}",), (f"AMOD{layer}",))
    tt(P, "dve", k.AMOD[:, layer], k.AMOD[:, layer],
       k.NRM[:, layer, :].unsqueeze(2).broadcast_to([128, 16, 2]), ALU.mult, (f"AMOD{layer}", "NRM"),
       (f"AMOD{layer}",))


def load_w(k, src, nk, ncols):
    P = k.P
    i = k.wi
    k.wi = (i + 1) % 2
    buf = k.W[i][:, 0:nk, 0:ncols]
    rn = f"W{i}"
    P.dma("pool", lambda e, o=buf, s=src.rearrange("(c p) n -> p c n", p=128): e.dma_start(out=o, in_=s),
          reads=(), writes=(rn,))
    mod_step(k, k.mod_rate)
    return buf, rn


def rstd_from_ps(k, out, ps, n, reads, writes):
    P = k.P
    act(P, out, ps, AF.Ln, reads, writes, bias=EPS, scale=1.0 / n)
    act(P, out, out, AF.Exp, writes, writes, scale=-0.5)


def program(k):
    P, I, O, X = k.P, k.I, k.O, k.X
    dma(P, "sp", k.CST[:], I["cst"], (), ("CST",))
    P.dma("pool", lambda e: e.dma_start(out=k.CSTB[:], in_=I["cst"]), (), ("CSTB",))
    P.op("pool", lambda e: e.memset(k.ONES[:], 1.0), (), ("ONES",))
    P.op("pool", lambda e: e.memset(k.ONEF[:], 1.0), (), ("ONEF",))
    dma(P, "sp", k.SEL[:], I["sel"], (), ("SEL",))
    dma(P, "sp", k.CONDF[:], I["condT"], (), ("CONDF",))
    dma(P, "sp", k.MODB[:], I["modb"].rearrange("l p n -> p l n"), (), ("MODB",))
    dma(P, "sp", k.NRM[:], I["normT"].rearrange("l p n -> p l n"), (), ("NRM",))
    dma(P, "sp", k.VAB[:], I["vecab"].rearrange("l p n -> p l n"), (), ("VAB",))
    dma(P, "sp", k.LBL[:], I["lbl"], (), ("LBL",))
    dma(P, "sp", k.VC[:], I["vecc"].rearrange("l p n -> p l n"), (), ("VC",))
    dma(P, "sp", k.RPA[:], I["ropeA"].rearrange("l p n -> p l n"), (), ("RPA",))
    dma(P, "sp", k.RPC[:], I["ropeC"].rearrange("l p n -> p l n"), (), ("RPC",))
    act(P, k.SC[:], k.CONDF[:], AF.Silu, ("CONDF",), ("SC",))
    P.op("dve", lambda e: e.memset(k.SCR[0:64, 0:1024], 0.0), (), ("scr_init",))
    dma(P, "sp", k.X["b_win"][64:128, :], k.SCR[0:64, 0:1024], ("scr_init",), ("b_win",))
    act(P, k.ESINK[:], k.VC[:, :, 2:18], AF.Exp, ("VC",), ("ESINK",))
    P.op("pool", lambda e: e.memset(k.LB[:, 0], 0.0), (), ("LB",))
    tt(P, "dve", k.LB[:, 1], k.LBL[:, 1], k.LBL[:, 0], ALU.subtract, ("LBL", "LB"), ("LB",))
    act(P, k.LB[:, 1], k.LB[:, 1], AF.Sigmoid, ("LB",), ("LB",))
    ts(P, "dve", k.OML[:], k.LB[:], -1.0, 1.0, ALU.mult, ALU.add, ("LB",), ("OML",))

    ckpt("const")
    modulation(k, 0)
    ckpt("mod")
    for layer in range(4):
        j = layer // 2
        if layer < 3:
            k.modq = [(layer + 1, n) for n in range(48)]
            k.mod_rate = 1 if layer % 2 == 0 else 2
        xin = I["xT"] if layer == 0 else X["xs"][(layer - 1) % 2]
        xout = O["yT"] if layer == 3 else X["xs"][layer % 2]
        norm_mod(k, layer, xin)
        ckpt(f"nm{layer}")
        if layer % 2 == 0:
            ab_layer(k, j)
            wo = I["w_out_ab"][j]
        else:
            c_layer(k, j)
            wo = I["w_out_c"][j]
        ckpt(f"mix{layer}")
        out_proj(k, layer, wo, xin, xout)
        if layer < 3:
            mod_flush(k, layer + 1)
        ckpt(f"out{layer}")
    P.barrier()


def modulation(k, layer):
    P, I = k.P, k.I
    wsrc = (I["mod_w_ab"] if layer % 2 == 0 else I["mod_w_c"])[layer // 2]
    for g in range(24):
        wb, rn = load_w(k, wsrc[:, g * 256:(g + 1) * 256], 16, 256)
        for h in range(2):
            n = g * 2 + h
            ps = k.PS[n % 2]
            pr = f"ps{n % 2}"
            for kc in range(16):
                mm(P, ps[:, 0:2], wb[:, kc, h * 128:(h + 1) * 128], k.SC[:, kc, :], kc == 0, kc == 15,
                   (rn, "SC"), (pr,))
            ts(P, "dve", k.MOD[:, layer, n, :], ps[:, 0:2], k.MODB[:, layer, n:n + 1], None, ALU.add, None,
               (pr, "MODB"), (f"MOD{layer}",))
    ts(P, "dve", k.AMOD[:, layer], k.MOD[:, layer, 16:32, :], 1.0, None, ALU.add, None,
       (f"MOD{layer}",), (f"AMOD{layer}",))
    tt(P, "dve", k.AMOD[:, layer], k.AMOD[:, layer],
       k.NRM[:, layer, :].unsqueeze(2).broadcast_to([128, 16, 2]), ALU.mult, (f"AMOD{layer}", "NRM"),
       (f"AMOD{layer}",))


def norm_mod(k, layer, xin):
    P = k.P
    P.barrier()
    s = Scr(k, "nm")
    XC = [s.F(f"xc{i}", 128, BS) for i in range(4)]
    SQ = [s.B(f"sq{i}", 128, BS) for i in range(2)]
    RS = s.F("rs", 128, BS)
    T = [s.F(f"t{i}", 128, BS) for i in range(2)]
    n = 0
    for tb in range(NBK):
        c = 0 if tb == 0 else 1
        cols = slice(tb * BS, (tb + 1) * BS)
        ps = k.PS[2 + tb % 2]
        pr = f"ps{2 + tb % 2}"
        for fc in range(16):
            xi = n % 4
            n += 1
            dma(P, "sp", XC[xi], xin[fc * 128:(fc + 1) * 128, cols], ("xin",), (f"nm_xc{xi}",))
            act(P, SQ[fc % 2], XC[xi], AF.Square, (f"nm_xc{xi}",), (f"nm_sq{fc % 2}",))
            mm(P, ps[:, :], k.ONES[:], SQ[fc % 2], fc == 0, fc == 15, ("ONES", f"nm_sq{fc % 2}"), (pr,))
        rstd_from_ps(k, RS, ps[:, :], D, (pr,), ("nm_rs",))
        for fc in range(16):
            xi = n % 4
            n += 1
            dma(P, "sp", XC[xi], xin[fc * 128:(fc + 1) * 128, cols], ("xin",), (f"nm_xc{xi}",))
            stt(P, "dve", T[fc % 2], XC[xi], k.AMOD[:, layer, fc, c:c + 1], RS, ALU.mult, ALU.mult,
                (f"nm_xc{xi}", "nm_rs", f"AMOD{layer}"), (f"nm_t{fc % 2}",))
            act(P, k.HT[:, fc, cols], T[fc % 2], AF.Identity, (f"nm_t{fc % 2}", f"MOD{layer}"), (f"HT{tb}",),
                bias=k.MOD[:, layer, fc, c:c + 1])


def out_proj(k, layer, wo, xin, xout):
    P = k.P
    P.barrier()
    s = Scr(k, "op")
    XC = [s.F(f"xc{i}", 128, BS) for i in range(4)]
    n = 0
    for g in range(8):
        wb, rn = load_w(k, wo[:, g * 256:(g + 1) * 256], 16, 256)
        for h in range(2):
            oc = g * 2 + h
            for tb in range(NBK):
                c = 0 if tb == 0 else 1
                cols = slice(tb * BS, (tb + 1) * BS)
                ps = k.PS[n % 4]
                pr = f"ps{n % 4}"
                xi = n % 4
                n += 1
                dma(P, "sp", XC[xi], xin[oc * 128:(oc + 1) * 128, cols], ("xin",), (f"op_xc{xi}",))
                for kc in range(16):
                    mm(P, ps[:, :], wb[:, kc, h * 128:(h + 1) * 128], k.OT[:, kc, cols], kc == 0, kc == 15,
                       (rn, f"OT{tb}"), (pr,))
                stt(P, "dve", XC[xi], ps[:, :], k.MOD[:, layer, 32 + oc, c:c + 1], XC[xi], ALU.mult, ALU.add,
                    (pr, f"op_xc{xi}", f"MOD{layer}"), (f"op_xc{xi}",))
                dma(P, "sp", xout[oc * 128:(oc + 1) * 128, cols], XC[xi], (f"op_xc{xi}",), ("xout",))
    P.res["xin"] = {"w": None, "r": {}}
    P.barrier()


def proj_block(k, wb, rn, c0, ncol, tb, ps, pr, nk=16, rhs=None, rres=None):
    P = k.P
    cols = slice(tb * BS, (tb + 1) * BS)
    for kc in range(nk):
        r = k.HT[:, kc, cols] if rhs is None else rhs[:, kc, cols]
        mm(P, ps[0:ncol, :], wb[:, kc, c0:c0 + ncol], r, kc == 0, kc == nk - 1,
           (rn, rres or f"HT{tb}"), (pr,))


def ab_layer(k, j):
    P, I, O, X = k.P, k.I, k.O, k.X
    P.barrier()
    s = Scr(k, "mla")
    QLN = k.OT[:, 8:12, :]
    CKV = k.OT[:, 12:16, :].rearrange("p a b -> p (a b)")[:, 0:2 * NKEY].rearrange("p (c n) -> p c n", c=2)
    KPEG = s.B("kpeg", 64, NKEY)
    KPSQ = s.B("kpsq", 64, NKEY)
    KTN = s.B("ktn", 128, NKEY)
    KTP = s.B("ktp", 64, NKEY)
    VH = s.B("vh", 128, 22, 128)
    QTN = s.B("qtn", 128, NT)
    QTP = s.B("qtp", 64, NT)
    PT = [s.B(f"pt{i}", 128, BS) for i in range(2)]
    SQb = [s.B(f"sqb{i}", 128, BS) for i in range(2)]
    TF = [s.F(f"tf{i}", 128, BS) for i in range(6)]
    RS = s.F("rs", 128, BS)
    w_in = I["w_in_ab"][j]
    RA = k.CST[0:64, 640:704]
    vab = k.VAB[:, j, :]

    def rope64(dst, x, tb, xres, dres):
        tc_ = slice((tb - 1) * BS, tb * BS)
        mm(P, k.PS[5][0:64, :], RA, x, True, True, ("CST", xres), ("ps5",))
        tt(P, "dve", TF[3][0:64, :], k.PS[5][0:64, :], k.RPA[:, 1, tc_], ALU.mult, ("ps5", "RPA"), ("tf3",))
        tt(P, "dve", x, x, k.RPA[:, 0, tc_], ALU.mult, (xres, "RPA"), (xres,))
        tt(P, "dve", dst, x, TF[3][0:64, :], ALU.add, (xres, "tf3"), (dres,))

    P.dma("pool", lambda e: e.dma_start(out=CKV[:, :, 512:768], in_=I["ckvT"][j].rearrange("(c p) n -> p c n", p=128)),
          (), ("CKVctx",))
    dma(P, "sp", TF[0][0:64, 0:256], I["kpeT"][j], (), ("tf0",))
    act(P, KPSQ[:, 512:768], TF[0][0:64, 0:256], AF.Square, ("tf0",), ("KPSQctx",))
    ts(P, "dve", KPEG[:, 512:768], TF[0][0:64, 0:256], vab[0:64, 9:10], None, ALU.mult, None, ("tf0", "VAB"),
       ("KPEGctx",))

    wq = []
    for g in range(2):
        wq.append(load_w(k, w_in[:, O_QL + g * 256:O_QL + (g + 1) * 256], 16, 256))
    for tb in range(NBK):
        cols = slice(tb * BS, (tb + 1) * BS)
        for c in range(4):
            wb, rn = wq[c // 2]
            proj_block(k, wb, rn, (c % 2) * 128, 128, tb, k.PS[c], f"ps{c}")
            cp(P, "act", TF[c], k.PS[c][:, :], (f"ps{c}",), (f"tf{c}",))
            act(P, SQb[c % 2], TF[c], AF.Square, (f"tf{c}",), (f"sqb{c % 2}",))
            mm(P, k.PS[4][:, :], k.ONES[:], SQb[c % 2], c == 0, c == 3, ("ONES", f"sqb{c % 2}"), ("ps4",))
        rstd_from_ps(k, RS, k.PS[4][:, :], 512, ("ps4",), ("rs",))
        for c in range(4):
            stt(P, "dve", QLN[:, c, cols], TF[c], vab[:, c:c + 1], RS, ALU.mult, ALU.mult,
                (f"tf{c}", "rs", "VAB"), (f"QLN{tb}",))
    wkv = load_w(k, w_in[:, O_KV:O_KV + 256], 16, 256)
    wkp = load_w(k, w_in[:, O_KPE:O_KPE + 64], 16, 64)
    for tb in range(NBK):
        cols = slice(tb * BS, (tb + 1) * BS)
        for c in range(2):
            proj_block(k, wkv[0], wkv[1], c * 128, 128, tb, k.PS[c], f"ps{c}")
            cp(P, "act", TF[c], k.PS[c][:, :], (f"ps{c}",), (f"tf{c}",))
            act(P, SQb[c % 2], TF[c], AF.Square, (f"tf{c}",), (f"sqb{c % 2}",))
            mm(P, k.PS[4][:, :], k.ONES[:], SQb[c % 2], c == 0, c == 1, ("ONES", f"sqb{c % 2}"), ("ps4",))
        rstd_from_ps(k, RS, k.PS[4][:, :], 256, ("ps4",), ("rs",))
        proj_block(k, wkp[0], wkp[1], 0, 64, tb, k.PS[2], "ps2")
        cp(P, "act", TF[4][0:64, :], k.PS[2][0:64, :], ("ps2",), ("tf4",))
        for c in range(2):
            stt(P, "dve", TF[c], TF[c], vab[:, 4 + c:5 + c], RS, ALU.mult, ALU.mult,
                (f"tf{c}", "rs", "VAB"), (f"tf{c}",))
        if tb == 0:
            for c in range(2):
                dma(P, "sp", O["o_ckv"][j, c * 128:(c + 1) * 128, :], TF[c], (f"tf{c}",), ("o_ckv",))
                cp(P, "act", CKV[:, c, 0:512], TF[c], (f"tf{c}",), ("CKVp",))
            dma(P, "sp", O["o_kpe"][j], TF[4][0:64, :], ("tf4",), ("o_kpe",))
            act(P, KPSQ[:, 0:512], TF[4][0:64, :], AF.Square, ("tf4",), ("KPSQp",))
            ts(P, "dve", KPEG[:, 0:512], TF[4][0:64, :], vab[0:64, 9:10], None, ALU.mult, None, ("tf4", "VAB"),
               ("KPEGp",))
        else:
            lc = slice((tb - 1) * BS, tb * BS)
            for c in range(2):
                dma(P, "sp", X["b_lat"][c * 128:(c + 1) * 128, lc], TF[c], (f"tf{c}",), ("b_lat",))
            dma(P, "sp", X["b_win"][0:64, lc], TF[4][0:64, :], ("tf4",), ("b_win",))
            ts(P, "dve", TF[5][0:64, :], TF[4][0:64, :], vab[0:64, 9:10], None, ALU.mult, None, ("tf4", "VAB"),
               ("tf5",))
            rope64(TF[2][0:64, :], TF[5][0:64, :], tb, "tf5", "tf2")
            dma(P, "sp", X["b_lat"][256:320, lc], TF[2][0:64, :], ("tf2",), ("b_lat",))
    P.dma("pool", lambda e: e.collective_compute("AllGather", ALU.bypass, replica_groups=RG,
                                                 ins=[X["b_lat"].opt()], outs=[X["g_lat"].opt()]),
          ("b_lat",), ("g_lat",), inc=1)
    P.dma("pool", lambda e: e.collective_compute("AllGather", ALU.bypass, replica_groups=RG,
                                                 ins=[X["b_win"].opt()], outs=[X["g_win"].opt()]),
          ("b_win",), ("g_win",), inc=1)
    for r in range(2):
        kc_ = slice(768 + r * 1024, 768 + (r + 1) * 1024)
        P.dma("pool", lambda e, r=r, kc_=kc_: e.dma_start(
            out=CKV[:, :, kc_], in_=X["g_lat"][r * 320:r * 320 + 256, :].rearrange("(c p) n -> p c n", p=128)),
            ("g_lat",), (f"CKVr{r}",))
        P.dma("pool", lambda e, r=r, kc_=kc_: e.dma_start(out=KPEG[:, kc_], in_=X["g_lat"][r * 320 + 256:r * 320 + 320, :]),
              ("g_lat",), (f"KPEGr{r}",))
        for hb in range(2):
            lc = slice(hb * BS, (hb + 1) * BS)
            kq = slice(768 + r * 1024 + hb * BS, 768 + r * 1024 + (hb + 1) * BS)
            dma(P, "sp", TF[3][0:64, :], X["g_win"][r * 128:r * 128 + 64, lc], ("g_win",), ("tf3",))
            act(P, KPSQ[:, kq], TF[3][0:64, :], AF.Square, ("tf3",), (f"KPSQr{r}",))
    ckpt(f"lat{j}")
    KRES = ("CKVctx", "CKVp", "CKVr0", "CKVr1")
    PRES = ("KPEGctx", "KPEGp", "KPEGr0", "KPEGr1")
    SRES = ("KPSQctx", "KPSQp", "KPSQr0", "KPSQr1")

    KB = [(i * 512, min(512, NKEY - i * 512)) for i in range(6)]
    for h in range(8):
        wqn = load_w(k, I["w_q_up"][j][:, h * 192:(h + 1) * 192], 4, 192)
        wkv_ = load_w(k, I["w_kv_up"][j][:, h * 256:(h + 1) * 256], 2, 256)
        for tb in range(NBK):
            cols = slice(tb * BS, (tb + 1) * BS)
            proj_block(k, wqn[0], wqn[1], 0, 128, tb, k.PS[0], "ps0", nk=4, rhs=QLN, rres=f"QLN{tb}")
            proj_block(k, wqn[0], wqn[1], 128, 64, tb, k.PS[1], "ps1", nk=4, rhs=QLN, rres=f"QLN{tb}")
            cp(P, "act", TF[0], k.PS[0][:, :], ("ps0",), ("tf0",))
            cp(P, "act", TF[1][0:64, :], k.PS[1][0:64, :], ("ps1",), ("tf1",))
            act(P, SQb[0], TF[0], AF.Square, ("tf0",), ("sqb0",))
            act(P, SQb[1][0:64, :], TF[1][0:64, :], AF.Square, ("tf1",), ("sqb1",))
            mm(P, k.PS[4][:, :], k.ONES[:], SQb[0], True, False, ("ONES", "sqb0"), ("ps4",))
            mm(P, k.PS[4][:, :], k.ONES[0:64, :], SQb[1][0:64, :], False, True, ("ONES", "sqb1"), ("ps4",))
            rstd_from_ps(k, RS, k.PS[4][:, :], 192, ("ps4",), ("rs",))
            stt(P, "dve", QTN[:, cols], TF[0], vab[:, 6:7], RS, ALU.mult, ALU.mult, ("tf0", "rs", "VAB"),
                (f"QTN{tb}",))
            stt(P, "dve", TF[2][0:64, :], TF[1][0:64, :], vab[0:64, 7:8], RS[0:64, :], ALU.mult, ALU.mult,
                ("tf1", "rs", "VAB"), ("tf2",))
            if tb == 0:
                cp(P, "dve", QTP[:, cols], TF[2][0:64, :], ("tf2",), (f"QTP{tb}",))
            else:
                rope64(QTP[:, cols], TF[2][0:64, :], tb, "tf2", f"QTP{tb}")
        for bi, (c0, n) in enumerate(KB):
            kc_ = slice(c0, c0 + n)
            ps = k.PS[bi % 2]
            pr = f"ps{bi % 2}"
            for c in range(2):
                mm(P, ps[:, 0:n], wkv_[0][:, c, 0:128], CKV[:, c, kc_], c == 0, c == 1, (wkv_[1],) + KRES, (pr,))
            cp(P, "act", TF[bi % 2][:, 0:n], ps[:, 0:n], (pr,), (f"tf{bi % 2}",))
            act(P, SQb[bi % 2][:, 0:n], TF[bi % 2][:, 0:n], AF.Square, (f"tf{bi % 2}",), (f"sqb{bi % 2}",))
            ps2 = k.PS[2 + bi % 2]
            pr2 = f"ps{2 + bi % 2}"
            mm(P, ps2[:, 0:n], k.ONES[:], SQb[bi % 2][:, 0:n], True, False, ("ONES", f"sqb{bi % 2}"), (pr2,))
            mm(P, ps2[:, 0:n], k.ONES[0:64, :], KPSQ[:, kc_], False, True, ("ONES",) + SRES, (pr2,))
            rstd_from_ps(k, RS[:, 0:n], ps2[:, 0:n], 192, (pr2,), ("rs",))
            stt(P, "dve", KTN[:, kc_], TF[bi % 2][:, 0:n], vab[:, 8:9], RS[:, 0:n], ALU.mult, ALU.mult,
                (f"tf{bi % 2}", "rs", "VAB"), ("KTN",))
            tt(P, "dve", KTP[:, kc_], KPEG[:, kc_], RS[0:64, 0:n], ALU.mult, PRES + ("rs",), ("KTP",))
        for g in range(6):
            nt_ = min(4, 22 - g * 4)
            ps = k.PS[g % 2]
            pr = f"ps{g % 2}"
            for t in range(nt_):
                kt = g * 4 + t
                for c in range(2):
                    mm(P, ps[:, t * 128:(t + 1) * 128], CKV[:, c, kt * 128:(kt + 1) * 128], wkv_[0][:, c, 128:256],
                       c == 0, c == 1, (wkv_[1],) + KRES, (pr,))
            cp(P, "act", VH[:, g * 4:g * 4 + nt_, :], ps[:, 0:nt_ * 128].rearrange("p (a b) -> p a b", b=128),
               (pr,), ("VH",))
        wg = load_w(k, w_in[:, O_AG + h * 128:O_AG + (h + 1) * 128], 16, 128)
        groups = [(0, 256, [0, 1]), (256, 256, [2, 3]), (512, 512, list(range(4, 22))),
                  (1024, 512, list(range(4, 22)))]
        for gi, (q0, qn, kts) in enumerate(groups):
            qs = slice(q0, q0 + qn)
            tb = q0 // BS
            Ops, Dps = k.PS[4], k.PS[5]
            def s_mm(ki):
                ks = slice(kts[ki] * 128, (kts[ki] + 1) * 128)
                sp_ = k.PS[ki % 2]
                spr = f"ps{ki % 2}"
                mm(P, sp_[:, 0:qn], KTN[:, ks], QTN[:, qs], True, False, ("KTN", f"QTN{tb}"), (spr,))
                mm(P, sp_[:, 0:qn], KTP[:, ks], QTP[:, qs], False, True, ("KTP", f"QTP{tb}"), (spr,))

            s_mm(0)
            for ki, kt in enumerate(kts):
                sp_ = k.PS[ki % 2]
                spr = f"ps{ki % 2}"
                if ki + 1 < len(kts):
                    s_mm(ki + 1)
                act(P, PT[ki % 2][:, 0:qn], sp_[:, 0:qn], AF.Exp, (spr,), (f"pt{ki % 2}",), scale=192 ** -0.5)
                mm(P, Ops[:, 0:qn], VH[:, kt, :], PT[ki % 2][:, 0:qn], ki == 0, ki == len(kts) - 1,
                   ("VH", f"pt{ki % 2}"), ("ps4",))
                mm(P, Dps[:, 0:qn], k.ONES[:], PT[ki % 2][:, 0:qn], ki == 0, ki == len(kts) - 1,
                   ("ONES", f"pt{ki % 2}"), ("ps5",))
            act(P, TF[4][:, 0:qn], Dps[:, 0:qn], AF.Ln, ("ps5",), ("tf4",))
            act(P, TF[4][:, 0:qn], TF[4][:, 0:qn], AF.Exp, ("tf4",), ("tf4",), scale=-1.0)
            tt(P, "dve", TF[5][:, 0:qn], Ops[:, 0:qn], TF[4][:, 0:qn], ALU.mult, ("ps4", "tf4"), ("tf5",))
            gp = k.PS[2 + gi % 2]
            gpr = f"ps{2 + gi % 2}"
            for kc in range(16):
                mm(P, gp[:, 0:qn], wg[0][:, kc, :], k.HT[:, kc, qs], kc == 0, kc == 15, (wg[1], f"HT{tb}"), (gpr,))
            act(P, TF[3][:, 0:qn], gp[:, 0:qn], AF.Silu, (gpr,), ("tf3",))
            tt(P, "dve", k.OT[:, h, qs], TF[5][:, 0:qn], TF[3][:, 0:qn], ALU.mult, ("tf5", "tf3"), (f"OT{tb}",))
    ckpt(f"mla{j}")
    hgrn(k, j)


def hgrn(k, j):
    P, I, O, X = k.P, k.I, k.O, k.X
    P.barrier()
    s = Scr(k, "hg")
    T1 = s.F("t1", 128, NT)
    T2 = s.F("t2", 128, NT)
    T3 = s.F("t3", 128, NT)
    T4 = s.F("t4", 128, NT)
    OPH = s.F("oph", 128, NT)
    OAC = OPH
    CS = s.F("cs", 128, 6, NCH)
    ST = s.F("st", 128, 128)
    ST1 = s.F("st1", 128, 128)
    GST = s.F("gst", 128, 2, 128)
    Qb = s.B("q", 128, NT)
    Kb = s.B("kb", 128, NT)
    QTl = s.B("qtl", 128, NT)
    KTl = s.B("ktl", 128, NT)
    KTM = s.B("ktm", 128, 12, 128)
    Gb = s.B("g", 128, NT)
    Vb = s.B("v", 128, 12, 128)
    AT = s.B("at", 128, NT // 2)
    STb = s.B("stb", 128, 128)
    SQb = s.B("sqb", 128, BS)
    w_in = I["w_in_ab"][j]
    identb = k.CSTB[:, 0:128]
    vab = k.VAB[:, j, :]
    SEGS = [(0, 256), (256, 256), (512, 1024)]

    for h in range(8):
        wqg = load_w(k, w_in[:, O_BQ + h * 128:O_BQ + (h + 1) * 128], 16, 128)
        for tb in range(NBK):
            cols = slice(tb * BS, (tb + 1) * BS)
            proj_block(k, wqg[0], wqg[1], 0, 128, tb, k.PS[tb % 2], f"ps{tb % 2}")
            act(P, Qb[:, cols], k.PS[tb % 2][:, :], AF.Silu, (f"ps{tb % 2}",), ("hq",))
        wgg = load_w(k, w_in[:, O_BG + h * 128:O_BG + (h + 1) * 128], 16, 128)
        for tb in range(NBK):
            cols = slice(tb * BS, (tb + 1) * BS)
            proj_block(k, wgg[0], wgg[1], 0, 128, tb, k.PS[2 + tb % 2], f"ps{2 + tb % 2}")
            act(P, Gb[:, cols], k.PS[2 + tb % 2][:, :], AF.Silu, (f"ps{2 + tb % 2}",), ("hg",))
        wv = load_w(k, w_in[:, O_BI + h * 128:O_BI + (h + 1) * 128], 16, 128)
        for g in range(3):
            ps = k.PS[g % 2]
            pr = f"ps{g % 2}"
            for t in range(4):
                tl = g * 4 + t
                for kc in range(16):
                    mm(P, ps[:, t * 128:(t + 1) * 128], k.HT[:, kc, tl * 128:(tl + 1) * 128], wv[0][:, kc, :],
                       kc == 0, kc == 15, (wv[1], f"HT{tl // 4}"), (pr,))
            cp(P, "act", Vb[:, g * 4:(g + 1) * 4, :], ps[:, :].rearrange("p (a b) -> p a b", b=128), (pr,), ("hv",))

        for ph in range(2):
            wf = load_w(k, w_in[:, (O_F1, O_F2)[ph] + h * 128:(O_F1, O_F2)[ph] + (h + 1) * 128], 16, 128)
            lb = k.LB[:, j, ph, h:h + 1]
            oml = k.OML[:, j, ph, h:h + 1]
            for tb in range(NBK):
                cols = slice(tb * BS, (tb + 1) * BS)
                proj_block(k, wf[0], wf[1], 0, 128, tb, k.PS[tb % 2], f"ps{tb % 2}")
                act(P, T1[:, cols], k.PS[tb % 2][:, :], AF.Sigmoid, (f"ps{tb % 2}",), ("t1",))
            ts(P, "dve", T1, T1, oml, lb, ALU.mult, ALU.add, ("t1", "LB", "OML"), ("t1",))
            act(P, T2, T1, AF.Ln, ("t1",), ("t2",))
            act(P, Kb, T1, AF.Identity, ("t1",), ("hk",), bias=1.0, scale=-1.0)
            P.op("dve", lambda e: e.tensor_tensor_scan(out=T1, data0=k.ONEF[:], data1=T2, initial=0.0,
                                                       op0=ALU.mult, op1=ALU.add), ("t2", "ONEF", "hk"), ("t1",))
            tt(P, "dve", T2, T1, T2, ALU.subtract, ("t1", "t2"), ("t2",))
            Bv = T1.rearrange("p (c t) -> p c t", t=HC)
            Xv = T2.rearrange("p (c t) -> p c t", t=HC)
            lo, hi, mid, din_, d2, d1 = (CS[:, i, :] for i in range(6))
            cp(P, "dve", lo, Xv[:, :, 0], ("t2",), ("cs",))
            cp(P, "dve", hi, Bv[:, :, HC - 1], ("t1", "cs"), ("cs",))
            if ph == 0:
                cp(P, "dve", mid, Bv[:, :, HC // 2], ("t1", "cs"), ("cs",))
                tt(P, "dve", T3.rearrange("p (c t) -> p c t", t=HC), Bv,
                   mid.unsqueeze(2).broadcast_to([128, NCH, HC]), ALU.subtract, ("t1", "cs"), ("t3",))
                tt(P, "dve", din_, mid, lo, ALU.subtract, ("cs",), ("cs",))
                tt(P, "dve", d2, hi, mid, ALU.subtract, ("cs",), ("cs",))
            else:
                cp(P, "dve", mid, Xv[:, :, HC // 2], ("t2", "cs"), ("cs",))
                tt(P, "dve", T3.rearrange("p (c t) -> p c t", t=HC),
                   mid.unsqueeze(2).broadcast_to([128, NCH, HC]), Xv, ALU.subtract, ("t2", "cs"), ("t3",))
                tt(P, "dve", din_, hi, mid, ALU.subtract, ("cs",), ("cs",))
                tt(P, "dve", d2, mid, lo, ALU.subtract, ("cs",), ("cs",))
            tt(P, "dve", d1, hi, lo, ALU.subtract, ("cs",), ("cs",))
            act(P, CS[:, 3:6, :], CS[:, 3:6, :], AF.Exp, ("cs",), ("cs",))
            act(P, T4, T3, AF.Exp, ("t3",), ("t4",))
            tt(P, "dve", QTl, Qb, T4, ALU.mult, ("hq", "t4"), ("qtl",))
            act(P, T4, T3, AF.Exp, ("t3", "qtl"), ("t4",), scale=-1.0)
            tt(P, "dve", KTl, Kb, T4, ALU.mult, ("hk", "t4"), ("ktl",))
            for tl in range(12):
                psb = k.PS[6][:, :].bitcast(BF16)
                o = psb[:, (tl % 4) * 128:(tl % 4 + 1) * 128]
                tr(P, o, KTl[:, tl * 128:(tl + 1) * 128], identb, ("ktl", "CSTB"), ("ps6",))
                if tl % 4 == 3:
                    cp(P, "act", KTM[:, tl - 3:tl + 1, :], psb[:, 0:512].rearrange("p (a b) -> p a b", b=128),
                       ("ps6",), ("ktm",))
            CPT = 128 // HC
            for g in range(3):
                ps = k.PS[g % 2]
                pr = f"ps{g % 2}"
                for t in range(4):
                    tl = g * 4 + t
                    for ci in range(CPT):
                        c0 = tl * 128 + ci * HC
                        mm(P, ps[ci * HC:(ci + 1) * HC, t * HC:(t + 1) * HC], KTl[:, c0:c0 + HC], QTl[:, c0:c0 + HC],
                           True, True, ("ktl", "qtl"), (pr,))
                mask = k.CST[:, 704 + ph * HC:704 + (ph + 1) * HC].bitcast(mybir.dt.uint32)
                atg = AT[:, g * 4 * HC:(g + 1) * 4 * HC]
                P.op("dve", lambda e, o=atg: e.memset(o, 0.0), (), ("at",))
                P.op("dve", lambda e, o=atg.rearrange("p (a b) -> p a b", b=HC),
                     d=ps[:, 0:4 * HC].rearrange("p (a b) -> p a b", b=HC),
                     m=mask.unsqueeze(1).broadcast_to([128, 4, HC]): e.copy_predicated(out=o, mask=m, data=d),
                     (pr, "CST", "at"), ("at",))
            DSS = T4[:, 0:1024].rearrange("p (r c v) -> p r c v", r=2, c=4)
            STBA = T3.bitcast(BF16).rearrange("p (c v) -> p c v", v=128)
            seg_groups = {0: [0], 1: [1], 2: [2, 3, 4, 5]}
            sorder = [2, 0, 1] if ph == 0 else [0, 1, 2]
            glist = []
            for si in sorder:
                gs = seg_groups[si] if ph == 0 else seg_groups[si][::-1]
                glist += [(si, g) for g in gs]

            def emit_dS(n, g):
                slot = n % 2
                for ci in range(CPT):
                    ps = k.PS[4 + 2 * slot + ci]
                    pr = f"ps{4 + 2 * slot + ci}"
                    for a_ in range(2):
                        c = g * 4 + a_ * 2 + ci
                        tl = c // CPT
                        mm(P, ps[:, a_ * 128:(a_ + 1) * 128], KTM[ci * HC:(ci + 1) * HC, tl, :],
                           Vb[ci * HC:(ci + 1) * HC, tl, :], True, True, ("ktm", "hv"), (pr,))
                for ci in range(CPT):
                    ps = k.PS[4 + 2 * slot + ci]
                    pr = f"ps{4 + 2 * slot + ci}"
                    tt(P, "dve", DSS[:, slot].rearrange("p (a b) v -> p a b v", b=2)[:, :, ci, :],
                       ps[:, 0:256].rearrange("p (c v) -> p c v", v=128),
                       d2[:, g * 4:(g + 1) * 4].rearrange("p (a b) -> p a b", b=2)[:, :, ci].unsqueeze(2)
                       .broadcast_to([128, 2, 128]), ALU.mult, (pr, "cs"), (f"dss{slot}", "t4"))

            def chain_group(n, g):
                slot = n % 2
                cs_ = list(range(g * 4, g * 4 + 4))
                if ph == 1:
                    cs_ = cs_[::-1]
                for c in cs_:
                    q = c - g * 4
                    ts(P, "dve", STBA[:, c, :], ST, din_[:, c:c + 1], None, ALU.mult, None, ("st", "cs"), ("t3",))
                    stt(P, "dve", ST, ST, d1[:, c:c + 1], DSS[:, slot, q, :], ALU.mult, ALU.add,
                        ("st", "cs", f"dss{slot}", "t4"), ("st",))

            emitted = [0]

            def ensure_dS(upto):
                while emitted[0] <= min(upto, len(glist) - 1):
                    emit_dS(emitted[0], glist[emitted[0]][1])
                    emitted[0] += 1

            ensure_dS(1)
            for n, (si, g) in enumerate(glist):
                first = (n == 0 or glist[n - 1][0] != si)
                last = (n == len(glist) - 1 or glist[n + 1][0] != si)
                if first:
                    if si < 2:
                        P.op("dve", lambda e: e.memset(ST, 0.0), ("st",), ("st",))
                    elif ph == 0:
                        dma(P, "sp", ST, I["s0"][j, h], ("st",), ("st",))
                    else:
                        dma(P, "sp", GST, X["g_st"][h].rearrange("(r p) n -> p r n", r=2), ("g_st", "gst"), ("gst",))
                        ts(P, "dve", ST1, GST[:, 0, :], k.SEL[:, 0:1], None, ALU.mult, None, ("gst", "SEL", "st1"),
                           ("st1",))
                        stt(P, "dve", ST, GST[:, 1, :], k.SEL[:, 1:2], ST1, ALU.mult, ALU.add,
                            ("gst", "st1", "SEL", "st"), ("st",))
                chain_group(n, g)
                ensure_dS(n + 2)
                if last:
                    if si < 2:
                        dma(P, "sp", O["o_st"][j, ph, si, h], ST, ("st",), ("o_st",))
                    elif ph == 0:
                        dma(P, "sp", X["b_st"][h], ST, ("st",), (f"b_st{h}",))
                        P.dma("pool", lambda e, h=h: e.collective_compute(
                            "AllGather", ALU.bypass, replica_groups=RG, ins=[X["b_st"][h].opt()],
                            outs=[X["g_st"][h].opt()]), (f"b_st{h}",), ("g_st",), inc=1)
            for bi, (b0, bl) in enumerate([(0, 256), (256, 256), (512, 512), (1024, 512)]):
                ops_b = k.PS[2 + bi % 2]
                opr_b = f"ps{2 + bi % 2}"
                for c in range(b0 // HC, (b0 + bl) // HC):
                    tl, ci = c // CPT, c % CPT
                    cc = slice(c * HC, (c + 1) * HC)
                    oc = slice(c * HC - b0, c * HC - b0 + HC)
                    mm(P, ops_b[:, oc], Vb[ci * HC:(ci + 1) * HC, tl, :], AT[ci * HC:(ci + 1) * HC, tl * HC:(tl + 1) * HC],
                       True, False, ("hv", "at"), (opr_b,))
                    mm(P, ops_b[:, oc], STBA[:, c, :], QTl[:, cc], False, True, ("t3", "qtl"), (opr_b,))
                if ph == 0:
                    cp(P, "act", OPH[:, b0:b0 + bl], ops_b[:, 0:bl], (opr_b,), ("oph",))
                else:
                    tt(P, "dve", OPH[:, b0:b0 + bl], ops_b[:, 0:bl], OPH[:, b0:b0 + bl], ALU.add,
                       (opr_b, "oph"), ("oph",))
        for tb in range(NBK):
            cols = slice(tb * BS, (tb + 1) * BS)
            act(P, SQb, OAC[:, cols], AF.Square, ("oph",), ("hsq",))
            mm(P, k.PS[0][:, :], k.ONES[:], SQb, True, True, ("ONES", "hsq"), ("ps0",))
            rstd_from_ps(k, T3[:, 0:BS], k.PS[0][:, :], 128, ("ps0",), ("t3",))
            stt(P, "dve", T4[:, 0:BS], OAC[:, cols], vab[:, 10:11], T3[:, 0:BS], ALU.mult, ALU.mult,
                ("oph", "t3", "VAB"), ("t4",))
            tt(P, "dve", k.OT[:, 8 + h, cols], T4[:, 0:BS], Gb[:, cols], ALU.mult, ("t4", "hg"), (f"OT{tb}",))


def c_layer(k, j):
    P, I, O, X = k.P, k.I, k.O, k.X
    P.barrier()
    s = Scr(k, "c")
    TFB = s.F("tfb", 128, 6, BS)
    TF = [TFB[:, i, :] for i in range(6)]
    GW = TFB[:, 0:4, :].rearrange("p a b -> p (a b)").rearrange("p (r n) -> p r n", r=2)
    GWR = ("tf0", "tf1", "tf2", "tf3")
    RS = s.F("rs", 128, BS)
    KT = s.B("kt", 128, 4, NT)
    VT = s.B("vt", 128, 12, 512)
    QT = s.B("qt", 128, 4, NT)
    KC = s.B("kc", 128, 4, 256)
    VCx = s.B("vcx", 128, 2, 512)
    KB_ = s.B("kbnd", 128, 4, 128)
    VB_ = s.B("vbnd", 128, 512)
    PT = [s.B(f"pt{i}", 128, BS) for i in range(2)]
    SQb = [s.B(f"sqb{i}", 128, BS) for i in range(2)]
    w_in = I["w_in_c"][j]
    RC = k.CST[:, 128:256]
    vc = k.VC[:, j, :]
    Mprev, Mnext, Manti = (k.CSTB[:, 256 + i * 128:256 + (i + 1) * 128] for i in range(3))
    scale = 128 ** -0.5

    P.dma("pool", lambda e: e.dma_start(out=KC[:], in_=I["kcT"][j].rearrange("g p n -> p g n")), (), ("KC",))
    P.dma("pool", lambda e: e.dma_start(out=VCx[:], in_=I["vc"][j].rearrange("(t p) n -> p t n", p=128)), (), ("VCx",))

    par_ctr = [0]

    def head_fm(gcol, out_bf, tb, outres, prompt_out=None):
        par = par_ctr[0]
        par_ctr[0] ^= 1
        A0, A1, A2 = TF[3 * par], TF[3 * par + 1], TF[3 * par + 2]
        n0, n1, n2 = f"tf{3 * par}", f"tf{3 * par + 1}", f"tf{3 * par + 2}"
        psp, sqn = f"ps{par}", f"sqb{par}"
        pst, prt = k.PS[4 + 2 * par], k.PS[5 + 2 * par]
        pstn, prtn = f"ps{4 + 2 * par}", f"ps{5 + 2 * par}"
        cp(P, "act", A0, k.PS[par][:, :], (psp,), (n0,))
        act(P, SQb[par], A0, AF.Square, (n0,), (sqn,))
        mm(P, pst[:, :], k.ONES[:], SQb[par], True, True, ("ONES", sqn), (pstn,))
        rstd_from_ps(k, A2, pst[:, :], 128, (pstn,), (n2,))
        stt(P, "dve", A1, A0, vc[:, gcol:gcol + 1], A2, ALU.mult, ALU.mult, (n0, n2, "VC"), (n1,))
        if tb == 0:
            if prompt_out is not None:
                dma(P, "sp", prompt_out, A1, (n1,), ("o_kc",))
            cp(P, "dve", out_bf, A1, (n1,), (outres,))
            return
        tc_ = slice((tb - 1) * BS, tb * BS)
        mm(P, prt[:, :], RC, A1, True, True, ("CST", n1), (prtn,))
        tt(P, "dve", A2, prt[:, :], k.RPC[:, 1, tc_], ALU.mult, (prtn, "RPC"), (n2,))
        tt(P, "dve", A1, A1, k.RPC[:, 0, tc_], ALU.mult, (n1, "RPC"), (n1,))
        tt(P, "dve", out_bf, A1, A2, ALU.add, (n1, n2), (outres,))

    def cur_ps():
        return k.PS[par_ctr[0]], f"ps{par_ctr[0]}"

    for g in range(2):
        wk = load_w(k, w_in[:, 2048 + g * 256:2048 + (g + 1) * 256], 16, 256)
        for hh in range(2):
            kh = g * 2 + hh
            for tb in range(NBK):
                proj_block(k, wk[0], wk[1], hh * 128, 128, tb, *cur_ps())
                head_fm(1, KT[:, kh, tb * BS:(tb + 1) * BS], tb, "KT",
                        prompt_out=O["o_kc"][j, kh] if tb == 0 else None)
    ckpt(f"c_k{j}")
    wvs = [load_w(k, w_in[:, 2560 + g * 256:2560 + (g + 1) * 256], 16, 256) for g in range(2)]
    for tl in range(12):
        ps = k.PS[tl % 2]
        pr = f"ps{tl % 2}"
        for g in range(2):
            for kc in range(16):
                mm(P, ps[:, g * 256:(g + 1) * 256], k.HT[:, kc, tl * 128:(tl + 1) * 128], wvs[g][0][:, kc, :],
                   kc == 0, kc == 15, (wvs[g][1], f"HT{tl // 4}"), (pr,))
        cp(P, "act", VT[:, tl, :], ps[:, :], (pr,), ("VT",))
        if tl < 4:
            cp(P, "dve", TF[4 + tl % 2], ps[:, :], (pr,), (f"tf{4 + tl % 2}",))
            dma(P, "sp", O["o_vc"][j, tl * 128:(tl + 1) * 128, :], TF[4 + tl % 2], (f"tf{4 + tl % 2}",), ("o_vc",))
    ckpt(f"c_v{j}")
    cp(P, "dve", GW[:, 0, 0:512].rearrange("p (a b) -> p a b", b=128), KT[:, :, NT - 128:NT], ("KT",) + GWR, GWR)
    cp(P, "dve", GW[:, 0, 512:1024], VT[:, 11, :], ("VT",) + GWR, GWR)
    dma(P, "sp", X["b_win"], GW[:, 0, :], GWR, ("b_win",))
    P.dma("pool", lambda e: e.collective_compute("AllGather", ALU.bypass, replica_groups=RG,
                                                 ins=[X["b_win"].opt()], outs=[X["g_win"].opt()]),
          ("b_win",), ("g_win",), inc=1)
    dma(P, "sp", GW, X["g_win"].rearrange("(r p) n -> p r n", r=2), ("g_win",) + GWR, GWR)
    ts(P, "dve", GW[:, 0, :], GW[:, 0, :], k.SEL[:, 0:1], None, ALU.mult, None, GWR + ("SEL",), GWR)
    stt(P, "dve", GW[:, 0, :], GW[:, 1, :], k.SEL[:, 1:2], GW[:, 0, :], ALU.mult, ALU.add, GWR + ("SEL",), GWR)
    cp(P, "dve", KB_, GW[:, 0, 0:512].rearrange("p (a b) -> p a b", b=128), GWR, ("KB",))
    cp(P, "dve", VB_, GW[:, 0, 512:1024], GWR, ("VB",))

    ckpt(f"c_kv{j}")
    for g in range(4):
        for pair in range(2):
            wq = load_w(k, w_in[:, g * 512 + pair * 256:g * 512 + (pair + 1) * 256], 16, 256)
            for hh in range(2):
                qh = pair * 2 + hh
                for tb in range(NBK):
                    cols = slice(tb * BS, (tb + 1) * BS)
                    proj_block(k, wq[0], wq[1], hh * 128, 128, tb, *cur_ps())
                    head_fm(0, QT[:, qh, cols], tb, f"QT{tb}")
        ckpt(f"c_q{j}_{g}")
        blocks = []
        for sq in range(2):
            for qb in range(2):
                q0 = sq * 256 + qb * 128
                keys = [(KT[:, g, sq * 256 + t * 128:sq * 256 + (t + 1) * 128], VT[:, sq * 2 + t, g * 128:(g + 1) * 128],
                         None, ("KT", "VT")) for t in range(2)]
                blocks.append((q0, keys))
        for qb in range(8):
            q0 = 512 + qb * 128
            keys = [(KC[:, g, t * 128:(t + 1) * 128], VCx[:, t, g * 128:(g + 1) * 128], None, ("KC", "VCx"))
                    for t in range(2)]
            if qb > 0:
                keys.append((KT[:, g, q0 - 128:q0], VT[:, 4 + qb - 1, g * 128:(g + 1) * 128], Mprev, ("KT", "VT")))
            keys.append((KT[:, g, q0:q0 + 128], VT[:, 4 + qb, g * 128:(g + 1) * 128], None, ("KT", "VT")))
            if qb < 7:
                keys.append((KT[:, g, q0 + 128:q0 + 256], VT[:, 4 + qb + 1, g * 128:(g + 1) * 128], Mnext, ("KT", "VT")))
            else:
                keys.append((KB_[:, g, :], VB_[:, g * 128:(g + 1) * 128], Manti, ("KB", "VB")))
            blocks.append((q0, keys))
        for bi, (q0, keys) in enumerate(blocks):
            tb = q0 // BS
            qs = slice(q0, q0 + 128)
            Ops, Dps = k.PS[4 + 2 * (bi % 2)], k.PS[5 + 2 * (bi % 2)]
            opr, dpr = f"ps{4 + 2 * (bi % 2)}", f"ps{5 + 2 * (bi % 2)}"
            def s_mm(ki):
                kap_, _, _, kres_ = keys[ki]
                mm(P, k.PS[2 + ki % 2][:, :].rearrange("p (a b) -> p a b", b=128), kap_, QT[:, :, qs], True, True,
                   kres_ + (f"QT{tb}",), (f"ps{2 + ki % 2}",))

            s_mm(0)
            for ki, (kap, vap, mask, kres) in enumerate(keys):
                sp_ = k.PS[2 + ki % 2]
                spr = f"ps{2 + ki % 2}"
                if ki + 1 < len(keys):
                    s_mm(ki + 1)
                act(P, PT[ki % 2], sp_[:, :], AF.Exp, (spr,), (f"pt{ki % 2}",), scale=scale)
                if mask is not None:
                    tt(P, "dve", PT[ki % 2].rearrange("p (a b) -> p a b", b=128),
                       PT[ki % 2].rearrange("p (a b) -> p a b", b=128),
                       mask.unsqueeze(1).broadcast_to([128, 4, 128]), ALU.mult, (f"pt{ki % 2}", "CSTB"),
                       (f"pt{ki % 2}",))
                mm(P, Ops[:, :], vap, PT[ki % 2], ki == 0, ki == len(keys) - 1, kres + (f"pt{ki % 2}",), (opr,))
                mm(P, Dps[:, :], k.ONES[:], PT[ki % 2], ki == 0, ki == len(keys) - 1, ("ONES", f"pt{ki % 2}"), (dpr,))
            tt(P, "dve", TF[4].rearrange("p (a b) -> p a b", b=128), Dps[:, :].rearrange("p (a b) -> p a b", b=128),
               k.ESINK[:, j, g * 4:(g + 1) * 4].unsqueeze(2).broadcast_to([128, 4, 128]), ALU.add,
               (dpr, "ESINK"), ("tf4",))
            act(P, TF[4], TF[4], AF.Ln, ("tf4",), ("tf4",))
            act(P, TF[4], TF[4], AF.Exp, ("tf4",), ("tf4",), scale=-1.0)
            tt(P, "dve", k.OT[:, g * 4:(g + 1) * 4, qs], Ops[:, :].rearrange("p (a b) -> p a b", b=128),
               TF[4].rearrange("p (a b) -> p a b", b=128), ALU.mult, (opr, "tf4"), (f"OT{tb}",))
        ckpt(f"c_att{j}_{g}")
        for pair in range(2):
            wg = load_w(k, w_in[:, 3072 + g * 512 + pair * 256:3072 + g * 512 + (pair + 1) * 256], 16, 256)
            for hh in range(2):
                qh = pair * 2 + hh
                for tb in range(NBK):
                    cols = slice(tb * BS, (tb + 1) * BS)
                    gp_ = tb % 2
                    proj_block(k, wg[0], wg[1], hh * 128, 128, tb, k.PS[gp_], f"ps{gp_}")
                    act(P, TF[4 + gp_], k.PS[gp_][:, :], AF.Silu, (f"ps{gp_}",), (f"tf{4 + gp_}",))
                    tt(P, "dve", k.OT[:, g * 4 + qh, cols], k.OT[:, g * 4 + qh, cols], TF[4 + gp_], ALU.mult,
                       (f"tf{4 + gp_}", f"OT{tb}"), (f"OT{tb}",))


_NC_CACHE = {}


def _f32(a):
    return np.ascontiguousarray(np.asarray(a, dtype=np.float32))


def _rope_tables(rot_dim, pos):
    n_freq = rot_dim // 4
    inv = (10000.0 ** (-np.arange(n_freq, dtype=np.float32) / n_freq)).astype(np.float32)
    row = np.floor(pos / 64).astype(np.float32)
    col = (pos % 64).astype(np.float32)
    ang = np.concatenate([row[:, None] * inv, col[:, None] * inv], axis=-1).astype(np.float32)
    cos, sin = np.cos(ang).astype(np.float32), np.sin(ang).astype(np.float32)
    idm = pos < 0
    cos[idm] = 1.0
    sin[idm] = 0.0
    c = np.concatenate([cos, cos], axis=-1).T
    s_ = np.concatenate([-sin, sin], axis=-1).T
    return np.stack([c, s_], axis=0).astype(np.float32)


def kernel(x_prompt, x_sample, cache_ckv, cache_kpe, state_hgrn_fwd, state_hgrn_bwd, cache_k_c, cache_v_c,
           c, c_ctx, mod_w_ab, mod_b_ab, norm_ab, w_in_ab, q_lora_norm, kv_lora_norm, w_q_up, w_kv_up,
           q_norm_ab, k_norm_ab, hgrn_lb_logits, hgrn_out_norm, w_out_ab, mod_w_c, mod_b_c, norm_c, w_in_c,
           q_norm_c, k_norm_c, sink_c, w_out_c):
    A = {n: _f32(v) for n, v in locals().items()}
    if "nc" not in _NC_CACHE:
        _NC_CACHE["nc"] = build_nc()
    nc = _NC_CACHE["nc"]

    cst = np.zeros((128, 832), np.float32)
    cst[:, 0:128] = np.eye(128)
    m = np.arange(128)
    cst[(m + 64) % 128, 128 + m] = 1.0
    jj, ii = np.meshgrid(np.arange(128), np.arange(128), indexing="ij")
    cst[:, 256:384] = (jj >= ii)
    cst[:, 384:512] = (jj <= ii)
    cst[:, 512:640] = (ii + jj >= 127)
    m64 = np.arange(64)
    cst[(m64 + 32) % 64, 640 + m64] = 1.0
    s32, t32 = np.meshgrid(np.arange(128) % HC, np.arange(HC), indexing="ij")
    cst[:, 704:704 + HC] = (s32 <= t32)
    cst[:, 704 + HC:704 + 2 * HC] = (s32 >= t32)

    def fm(v, n):
        return np.ascontiguousarray(v.reshape(n, 128).T)

    modb = np.stack([fm(A["mod_b_ab"][0], 48), fm(A["mod_b_c"][0], 48), fm(A["mod_b_ab"][1], 48),
                     fm(A["mod_b_c"][1], 48)])
    normT = np.stack([fm(A["norm_ab"][0], 16), fm(A["norm_c"][0], 16), fm(A["norm_ab"][1], 16),
                      fm(A["norm_c"][1], 16)])
    vecab = np.zeros((2, 128, 16), np.float32)
    for j in range(2):
        vecab[j, :, 0:4] = fm(A["q_lora_norm"][j], 4)
        vecab[j, :, 4:6] = fm(A["kv_lora_norm"][j], 2)
        vecab[j, :, 6] = A["q_norm_ab"][j][0:128]
        vecab[j, 0:64, 7] = A["q_norm_ab"][j][128:192]
        vecab[j, :, 8] = A["k_norm_ab"][j][0:128]
        vecab[j, 0:64, 9] = A["k_norm_ab"][j][128:192]
        vecab[j, :, 10] = A["hgrn_out_norm"][j]
    vecc = np.zeros((2, 128, 18), np.float32)
    for j in range(2):
        vecc[j, :, 0] = A["q_norm_c"][j]
        vecc[j, :, 1] = A["k_norm_c"][j]
        vecc[j, :, 2:18] = A["sink_c"][j][None, :]
    w_in_ab_sw = A["w_in_ab"].copy()
    w_in_ab_sw[:, :, O_F1:O_F1 + 1024] = A["w_in_ab"][:, :, O_F2:O_F2 + 1024]
    w_in_ab_sw[:, :, O_F2:O_F2 + 1024] = A["w_in_ab"][:, :, O_F1:O_F1 + 1024]

    in_maps = []
    for core in range(8):
        odd = core % 2
        b = core // 2
        xp = A["x_prompt"][2 * core:2 * core + 2]
        xs = A["x_sample"][b, 0:1024] if not odd else A["x_sample"][b, 1024:2048]
        pos = np.arange(1024, dtype=np.float32) if not odd else np.arange(1024, 2048, dtype=np.float32)
        if odd:
            xp = xp[:, ::-1]
            xs = xs[::-1]
            pos = pos[::-1]
        xt = np.concatenate([xp[0], xp[1], xs], axis=0)
        posall = np.concatenate([-np.ones(512, np.float32), pos])
        lbl = A["hgrn_lb_logits"]
        if odd:
            lbl = lbl[:, ::-1]
        lbl_fm = np.ascontiguousarray(lbl.reshape(2, 2, 8, 128).transpose(3, 0, 1, 2))
        condT = np.stack([fm(A["c_ctx"], 16), fm(A["c"][b], 16)], axis=-1)
        s0 = A["state_hgrn_bwd"][b] if odd else A["state_hgrn_fwd"][b]
        sel = np.zeros((128, 2), np.float32)
        sel[:, 1 - odd] = 1.0
        in_maps.append({
            "xT": np.ascontiguousarray(xt.T),
            "condT": np.ascontiguousarray(condT),
            "mod_w_ab": A["mod_w_ab"], "mod_w_c": A["mod_w_c"], "modb": modb, "normT": normT,
            "w_in_ab": w_in_ab_sw if odd else A["w_in_ab"],
            "w_q_up": A["w_q_up"], "w_kv_up": A["w_kv_up"], "w_out_ab": A["w_out_ab"],
            "w_in_c": A["w_in_c"], "w_out_c": A["w_out_c"],
            "vecab": vecab, "lbl": lbl_fm, "vecc": vecc,
            "ropeA": _rope_tables(64, pos), "ropeC": _rope_tables(128, pos),
            "cst": cst, "sel": sel,
            "ckvT": np.ascontiguousarray(A["cache_ckv"][b].transpose(0, 2, 1)),
            "kpeT": np.ascontiguousarray(A["cache_kpe"][b].transpose(0, 2, 1)),
            "s0": np.ascontiguousarray(s0),
            "kcT": np.ascontiguousarray(A["cache_k_c"][b].transpose(0, 2, 3, 1)),
            "vc": np.ascontiguousarray(A["cache_v_c"][b].reshape(2, 256, 512)),
        })
    res = run_bass_kernel_spmd(nc, in_maps, core_ids=list(range(8)))
    R = res.results

    y_prompt = np.zeros((16, 256, D), np.float32)
    y_sample = np.zeros((4, 2048, D), np.float32)
    new_ckv = np.zeros((16, 2, 256, 256), np.float32)
    new_kpe = np.zeros((16, 2, 256, 64), np.float32)
    new_sf = np.zeros((16, 2, 8, 128, 128), np.float32)
    new_sb = np.zeros((16, 2, 8, 128, 128), np.float32)
    new_kc = np.zeros((16, 2, 256, 4, 128), np.float32)
    new_vc = np.zeros((16, 2, 256, 4, 128), np.float32)
    for core in range(8):
        odd = core % 2
        b = core // 2
        r = R[core]
        y = r["yT"].T
        ckv = r["o_ckv"].transpose(0, 2, 1)
        kpe = r["o_kpe"].transpose(0, 2, 1)
        kc = r["o_kc"].transpose(0, 3, 1, 2)
        vcx = r["o_vc"].reshape(2, 512, 4, 128)
        st = r["o_st"]
        for sq in range(2):
            sl = slice(sq * 256, (sq + 1) * 256)
            bi = 2 * core + sq
            f = (lambda a: a[::-1]) if odd else (lambda a: a)
            y_prompt[bi] = f(y[sl])
            for j in range(2):
                new_ckv[bi, j] = f(ckv[j, sl])
                new_kpe[bi, j] = f(kpe[j, sl])
                new_kc[bi, j] = f(kc[j, sl])
                new_vc[bi, j] = f(vcx[j, sl])
                new_sf[bi, j] = st[j, 1 if odd else 0, sq]
                new_sb[bi, j] = st[j, 0 if odd else 1, sq]
        ys = y[512:1536]
        if odd:
            y_sample[b, 1024:2048] = ys[::-1]
        else:
            y_sample[b, 0:1024] = ys
    return (y_prompt, y_sample, new_ckv, new_kpe, new_sf, new_sb, new_kc, new_vc)
```
